# Optimizing a Trainium2 kernel written in Bass

```python
import jax, jax.numpy as jnp
from jax import lax
import numpy as np


D_MODEL = 1024
BATCH = 4
SEQ = 8192
DEPTH = 2
DEC_BATCH = 1
DEC_SEQ = 16384
PAST_LEN = 128

GRID_W = 64
HEAD_DIM = 64
NA_HEADS = 8
NA_KH_MAX = 8
NA_KW = 16
MLA_HEADS = 8
MLA_Q_RANK = 384
MLA_KV_RANK = 256
MLA_NOPE = 64
MLA_ROPE = 32
MLA_V = 64
MLA_THETA = 10000.0
GQA_Q_HEADS = 16
GQA_KV_HEADS = 4
WINDOW = 128
ROPE_THETA = 500000.0
ROT_DIM = HEAD_DIM // 4
D_FF = 2816
CONV_W = 3
BLOCK_Q = 128
ALPHA = (2 * DEPTH) ** 0.25
BETA = (8 * DEPTH) ** -0.25
NORM_EPS = 1e-5
NA_WIDTH = NA_HEADS * HEAD_DIM
L0_IN = 3 * NA_WIDTH + MLA_Q_RANK + MLA_KV_RANK + MLA_ROPE
L1_IN = (GQA_Q_HEADS + 2 * GQA_KV_HEADS) * HEAD_DIM

kernel_name = "hybrid_na_mla_swa_encoder"


def layer_norm(x, g, b):
    xf = x.astype(jnp.float32)
    mu = jnp.mean(xf, axis=-1, keepdims=True)
    xc = xf - mu
    var = jnp.mean(xc * xc, axis=-1, keepdims=True)
    y = xc * lax.rsqrt(var + NORM_EPS)
    return (y * g.astype(jnp.float32) + b.astype(jnp.float32)).astype(x.dtype)


def rms_norm(x, g):
    xf = x.astype(jnp.float32)
    y = xf * lax.rsqrt(jnp.mean(xf * xf, axis=-1, keepdims=True) + NORM_EPS)
    return (y * g.astype(jnp.float32)).astype(x.dtype)


def rope(x, theta, rot_dim):
    S = x.shape[1]
    half = rot_dim // 2
    inv = 1.0 / (theta ** (jnp.arange(0, rot_dim, 2, dtype=jnp.float32) / rot_dim))
    ang = jnp.arange(S, dtype=jnp.float32)[:, None] * inv[None, :]
    cos = jnp.cos(ang)[None, :, None, :]
    sin = jnp.sin(ang)[None, :, None, :]
    xr = x[..., :rot_dim].astype(jnp.float32)
    x1, x2 = xr[..., :half], xr[..., half:]
    rot = jnp.concatenate([x1 * cos - x2 * sin, x2 * cos + x1 * sin], axis=-1)
    return jnp.concatenate([rot.astype(x.dtype), x[..., rot_dim:]], axis=-1)


def neighbourhood_attention(q, k, v, rpb):
    B, S, H, dh = q.shape
    rows = S // GRID_W
    kh = min(NA_KH_MAX, rows)
    n = kh * NA_KW
    col = jnp.arange(GRID_W)
    cs = jnp.clip(col - NA_KW // 2, 0, GRID_W - NA_KW)
    key_cols = cs[:, None] + jnp.arange(NA_KW)[None, :]
    dc = jnp.broadcast_to((key_cols - col[:, None])[:, None, :], (GRID_W, kh, NA_KW))
    scale = dh ** -0.5

    def one_row(args):
        q_r, r = args
        rs = jnp.clip(r - kh // 2, 0, rows - kh)
        key_rows = rs + jnp.arange(kh)
        idx = (key_rows[None, :, None] * GRID_W + key_cols[:, None, :]).reshape(GRID_W, n)
        dr = jnp.broadcast_to((key_rows - r)[None, :, None], (GRID_W, kh, NA_KW))
        bias = rpb[:, dr + NA_KH_MAX - 1, dc + NA_KW - 1].reshape(H, GRID_W, n)
        k_g = k[:, idx]
        v_g = v[:, idx]
        s = jnp.einsum('bqhd,bqnhd->bhqn', q_r, k_g).astype(jnp.float32) * scale
        s = s + bias.astype(jnp.float32)[None]
        p = jax.nn.softmax(s, axis=-1).astype(v.dtype)
        return jnp.einsum('bhqn,bqnhd->bqhd', p, v_g)

    q_rows = q.reshape(B, rows, GRID_W, H, dh).transpose(1, 0, 2, 3, 4)
    out = lax.map(one_row, (q_rows, jnp.arange(rows)))
    return out.transpose(1, 0, 2, 3, 4).reshape(B, S, H * dh)


def mla_attention(q_nope, q_rope, k_nope, k_rope, v):
    B, S, H, _ = q_nope.shape
    nb = S // BLOCK_Q
    scale = (MLA_NOPE + MLA_ROPE) ** -0.5

    def blk(args):
        qn, qr = args
        s = jnp.einsum('bqhd,bkhd->bhqk', qn, k_nope) + jnp.einsum('bqhr,bkr->bhqk', qr, k_rope)
        p = jax.nn.softmax(s.astype(jnp.float32) * scale, axis=-1).astype(v.dtype)
        return jnp.einsum('bhqk,bkhd->bqhd', p, v)

    def to_blocks(t):
        return t.reshape(B, nb, BLOCK_Q, *t.shape[2:]).swapaxes(0, 1)

    out = lax.map(blk, (to_blocks(q_nope), to_blocks(q_rope)))
    return out.swapaxes(0, 1).reshape(B, S, H * MLA_V)


def window_gqa_sink(q, k, v, sinks):
    B, S, Hq, dh = q.shape
    Hkv = k.shape[2]
    G = Hq // Hkv
    nb = S // BLOCK_Q
    pad = ((0, 0), (BLOCK_Q, BLOCK_Q), (0, 0), (0, 0))
    kp = jnp.pad(k, pad).reshape(B, nb + 2, BLOCK_Q, Hkv, dh)
    vp = jnp.pad(v, pad).reshape(B, nb + 2, BLOCK_Q, Hkv, dh)
    kb = jnp.concatenate([kp[:, :-2], kp[:, 1:-1], kp[:, 2:]], axis=2)
    vb = jnp.concatenate([vp[:, :-2], vp[:, 1:-1], vp[:, 2:]], axis=2)
    qb = q.reshape(B, nb, BLOCK_Q, Hkv, G, dh)
    s = jnp.einsum('bnqkgd,bnjkd->bnkgqj', qb, kb).astype(jnp.float32) * (dh ** -0.5)
    blk = jnp.arange(nb)[:, None, None]
    qpos = blk * BLOCK_Q + jnp.arange(BLOCK_Q)[None, :, None]
    kpos = (blk - 1) * BLOCK_Q + jnp.arange(3 * BLOCK_Q)[None, None, :]
    valid = (jnp.abs(qpos - kpos) <= WINDOW) & (kpos >= 0) & (kpos < S)
    s = jnp.where(valid[None, :, None, None], s, -jnp.inf)
    sink = sinks.astype(jnp.float32).reshape(1, 1, Hkv, G, 1, 1)
    m = jnp.maximum(jnp.max(s, axis=-1, keepdims=True), sink)
    p = jnp.exp(s - m)
    p = (p / (jnp.sum(p, axis=-1, keepdims=True) + jnp.exp(sink - m))).astype(v.dtype)
    out = jnp.einsum('bnkgqj,bnjkd->bnqkgd', p, vb)
    return out.reshape(B, S, Hq * dh)


def even_layer(x, w_in, rpb, g_qn, w_q_up, g_kvn, w_kv_up, w_out, ln_g, ln_b):
    B, S, _ = x.shape
    h = x @ w_in
    sizes = [NA_WIDTH, NA_WIDTH, NA_WIDTH, MLA_Q_RANK, MLA_KV_RANK]
    splits = [int(c) for c in np.cumsum(sizes)]
    qa, ka, va, q_lat, kv_lat, k_r = jnp.split(h, splits, axis=-1)
    shp = (B, S, NA_HEADS, HEAD_DIM)
    a_out = neighbourhood_attention(qa.reshape(shp), ka.reshape(shp), va.reshape(shp), rpb)
    qm = (rms_norm(q_lat, g_qn) @ w_q_up).reshape(B, S, MLA_HEADS, MLA_NOPE + MLA_ROPE)
    q_nope = qm[..., :MLA_NOPE]
    q_rope = rope(qm[..., MLA_NOPE:], MLA_THETA, MLA_ROPE)
    kvm = (rms_norm(kv_lat, g_kvn) @ w_kv_up).reshape(B, S, MLA_HEADS, MLA_NOPE + MLA_V)
    k_nope, v_m = kvm[..., :MLA_NOPE], kvm[..., MLA_NOPE:]
    k_rope = rope(k_r[:, :, None, :], MLA_THETA, MLA_ROPE)[:, :, 0]
    b_out = mla_attention(q_nope, q_rope, k_nope, k_rope, v_m)
    mix = jnp.concatenate([a_out, b_out], axis=-1) @ w_out
    return layer_norm(ALPHA * x + mix, ln_g, ln_b)


def odd_layer(x, w_in, sinks, w_out, ln_g, ln_b):
    B, S, _ = x.shape
    h = x @ w_in
    qw = GQA_Q_HEADS * HEAD_DIM
    kw = GQA_KV_HEADS * HEAD_DIM
    q, k, v = jnp.split(h, [qw, qw + kw], axis=-1)
    q = rope(q.reshape(B, S, GQA_Q_HEADS, HEAD_DIM), ROPE_THETA, ROT_DIM)
    k = rope(k.reshape(B, S, GQA_KV_HEADS, HEAD_DIM), ROPE_THETA, ROT_DIM)
    v = v.reshape(B, S, GQA_KV_HEADS, HEAD_DIM)
    mix = window_gqa_sink(q, k, v, sinks) @ w_out
    return layer_norm(ALPHA * x + mix, ln_g, ln_b)


def channel_block(x, w_up, conv_w, conv_b, w_down, ln_g, ln_b):
    S = x.shape[1]
    h = x @ w_up
    hp = jnp.pad(h, ((0, 0), (CONV_W // 2, CONV_W // 2), (0, 0)))
    h = sum(hp[:, j:j + S] * conv_w[j] for j in range(CONV_W)) + conv_b
    gate, val = jnp.split(h, 2, axis=-1)
    y = (jax.nn.gelu(gate, approximate=False) * val) @ w_down
    return layer_norm(ALPHA * x + y, ln_g, ln_b)


def encoder(x, even_params, odd_params, ffn_params):
    for layer in range(DEPTH):
        if layer % 2 == 0:
            x = even_layer(x, *even_params)
        else:
            x = odd_layer(x, *odd_params)
        x = channel_block(x, *ffn_params[layer])
    return x


def setup_inputs(seed: int = 0) -> dict:
    key = jax.random.key(seed)
    ks = iter(jax.random.split(key, 40))

    def nrm(shape, scale):
        return jax.random.normal(next(ks), shape, jnp.float32) * scale

    def gain(n):
        return 1.0 + nrm((n,), 0.01)

    d = D_MODEL
    inp = {}
    inp['x_prompt'] = nrm((BATCH, SEQ, d), 1.0)
    inp['x_sample'] = nrm((DEC_BATCH, DEC_SEQ, d), 1.0)
    inp['l0_w_in'] = nrm((d, L0_IN), d ** -0.5)
    inp['l0_rpb'] = nrm((NA_HEADS, 2 * NA_KH_MAX - 1, 2 * NA_KW - 1), 0.1)
    inp['l0_g_q_norm'] = gain(MLA_Q_RANK)
    inp['l0_w_q_up'] = nrm((MLA_Q_RANK, MLA_HEADS * (MLA_NOPE + MLA_ROPE)), MLA_Q_RANK ** -0.5)
    inp['l0_g_kv_norm'] = gain(MLA_KV_RANK)
    inp['l0_w_kv_up'] = nrm((MLA_KV_RANK, MLA_HEADS * (MLA_NOPE + MLA_V)), MLA_KV_RANK ** -0.5)
    inp['l0_w_out'] = nrm((NA_WIDTH + MLA_HEADS * MLA_V, d), BETA * (NA_WIDTH + MLA_HEADS * MLA_V) ** -0.5)
    inp['l0_ln1_g'] = gain(d)
    inp['l0_ln1_b'] = nrm((d,), 0.01)
    inp['l0_ffn_w_up'] = nrm((d, 2 * D_FF), d ** -0.5)
    inp['l0_ffn_conv_w'] = nrm((CONV_W, 2 * D_FF), CONV_W ** -0.5)
    inp['l0_ffn_conv_b'] = nrm((2 * D_FF,), 0.01)
    inp['l0_ffn_w_down'] = nrm((D_FF, d), BETA * D_FF ** -0.5)
    inp['l0_ln2_g'] = gain(d)
    inp['l0_ln2_b'] = nrm((d,), 0.01)
    inp['l1_w_in'] = nrm((d, L1_IN), d ** -0.5)
    inp['l1_sinks'] = nrm((GQA_Q_HEADS,), 1.0)
    inp['l1_w_out'] = nrm((GQA_Q_HEADS * HEAD_DIM, d), BETA * (GQA_Q_HEADS * HEAD_DIM) ** -0.5)
    inp['l1_ln1_g'] = gain(d)
    inp['l1_ln1_b'] = nrm((d,), 0.01)
    inp['l1_ffn_w_up'] = nrm((d, 2 * D_FF), d ** -0.5)
    inp['l1_ffn_conv_w'] = nrm((CONV_W, 2 * D_FF), CONV_W ** -0.5)
    inp['l1_ffn_conv_b'] = nrm((2 * D_FF,), 0.01)
    inp['l1_ffn_w_down'] = nrm((D_FF, d), BETA * D_FF ** -0.5)
    inp['l1_ln2_g'] = gain(d)
    inp['l1_ln2_b'] = nrm((d,), 0.01)
    return inp


def reference(x_prompt, x_sample,
              l0_w_in, l0_rpb, l0_g_q_norm, l0_w_q_up, l0_g_kv_norm, l0_w_kv_up, l0_w_out,
              l0_ln1_g, l0_ln1_b, l0_ffn_w_up, l0_ffn_conv_w, l0_ffn_conv_b, l0_ffn_w_down,
              l0_ln2_g, l0_ln2_b,
              l1_w_in, l1_sinks, l1_w_out, l1_ln1_g, l1_ln1_b,
              l1_ffn_w_up, l1_ffn_conv_w, l1_ffn_conv_b, l1_ffn_w_down, l1_ln2_g, l1_ln2_b):
    even_params = (l0_w_in, l0_rpb, l0_g_q_norm, l0_w_q_up, l0_g_kv_norm, l0_w_kv_up,
                   l0_w_out, l0_ln1_g, l0_ln1_b)
    odd_params = (l1_w_in, l1_sinks, l1_w_out, l1_ln1_g, l1_ln1_b)
    ffn_params = [
        (l0_ffn_w_up, l0_ffn_conv_w, l0_ffn_conv_b, l0_ffn_w_down, l0_ln2_g, l0_ln2_b),
        (l1_ffn_w_up, l1_ffn_conv_w, l1_ffn_conv_b, l1_ffn_w_down, l1_ln2_g, l1_ln2_b),
    ]
    y_prompt = encoder(x_prompt, even_params, odd_params, ffn_params)
    y_sample = encoder(x_sample, even_params, odd_params, ffn_params)
    return (y_prompt, y_sample)
```

```python
import numpy as np
import ml_dtypes
from contextlib import ExitStack
import concourse.bass as bass
import concourse.mybir as mybir
from concourse.bass_utils import run_bass_kernel_spmd

F32 = mybir.dt.float32
BF16 = mybir.dt.bfloat16
AF = mybir.ActivationFunctionType
ALU = mybir.AluOpType
AX = mybir.AxisListType

D = 1024
DFF = 2816
NEG = -30000.0
ALPHA = float(4 ** 0.25)
EPS = 1e-5
HALO = 256
KH = 512
NB = 256
QPERM = [0, 4, 1, 5, 2, 6, 3, 7, 8, 12, 9, 13, 10, 14, 11, 15]
MLA_SCALE = float(96 ** -0.5)


class SegCfg:
    def __init__(self, name, S, NQ):
        self.name = name
        self.S = S
        self.NQ = NQ
        self.T = NQ + 2 * HALO
        self.T2 = self.T + 2 * KH
        self.nt = self.T // 128
        self.nt2 = self.T2 // 128
        self.nts = S // 128
        self.nqt = self.nt + 1


FULL_CFG = [SegCfg("P", 8192, 4096), SegCfg("S", 16384, 2048)]


DMAQ = {"pool": "sp"}
LOADQ = "act"
HOIST = True
NOCLEAR = False


class _Op:
    __slots__ = ("eng", "fn", "reads", "writes", "dma", "key", "needs_inc", "count", "deps",
                 "sem", "semval", "waits")


class Prog:
    ENGS = ("pe", "act", "dve", "pool", "sp")

    def __init__(self, nc, es, ndma=38):
        self.nc = nc
        self.sets = []
        for s in range(2):
            eng = {e: es.enter_context(nc.semaphore(f"s{s}_{e}")) for e in self.ENGS}
            dma = [es.enter_context(nc.semaphore(f"s{s}_d{i}")) for i in range(ndma)]
            self.sets.append((eng, dma))
        self.phase = 0
        self.ops = []
        self.first = True

    def add(self, eng, fn, reads=(), writes=(), dma=False, key=None):
        op = _Op()
        op.eng = eng
        op.fn = fn
        op.reads = tuple(reads)
        op.writes = tuple(writes)
        op.dma = dma
        op.key = key
        op.needs_inc = False
        op.count = 0
        self.ops.append(op)
        return op

    def mm(self, out, lhsT, rhs, start, stop, reads, writes):
        self.add("pe", lambda h: h.matmul(out, lhsT, rhs, start=start, stop=stop), reads, writes)

    def tr(self, out, in_, ident, reads, writes):
        self.add("pe", lambda h: h.transpose(out, in_, ident), reads, writes)

    def act(self, out, in_, func, reads, writes, **kw):
        self.add("act", lambda h: h.activation(out=out, in_=in_, func=func, **kw), reads, writes)

    def copy(self, eng, out, in_, reads, writes, scale=None):
        if eng == "act":
            if scale is None:
                self.add("act", lambda h: h.activation(out=out, in_=in_, func=AF.Copy), reads, writes)
            else:
                self.add("act", lambda h: h.activation(out=out, in_=in_, func=AF.Copy, scale=scale),
                         reads, writes)
        else:
            if scale is None:
                self.add(eng, lambda h: h.tensor_copy(out=out, in_=in_), reads, writes)
            else:
                self.add(eng, lambda h: h.tensor_scalar(out=out, in0=in_, scalar1=scale, scalar2=None,
                                                        op0=ALU.mult), reads, writes)

    def tt(self, eng, out, in0, in1, op, reads, writes):
        self.add(eng, lambda h: h.tensor_tensor(out=out, in0=in0, in1=in1, op=op), reads, writes)

    def ts(self, eng, out, in0, s1, s2, op0, op1, reads, writes):
        if op1 is None:
            self.add(eng, lambda h: h.tensor_scalar(out=out, in0=in0, scalar1=s1, scalar2=None, op0=op0),
                     reads, writes)
        else:
            self.add(eng, lambda h: h.tensor_scalar(out=out, in0=in0, scalar1=s1, scalar2=s2, op0=op0,
                                                    op1=op1), reads, writes)

    def stt(self, eng, out, in0, scalar, in1, op0, op1, reads, writes):
        self.add(eng, lambda h: h.scalar_tensor_tensor(out=out, in0=in0, scalar=scalar, in1=in1,
                                                       op0=op0, op1=op1), reads, writes)

    def memset(self, eng, ap, val, writes):
        self.add(eng, lambda h: h.memset(ap, val), (), writes)

    def dma(self, q, out, in_, reads, writes, key):
        q = DMAQ.get(q, q)
        if writes and LOADQ:
            q = LOADQ
        self.add(q, lambda h: h.dma_start(out=out, in_=in_), reads, writes, dma=True, key=key)

    def flush(self):
        nc = self.nc
        engsem, dmapool = self.sets[self.phase % 2]
        oeng, odma = self.sets[(self.phase + 1) % 2]
        ops = self.ops
        if HOIST:
            touch = {}
            keyed = []
            for i, op in enumerate(ops):
                k = float(i)
                if op.dma and op.writes:
                    prev = [touch[b] for b in op.writes if b in touch]
                    if prev:
                        k = max(prev) + 0.5 + 1e-6 * (i % 1000)
                keyed.append((k, i, op))
                for b in op.reads + op.writes:
                    touch[b] = max(touch.get(b, -1.0), k)
            keyed.sort(key=lambda x: (x[0], x[1]))
            ops = [x[2] for x in keyed]
        last_w = {}
        readers = {}
        dmakey = {}
        for op in ops:
            deps = []
            for b in op.reads:
                w = last_w.get(b)
                if w is not None:
                    deps.append((w, 0))
            for b in op.writes:
                w = last_w.get(b)
                if w is not None:
                    deps.append((w, 1))
                for r in readers.get(b, {}).values():
                    deps.append((r, 2))
            for b in op.writes:
                last_w[b] = op
                readers[b] = {}
            for b in op.reads:
                readers.setdefault(b, {})[(op.eng, op.key if op.dma else None)] = op
            op.deps = []
            for (s, kind) in deps:
                if s is op:
                    continue
                if (not s.dma) and (not op.dma) and s.eng == op.eng:
                    if kind != 0 or op.eng == "pe":
                        continue
                op.deps.append(s)
                if not s.dma:
                    s.needs_inc = True
            if op.dma:
                ent = dmakey.get(op.key)
                if ent is None:
                    assert len(dmakey) < len(dmapool), "out of dma semaphores"
                    ent = [dmapool[len(dmakey)], 0, None]
                    dmakey[op.key] = ent
                if ent[2] is not None:
                    op.deps.append(ent[2])
                ent[2] = op
                ent[1] += 16
                op.sem = ent[0]
                op.semval = ent[1]
        cnt = {e: 0 for e in self.ENGS}
        for op in ops:
            if (not op.dma) and op.needs_inc:
                cnt[op.eng] += 1
                op.count = cnt[op.eng]
        waited = {e: {} for e in self.ENGS}
        for op in ops:
            need = {}
            for s in op.deps:
                if s.dma:
                    sid, sem, val = ("d", id(s.sem)), s.sem, s.semval
                else:
                    sid, sem, val = ("e", s.eng), engsem[s.eng], s.count
                if sid not in need or need[sid][1] < val:
                    need[sid] = (sem, val)
            op.waits = []
            wd = waited[op.eng]
            for sid, (sem, val) in need.items():
                if wd.get(sid, 0) >= val:
                    continue
                wd[sid] = val
                op.waits.append((sem, val))
        per = {e: [] for e in self.ENGS}
        for op in ops:
            per[op.eng].append(op)
        finals = [(ent[0], ent[1]) for ent in dmakey.values()]
        first = self.first

        def mk(e):
            def body(h):
                if e == "sp" and not first and not NOCLEAR:
                    for s_ in list(oeng.values()) + list(odma):
                        h.sem_clear(s_)
                for op in per[e]:
                    for (sem, val) in op.waits:
                        h.wait_ge(sem, val)
                    ins = op.fn(h)
                    if op.dma:
                        ins.then_inc(op.sem, 16)
                    elif op.needs_inc:
                        ins.then_inc(engsem[e], 1)
                if e == "sp":
                    for (sem, val) in finals:
                        h.wait_ge(sem, val)
            return body

        with nc.Block() as block:
            block.tensor(mk("pe"))
            block.scalar(mk("act"))
            block.vector(mk("dve"))
            block.gpsimd(mk("pool"))
            block.sync(mk("sp"))
        self.n_ops = getattr(self, "n_ops", 0) + len(ops)
        self.ops = []
        self.phase += 1
        self.first = False


def _rope_tab(pos, theta, rot_dim):
    half = rot_dim // 2
    inv = (np.float32(1.0) / (np.float32(theta) ** (np.arange(0, rot_dim, 2, dtype=np.float32)
                                                      / np.float32(rot_dim)))).astype(np.float32)
    ang = pos.astype(np.float32)[:, None] * inv[None, :]
    cos = np.cos(ang).astype(np.float32)
    sin = np.sin(ang).astype(np.float32)
    c2 = np.concatenate([cos, cos], axis=1)
    s2 = np.concatenate([-sin, sin], axis=1)
    return c2, s2


def _bc(v, n=128):
    return np.ascontiguousarray(np.broadcast_to(np.asarray(v, np.float32)[None, :], (n, v.shape[0])))


def prep_shared(inp):
    sh = {}
    w = inp["l0_w_in"]
    kr = w[:, 2176:2208]
    sh["wa"] = np.ascontiguousarray(np.concatenate([w[:, 1920:2208], kr[:, 16:32], kr[:, 0:16]], axis=1))
    sh["wb"] = np.ascontiguousarray(w[:, 0:1920])
    wq = inp["l0_w_q_up"].reshape(384, 8, 96)
    sh["wq"] = np.ascontiguousarray(wq.reshape(384, 768))
    wqs = np.concatenate([wq[:, :, 0:64], wq[:, :, 80:96], wq[:, :, 64:80]], axis=2)
    sh["wqs"] = np.ascontiguousarray(wqs.reshape(384, 768))
    wkv = inp["l0_w_kv_up"].reshape(256, 8, 128)
    sh["wkvK"] = np.ascontiguousarray(wkv[:, :, 0:64].reshape(256, 512))
    sh["wkvV"] = np.ascontiguousarray(wkv[:, :, 64:128].reshape(256, 512))
    sh["gq"] = _bc(inp["l0_g_q_norm"])
    sh["gkv"] = _bc(inp["l0_g_kv_norm"])
    sh["wo0"] = np.ascontiguousarray(inp["l0_w_out"])
    rpb = inp["l0_rpb"]
    B = np.full((8, 2, 64, 16, 64), NEG, np.float32)
    c = np.arange(64)
    cs = np.clip(c - 8, 0, 48)
    for i in range(2):
        for wv in range(16):
            dr = wv - i
            if dr < 0 or dr > 14:
                continue
            for cc in range(64):
                kc = cs[cc] + np.arange(16)
                B[:, i, cc, wv, kc] = rpb[:, dr, kc - cc + 15]
    sh["nab"] = np.ascontiguousarray(B.reshape(8, 128, 1024).transpose(1, 0, 2))
    for l in (0, 1):
        p = f"l{l}_"
        sh[p + "ln1g"] = _bc(inp[p + "ln1_g"])
        sh[p + "ln1b"] = _bc(inp[p + "ln1_b"])
        sh[p + "ln2g"] = _bc(inp[p + "ln2_g"])
        sh[p + "ln2b"] = _bc(inp[p + "ln2_b"])
        sh[p + "wup"] = np.ascontiguousarray(inp[p + "ffn_w_up"])
        sh[p + "wdn"] = np.ascontiguousarray(inp[p + "ffn_w_down"])
        cw = inp[p + "ffn_conv_w"]
        sh[p + "cw"] = np.ascontiguousarray(cw.reshape(3, 44, 128).transpose(2, 1, 0))
        sh[p + "cb"] = np.ascontiguousarray(inp[p + "ffn_conv_b"].reshape(44, 128).T)
    w1 = inp["l1_w_in"]
    q = w1[:, 0:1024].reshape(D, 16, 64)[:, QPERM, :]
    k = w1[:, 1024:1280].reshape(D, 4, 64)
    qk = np.concatenate([q, k], axis=1)
    sw = np.concatenate([qk[:, :, 8:16], qk[:, :, 0:8]], axis=2)
    sh["w1in"] = np.ascontiguousarray(np.concatenate([qk.reshape(D, 1280), w1[:, 1280:1536],
                                                      sw.reshape(D, 320)], axis=1))
    sh["wo1"] = np.ascontiguousarray(inp["l1_w_out"].reshape(16, 64, D)[QPERM].reshape(1024, D))
    sh["sinks"] = _bc(inp["l1_sinks"][QPERM])
    sh["ident"] = np.eye(128, dtype=np.float32)
    a = np.arange(128)[:, None]
    j = np.arange(384)[None, :]
    sh["band"] = np.where((j >= a) & (j <= a + 256), 0.0, NEG).astype(ml_dtypes.bfloat16)
    qs = np.zeros((2, 128), np.float32)
    qs[0, 0:64] = 1.0
    qs[1, 64:128] = 1.0
    sh["qsel"] = qs.astype(ml_dtypes.bfloat16)
    return sh


def prep_seg(seg, x_seq, a):
    S, T, T2 = seg.S, seg.T, seg.T2
    n = seg.name
    m = {}
    m[n + "_xseq"] = np.ascontiguousarray(x_seq)
    base2 = a - HALO - KH
    xe = np.zeros((T2, D), np.float32)
    lo, hi = max(base2, 0), min(base2 + T2, S)
    xe[lo - base2:hi - base2] = x_seq[lo:hi]
    m[n + "_xext"] = xe
    pos_e = np.arange(a - HALO, a - HALO + T)
    valid = ((pos_e >= 0) & (pos_e < S)).astype(np.float32)
    m[n + "_valid"] = np.ascontiguousarray(valid.reshape(seg.nt, 128).T)
    c2, s2 = _rope_tab(np.arange(S), 10000.0, 32)
    kcs = np.concatenate([c2, s2], axis=1).reshape(seg.nts, 128, 64).transpose(1, 0, 2)
    m[n + "_kcs"] = np.ascontiguousarray(kcs)
    pos2 = np.clip(np.arange(base2, base2 + T2), 0, S - 1)
    c2, s2 = _rope_tab(pos2, 10000.0, 32)
    m[n + "_qcsT"] = np.ascontiguousarray(np.concatenate([c2.T, s2.T], axis=0))
    c2, s2 = _rope_tab(np.clip(pos_e, 0, S - 1), 500000.0, 16)
    l1cs = np.concatenate([c2, s2], axis=1).reshape(seg.nt, 128, 32).transpose(1, 0, 2)
    m[n + "_l1cs"] = np.ascontiguousarray(l1cs)
    m[n + "_kpen"] = np.where(valid > 0, 0.0, NEG).astype(ml_dtypes.bfloat16)[None, :]
    rows = S // 64
    pen = np.zeros((seg.nqt, 2, 16, 64), np.float32)
    for j in range(seg.nqt):
        r_abs = (a - HALO) // 64 + 2 * j - 1
        for i in range(2):
            rq = r_abs + i
            if rq < 0 or rq >= rows:
                continue
            rs = min(max(rq - 4, 0), rows - 8)
            for wv in range(16):
                krow = r_abs - 7 + wv
                if not (rs <= krow < rs + 8):
                    pen[j, i, wv, :] = NEG
    m[n + "_pen"] = pen.reshape(seg.nqt, 2, 1024).astype(ml_dtypes.bfloat16)
    return m


class Ctx:
    pass


def build(cfg, debug=()):
    nc = bass.Bass("TRN2", target_bir_lowering=False)
    C = Ctx()
    C.nc = nc
    C.debug = set(debug)

    def din(name, shape, dt=F32):
        return nc.dram_tensor(name, list(shape), dt, kind="ExternalInput").ap()

    def dscr(name, shape, dt):
        if name in C.debug:
            return nc.dram_tensor(name, list(shape), dt, kind="ExternalOutput").ap()
        return nc.dram_tensor(name, list(shape), dt).ap()

    W = {}
    W["wa"] = din("wa", [D, 320])
    W["wb"] = din("wb", [D, 1920])
    W["wq"] = din("wq", [384, 768])
    W["wqs"] = din("wqs", [384, 768])
    W["wkvK"] = din("wkvK", [256, 512])
    W["wkvV"] = din("wkvV", [256, 512])
    W["gq"] = din("gq", [128, 384])
    W["gkv"] = din("gkv", [128, 256])
    W["wo0"] = din("wo0", [D, D])
    W["nab"] = din("nab", [128, 8, 1024])
    for l in (0, 1):
        p = f"l{l}_"
        for nm in ("ln1g", "ln1b", "ln2g", "ln2b"):
            W[p + nm] = din(p + nm, [128, D])
        W[p + "wup"] = din(p + "wup", [D, 2 * DFF])
        W[p + "wdn"] = din(p + "wdn", [DFF, D])
        W[p + "cw"] = din(p + "cw", [128, 44, 3])
        W[p + "cb"] = din(p + "cb", [128, 44])
    W["w1in"] = din("w1in", [D, 1856])
    W["wo1"] = din("wo1", [D, D])
    W["sinks"] = din("sinks", [128, 16])
    W["ident"] = din("ident", [128, 128])
    W["band"] = din("band", [128, 384], BF16)
    W["qsel"] = din("qsel", [2, 128], BF16)
    C.W = W

    segs = []
    for sc in cfg:
        n = sc.name
        g = Ctx()
        g.c = sc
        g.xseq = din(n + "_xseq", [sc.S, D])
        g.xext = din(n + "_xext", [sc.T2, D])
        g.valid = din(n + "_valid", [128, sc.nt])
        g.kcs = din(n + "_kcs", [128, sc.nts, 64])
        g.qcsT = din(n + "_qcsT", [64, sc.T2])
        g.l1cs = din(n + "_l1cs", [128, sc.nt, 32])
        g.kpen = din(n + "_kpen", [1, sc.T], BF16)
        g.pen = din(n + "_pen", [sc.nqt, 2, 1024], BF16)
        g.Ks = dscr(n + "_Ks", [8, 97, sc.S], BF16)
        g.Vs = dscr(n + "_Vs", [8, 128, sc.nts, 65], BF16)
        g.qaug = dscr(n + "_qaug", [8, 96, sc.T2], BF16)
        g.qaT = dscr(n + "_qaT", [4, 128, sc.T2], BF16)
        g.kaT = dscr(n + "_kaT", [4, 128, sc.T2], BF16)
        g.va = dscr(n + "_va", [sc.T2, 512], BF16)
        g.catT = dscr(n + "_catT", [8, 128, sc.T], BF16)
        g.xmid = dscr(n + "_xmid", [sc.T, D], F32)
        g.xmidT = dscr(n + "_xmidT", [8, 128, sc.T + 2], BF16)
        g.x1 = dscr(n + "_x1", [sc.T, D], F32)
        g.q1T = dscr(n + "_q1T", [8, 128, sc.T], BF16)
        g.k1T = dscr(n + "_k1T", [2, 128, sc.T], BF16)
        g.v1 = dscr(n + "_v1", [sc.T, 256], BF16)
        g.xn1 = dscr(n + "_xn1", [sc.T, D], F32)
        g.xn1T = dscr(n + "_xn1T", [8, 128, sc.T + 2], BF16)
        g.out = nc.dram_tensor(n + "_out", [sc.NQ, D], F32, kind="ExternalOutput").ap()
        segs.append(g)

    with ExitStack() as es:
        pg = Prog(nc, es)
        C.pg = pg
        C.identF = es.enter_context(nc.sbuf_tensor("identF", [128, 128], F32))
        C.identB = es.enter_context(nc.sbuf_tensor("identB", [128, 128], BF16))
        C.ones = es.enter_context(nc.sbuf_tensor("ones", [128, 64], F32))
        C.onesB = es.enter_context(nc.sbuf_tensor("onesB", [128, 128], BF16))
        C.zerosB = es.enter_context(nc.sbuf_tensor("zerosB", [128, 8], BF16))
        pg.dma("sp", C.identF[:], W["ident"], [], ["identF"], key="identF")
        pg.copy("dve", C.identB[:], C.identF[:], ["identF"], ["identB"])
        pg.memset("pool", C.ones[:], 1.0, ["ones"])
        pg.memset("pool", C.onesB[:], 1.0, ["onesB"])
        pg.memset("pool", C.zerosB[:], 0.0, ["zerosB"])
        pg.flush()
        stop_after = C.debug_stop = [d for d in C.debug if d.startswith("stop:")]
        stop = stop_after[0][5:] if stop_after else None
        phases = [("A", phase_A), ("B", phase_B), ("C", phase_C), ("D", phase_D), ("E", phase_E),
                  ("F", lambda C_, g_: phase_FFN(C_, g_, 0)), ("G", phase_G), ("H", phase_H),
                  ("I", lambda C_, g_: phase_FFN(C_, g_, 1))]
        only = [d[5:] for d in C.debug if d.startswith("only:")]
        for (pn, fn) in phases:
            with ExitStack() as pes:
                C.pes = pes
                C.shared = {}
                for g in segs:
                    if only and pn not in only[0]:
                        continue
                    fn(C, g)
                    pg.flush()
            if stop == pn:
                break
    C.n_ops = pg.n_ops
    return nc, C


_UID = [0]


def shared_sb(C, name, shp, dt):
    if name in C.shared:
        return C.shared[name], False
    _UID[0] += 1
    t = C.pes.enter_context(C.nc.sbuf_tensor(f"sh{_UID[0]}_{name}", list(shp), dt))
    C.shared[name] = t
    return t, True


def _pools(C, es):
    nc = C.nc
    _UID[0] += 1
    u = _UID[0]

    def sb(n, shp, dt):
        return es.enter_context(nc.sbuf_tensor(f"sb{u}_{n}", list(shp), dt))

    def ps(n, shp, dt=F32):
        return es.enter_context(nc.psum_tensor(f"ps{u}_{n}", list(shp), dt))

    return sb, ps


def load_cast(C, dst, src, K, N, stage, name, engs=("pool", "act", "dve"), ctr=[0], cwmax=2048):
    pg = C.pg
    for k in range(K):
        for c0 in range(0, N, cwmax):
            cw = min(cwmax, N - c0)
            i = ctr[0]
            ctr[0] += 1
            b = i % 2
            pg.dma("sp", stage[b][:, 0:cw], src[k * 128:(k + 1) * 128, c0:c0 + cw], [], [f"stg{b}"],
                   key=f"stg{b}")
            pg.copy(engs[i % len(engs)], dst[:, k, c0:c0 + cw], stage[b][:, 0:cw], [f"stg{b}"],
                    [f"{name}{k}"])


def load_x_T(C, src_rows, xs, xT, psT, b, tagx):
    pg = C.pg
    pg.dma("sp", xs[b][:], src_rows, [], [f"xs{b}"], key=f"xs{b}")
    for k in range(8):
        pg.tr(psT[:, k, :], xs[b][:, k * 128:(k + 1) * 128], C.identF[:], [f"xs{b}", "identF"], ["psT"])
    pg.copy("act", xT[b][:, 0:4, :], psT[:, 0:4, :], ["psT"], [f"{tagx}{b}a"])
    pg.copy("dve", xT[b][:, 4:8, :], psT[:, 4:8, :], ["psT"], [f"{tagx}{b}b"])


def rms_rstd(C, ss, lnv, rstd, n, tag):
    pg = C.pg
    pg.act(lnv[:, 0:1], ss[:, 0:1], AF.Ln, [tag + "ss"], [tag + "lnv"], scale=1.0 / n, bias=C.epsb[:, 0:1])
    pg.act(rstd[:, 0:1], lnv[:, 0:1], AF.Exp, [tag + "lnv"], [tag + "rstd"], scale=-0.5)


def phase_A(C, g):
    pg, nc, sc, W = C.pg, C.nc, g.c, C.W
    with ExitStack() as es:
        sb, ps = _pools(C, es)
        wa = sb("wa_sb", [128, 8, 320], BF16)
        wkK = sb("wkK", [128, 2, 512], BF16)
        wkV = sb("wkV", [128, 2, 512], BF16)
        gkv = sb("gkv", [128, 256], F32)
        kcs = sb("kcs", [128, sc.nts, 64], F32)
        C.epsb = sb("epsb", [128, 1], F32)
        stage = [sb(f"stg{i}", [128, 2048], F32) for i in range(2)]
        xs = [sb(f"xs{i}", [128, 1024], F32) for i in range(2)]
        xT = [sb(f"xT{i}", [128, 8, 128], BF16) for i in range(2)]
        junk = sb("junk", [128, 256], F32)
        ss = sb("ss", [128, 1], F32)
        lnv = sb("lnv", [128, 1], F32)
        rstd = sb("rstd", [128, 1], F32)
        ckr = sb("ckr", [128, 352], BF16)
        m1 = sb("m1", [128, 32], F32)
        m2 = sb("m2", [128, 32], F32)
        cTg = sb("cTg", [128, 2, 512], BF16)
        kst = sb("kst", [64, 8, 512], BF16)
        krT = sb("krT", [32, 512], BF16)
        vst = [sb(f"vst{i}", [128, 8, 65], BF16) for i in range(2)]
        psT = ps("psT", [128, 8, 128])
        pkv = ps("pkv", [128, 512])
        pcT = ps("pcT", [128, 3, 128], BF16)
        pV = ps("pV", [128, 512])
        pK = [ps(f"pK{i}", [128, 512]) for i in range(2)]

        pg.memset("dve", C.epsb[:], EPS, ["epsb"])
        load_cast(C, wa, W["wa"], 8, 320, stage, "wa")
        load_cast(C, wkK, W["wkvK"], 2, 512, stage, "wkK")
        load_cast(C, wkV, W["wkvV"], 2, 512, stage, "wkV")
        pg.dma("sp", gkv[:], W["gkv"], [], ["gkv"], key="gkv")
        pg.dma("sp", kcs[:], g.kcs, [], ["kcs"], key="kcs")
        pg.memset("pool", ckr[:], 0.0, ["ckr_c", "ckr_r"])
        for i in range(2):
            pg.memset("pool", vst[i][:], 1.0, [f"vst{i}"])
        wa_r = [f"wa{k}" for k in range(8)]
        import os
        CUT = int(os.environ.get("CUT", "9"))
        if CUT <= 1:
            return
        for grp in range(sc.nts // 4):
            for j in range(4):
                t = grp * 4 + j
                b = t % 2
                load_x_T(C, g.xseq[t * 128:(t + 1) * 128, :], xs, xT, psT, b, "xT")
                if CUT <= 2:
                    continue
                for k in range(8):
                    pg.mm(pkv[:, 0:320], xT[b][:, k, :], wa[:, k, :], k == 0, k == 7,
                          [f"xT{b}a" if k < 4 else f"xT{b}b", wa_r[k]], ["pkv"])
                pg.act(junk[:, 0:256], pkv[:, 0:256], AF.Square, ["pkv"], ["junk", "Ass"], accum_out=ss[:, 0:1])
                rms_rstd(C, ss, lnv, rstd, 256, "A")
                pg.stt("dve", ckr[:, 0:256], pkv[:, 0:256], rstd[:, 0:1], gkv[:], ALU.mult, ALU.mult,
                       ["pkv", "Arstd", "gkv"], ["ckr_c"])
                if CUT <= 3:
                    continue
                pg.tt("dve", m1[:], pkv[:, 256:288], kcs[:, t, 0:32], ALU.mult, ["pkv", "kcs"], ["m1"])
                pg.tt("dve", m2[:], pkv[:, 288:320], kcs[:, t, 32:64], ALU.mult, ["pkv", "kcs"], ["m2"])
                pg.tt("pool", ckr[:, 320:352], m1[:], m2[:], ALU.add, ["m1", "m2"], ["ckr_r"])
                CUT2 = int(os.environ.get("CUT2", "9"))
                if CUT2 <= 0:
                    continue
                pg.tr(pcT[:, 0, :], ckr[:, 0:128], C.identB[:], ["ckr_c", "identB"], ["pcT"])
                pg.tr(pcT[:, 1, :], ckr[:, 128:256], C.identB[:], ["ckr_c", "identB"], ["pcT"])
                pg.tr(pcT[0:32, 2, :], ckr[:, 320:352], C.identB[:], ["ckr_r", "identB"], ["pcT"])
                pg.copy("act", cTg[:, :, j * 128:(j + 1) * 128], pcT[:, 0:2, :], ["pcT"], [f"cTg{j}"])
                pg.copy("dve", krT[0:32, j * 128:(j + 1) * 128], pcT[0:32, 2, :], ["pcT"], [f"krT{j}"])
                if CUT <= 4:
                    continue
                vb = t % 2
                for k in range(2):
                    pg.mm(pV[:, :], cTg[:, k, j * 128:(j + 1) * 128], wkV[:, k, :], k == 0, k == 1,
                          [f"cTg{j}", f"wkV{k}"], ["pV"])
                pg.copy("dve", vst[vb][:, :, 0:64], pV[:, :].rearrange("p (h d) -> p h d", h=8), ["pV"],
                        [f"vst{vb}"])
                pg.dma("pool", g.Vs[:, :, t, :].rearrange("h p c -> p h c"), vst[vb][:], [f"vst{vb}"], [],
                       key=f"vst{vb}")
            if CUT <= 5:
                continue
            ctg_r = [f"cTg{j}" for j in range(4)]
            for h in range(8):
                pb = h % 2
                for k in range(2):
                    pg.mm(pK[pb][0:64, :], wkK[:, k, h * 64:(h + 1) * 64], cTg[:, k, :], k == 0, k == 1,
                          ctg_r + [f"wkK{k}"], [f"pK{pb}"])
                pg.copy("act" if h % 2 else "dve", kst[0:64, h, :], pK[pb][0:64, :], [f"pK{pb}"], [f"kstn{h}"])
            if int(os.environ.get("CUT3", "9")) <= 0:
                continue
            pg.dma("pool", g.Ks[:, 0:64, grp * 512:(grp + 1) * 512].rearrange("h d c -> d h c"), kst[:],
                   [f"kstn{h}" for h in range(8)], [], key="kst")
            for h in range(8 if int(os.environ.get("CUT3", "9")) >= 2 else 0):
                pg.dma("pool", g.Ks[h, 64:96, grp * 512:(grp + 1) * 512], krT[:, :], [f"krT{j}" for j in range(4)], [],
                       key=f"krT{h}")


def phase_B(C, g):
    pg, nc, sc, W = C.pg, C.nc, g.c, C.W
    with ExitStack() as es:
        sb, ps = _pools(C, es)
        wb, fresh = shared_sb(C, "wb_sb", [128, 8, 1920], BF16)
        wq, _ = shared_sb(C, "wq_sb", [128, 3, 768], BF16)
        wqs, _ = shared_sb(C, "wqs_sb", [128, 3, 768], BF16)
        gq, _ = shared_sb(C, "gq", [128, 384], F32)
        C.epsb = sb("epsb", [128, 1], F32)
        ctab = sb("ctab", [96, 512], F32)
        stab = sb("stab", [96, 512], F32)
        stage = [sb(f"stg{i}", [128, 2048], F32) for i in range(2)]
        xs = [sb(f"xs{i}", [128, 1024], F32) for i in range(2)]
        xTg = sb("xTg", [128, 8, 512], BF16)
        junk = sb("junk", [128, 384], F32)
        ss = sb("ss", [128, 1], F32)
        lnv = sb("lnv", [128, 1], F32)
        rstd = sb("rstd", [128, 1], F32)
        cq = sb("cq", [128, 384], BF16)
        cqT = sb("cqT", [128, 3, 512], BF16)
        qkst = [sb(f"qkst{i}", [128, 512], BF16) for i in range(2)]
        vast = [sb(f"vast{i}", [128, 512], BF16) for i in range(2)]
        qst = [sb(f"qst{i}", [96, 512], BF16) for i in range(2)]
        r1 = sb("r1", [96, 512], F32)
        r2 = sb("r2", [96, 512], F32)
        psT = ps("psT", [128, 8, 128])
        pA = [ps(f"pA{i}", [128, 512]) for i in range(2)]
        pB = [ps(f"pB{i}", [128, 512]) for i in range(2)]
        pcq = ps("pcq", [128, 3, 128], BF16)

        pg.memset("dve", C.epsb[:], EPS, ["epsb"])
        if fresh:
            load_cast(C, wb, W["wb"], 8, 1920, stage, "wb")
            load_cast(C, wq, W["wq"], 3, 768, stage, "wq")
            load_cast(C, wqs, W["wqs"], 3, 768, stage, "wqs")
            pg.dma("sp", gq[:], W["gq"], [], ["gq"], key="gq")
        wb_r = [f"wb{k}" for k in range(8)]
        ngrp = sc.nt2 // 4
        for grp in range(ngrp):
            c0 = grp * 512
            xr = []
            for j in range(4):
                t = grp * 4 + j
                b = t % 2
                pg.dma("sp", xs[b][:], g.xext[t * 128:(t + 1) * 128, :], [], [f"xs{b}"], key=f"xs{b}")
                for k in range(8):
                    pg.tr(psT[:, k, :], xs[b][:, k * 128:(k + 1) * 128], C.identF[:], [f"xs{b}", "identF"],
                          ["psT"])
                pg.copy("act", xTg[:, 0:4, j * 128:(j + 1) * 128], psT[:, 0:4, :], ["psT"], [f"xTg{j}"])
                pg.copy("dve", xTg[:, 4:8, j * 128:(j + 1) * 128], psT[:, 4:8, :], ["psT"], [f"xTg{j}"])
                xr.append(f"xTg{j}")
            for which, col0, dst, scl in (("q", 0, g.qaT, 0.125), ("k", 512, g.kaT, None)):
                for p in range(4):
                    pb = p % 2
                    for k in range(8):
                        pg.mm(pA[pb][:, :], wb[:, k, col0 + p * 128: col0 + (p + 1) * 128], xTg[:, k, :],
                              k == 0, k == 7, xr + [wb_r[k]], [f"pA{pb}"])
                    pg.copy("act" if pb else "dve", qkst[pb][:, :], pA[pb][:, :], [f"pA{pb}"], [f"qkst{pb}"],
                            scale=scl)
                    pg.dma("pool", dst[p, :, c0:c0 + 512], qkst[pb][:], [f"qkst{pb}"], [], key=f"qkst{pb}")
            for j in range(4):
                t = grp * 4 + j
                pb = j % 2
                for k in range(8):
                    pg.mm(pA[pb][:, :], xTg[:, k, j * 128:(j + 1) * 128], wb[:, k, 1024:1536], k == 0, k == 7,
                          [f"xTg{j}", wb_r[k]], [f"pA{pb}"])
                pg.copy("act", vast[pb][:, :], pA[pb][:, :], [f"pA{pb}"], [f"vast{pb}"])
                pg.dma("pool", g.va[t * 128:(t + 1) * 128, :], vast[pb][:], [f"vast{pb}"], [], key=f"vast{pb}")
                for k in range(8):
                    pg.mm(pB[pb][:, 0:384], xTg[:, k, j * 128:(j + 1) * 128], wb[:, k, 1536:1920], k == 0,
                          k == 7, [f"xTg{j}", wb_r[k]], [f"pB{pb}"])
                pg.act(junk[:, 0:384], pB[pb][:, 0:384], AF.Square, [f"pB{pb}"], ["junk", "Bss"],
                       accum_out=ss[:, 0:1])
                rms_rstd(C, ss, lnv, rstd, 384, "B")
                pg.stt("dve", cq[:, :], pB[pb][:, 0:384], rstd[:, 0:1], gq[:], ALU.mult, ALU.mult,
                       [f"pB{pb}", "Brstd", "gq"], ["cq"])
                for k in range(3):
                    pg.tr(pcq[:, k, :], cq[:, k * 128:(k + 1) * 128], C.identB[:], ["cq", "identB"], ["pcq"])
                pg.copy("act", cqT[:, :, j * 128:(j + 1) * 128], pcq[:, :, :], ["pcq"], [f"cqT{j}"])
            pg.dma("sp", ctab[64:96, :], g.qcsT[0:32, c0:c0 + 512], [], ["ctab"], key="ctab")
            pg.dma("sp", stab[64:96, :], g.qcsT[32:64, c0:c0 + 512], [], ["stab"], key="stab")
            cq_r = [f"cqT{j}" for j in range(4)]
            for h in range(8):
                pb = h % 2
                for k in range(3):
                    pg.mm(pA[pb][0:96, :], wq[:, k, h * 96:(h + 1) * 96], cqT[:, k, :], k == 0, k == 2,
                          cq_r + [f"wq{k}"], [f"pA{pb}"])
                for k in range(3):
                    pg.mm(pB[pb][0:96, :], wqs[:, k, h * 96:(h + 1) * 96], cqT[:, k, :], k == 0, k == 2,
                          cq_r + [f"wqs{k}"], [f"pB{pb}"])
                pg.copy("act", qst[pb][0:64, :], pA[pb][0:64, :], [f"pA{pb}"], [f"qst{pb}n"])
                pg.tt("dve", r1[64:96, :], pA[pb][64:96, :], ctab[64:96, :], ALU.mult, [f"pA{pb}", "ctab"], ["r1"])
                pg.tt("dve", r2[64:96, :], pB[pb][64:96, :], stab[64:96, :], ALU.mult, [f"pB{pb}", "stab"], ["r2"])
                pg.tt("pool", qst[pb][64:96, :], r1[64:96, :], r2[64:96, :], ALU.add, ["r1", "r2"], [f"qst{pb}r"])
                pg.dma("pool", g.qaug[h, :, c0:c0 + 512], qst[pb][:], [f"qst{pb}n", f"qst{pb}r"], [],
                       key=f"qst{pb}")


def phase_C(C, g):
    pg, nc, sc, W = C.pg, C.nc, g.c, C.W
    with ExitStack() as es:
        sb, ps = _pools(C, es)
        nab, fresh = shared_sb(C, "nab", [128, 8, 1024], BF16)
        qsel, _ = shared_sb(C, "qsel", [2, 128], BF16)
        stage = [sb(f"stg{i}", [128, 2048], F32) for i in range(2)]
        kaT = sb("kaT", [128, 4, sc.T2], BF16)
        va = sb("va", [128, sc.nt2, 512], BF16)
        qa = [sb(f"qa{i}", [128, 4, 128], BF16) for i in range(2)]
        pen = [sb(f"pen{i}", [2, 1024], BF16) for i in range(2)]
        P = [sb(f"P{i}", [128, 1024], BF16) for i in range(2)]
        PT = [sb(f"PT{i}", [128, 8, 128], BF16) for i in range(2)]
        mx = [sb(f"mx{i}", [128, 2], F32) for i in range(2)]
        rs = sb("rs", [128, 8], F32)
        rinv = sb("rinv", [128, 8], F32)
        ao = [sb(f"ao{i}", [128, 512], BF16) for i in range(2)]
        aoT = [sb(f"aoT{i}", [128, 4, 128], BF16) for i in range(2)]
        pS = [ps(f"pS{i}", [128, 1024]) for i in range(2)]
        pPT = [ps(f"pPT{i}", [128, 8, 128], BF16) for i in range(2)]
        pO = ps("pO", [128, 512])
        paT = ps("paT", [128, 4, 128], BF16)

        for h in range(8 if fresh else 0):
            b = h % 2
            pg.dma("sp", stage[b][:, 0:1024], W["nab"][:, h, :], [], [f"stg{b}"], key=f"stg{b}")
            pg.copy("pool" if b else "dve", nab[:, h, :], stage[b][:, 0:1024], [f"stg{b}"], [f"nab{h}"])
        if fresh:
            pg.dma("sp", qsel[:], W["qsel"], [], ["qsel"], key="qsel")
        for p in range(4):
            pg.dma("sp", kaT[:, p, :], g.kaT[p, :, :], [], [f"kaT{p}"], key=f"kaT{p}")
        pg.dma("sp", va[:], g.va.rearrange("(t p) c -> p t c", p=128), [], ["va"], key="va")

        nq = sc.nqt
        for j in range(nq):
            qb = j % 2
            q0 = KH + 128 * j - 64
            pg.dma("sp", qa[qb][:], g.qaT[:, :, q0:q0 + 128].rearrange("k p t -> p k t"), [], [f"qa{qb}"],
                   key=f"qa{qb}")
            pg.dma("sp", pen[qb][:], g.pen[j, :, :], [], [f"pen{qb}"], key=f"pen{qb}")
            k0 = 128 * j

            def s1(h):
                sbuf = h % 2
                half = h % 2
                pr = h // 2
                for n2 in range(2):
                    o = pS[sbuf][:, n2 * 512:(n2 + 1) * 512]
                    pg.mm(o, qa[qb][half * 64:(half + 1) * 64, pr, :],
                          kaT[half * 64:(half + 1) * 64, pr, k0 + n2 * 512:k0 + (n2 + 1) * 512], True, False,
                          [f"qa{qb}", f"kaT{pr}"], [f"pS{sbuf}_{n2}"])
                    pg.mm(o, C.identB[:], nab[:, h, n2 * 512:(n2 + 1) * 512], False, False,
                          ["identB", f"nab{h}"], [f"pS{sbuf}_{n2}"])
                    pg.mm(o, qsel[0:2, :], pen[qb][0:2, n2 * 512:(n2 + 1) * 512], False, True,
                          ["qsel", f"pen{qb}"], [f"pS{sbuf}_{n2}"])

            def s2(h):
                sbuf = h % 2
                rd = [f"pS{sbuf}_0", f"pS{sbuf}_1"]
                pg.add("dve", lambda e, o=mx[sbuf][:, 0:1], i_=pS[sbuf][:, :]: e.reduce_max(
                    out=o, in_=i_, axis=AX.X), rd, [f"mxa{sbuf}"])
                pg.ts("dve", mx[sbuf][:, 1:2], mx[sbuf][:, 0:1], -1.0, None, ALU.mult, None, [f"mxa{sbuf}"],
                      [f"mx{sbuf}"])
                pg.act(P[sbuf][:, :], pS[sbuf][:, :], AF.Exp, rd + [f"mx{sbuf}"], [f"P{sbuf}", f"rs{h}"],
                       bias=mx[sbuf][:, 1:2], accum_out=rs[:, h:h + 1])

            def s3(h):
                sbuf = h % 2
                for kt in range(8):
                    pg.tr(pPT[sbuf][:, kt, :], P[sbuf][:, kt * 128:(kt + 1) * 128], C.identB[:],
                          [f"P{sbuf}", "identB"], [f"pPT{sbuf}"])
                pg.copy("dve" if h % 2 else "act", PT[sbuf][:, :, :], pPT[sbuf][:, :, :], [f"pPT{sbuf}"],
                        [f"PT{sbuf}"])

            def s4(h):
                sbuf = h % 2
                for kt in range(8):
                    pg.mm(pO[:, h * 64:(h + 1) * 64], PT[sbuf][:, kt, :], va[:, j + kt, h * 64:(h + 1) * 64],
                          kt == 0, kt == 7, [f"PT{sbuf}", "va"], ["pO"])

            for step in range(8 + 3):
                if step < 8:
                    s1(step)
                if 0 <= step - 1 < 8:
                    s2(step - 1)
                if 0 <= step - 2 < 8:
                    s3(step - 2)
                if 0 <= step - 3 < 8:
                    s4(step - 3)
            pg.add("dve", lambda e: e.reciprocal(out=rinv[:, :], in_=rs[:, :]), [f"rs{h}" for h in range(8)],
                   ["rinv"])
            ab = j % 2
            pg.tt("dve", ao[ab][:, :].rearrange("p (h d) -> p h d", h=8),
                  pO[:, :].rearrange("p (h d) -> p h d", h=8),
                  rinv[:, :].unsqueeze(2).broadcast_to([128, 8, 64]), ALU.mult, ["pO", "rinv"], [f"ao{ab}"])
            for k in range(4):
                pg.tr(paT[:, k, :], ao[ab][:, k * 128:(k + 1) * 128], C.identB[:], [f"ao{ab}", "identB"], ["paT"])
            pg.copy("act", aoT[ab][:, :, :], paT[:, :, :], ["paT"], [f"aoT{ab}"])
            e0 = 128 * j - 64
            lo, hi = max(e0, 0), min(e0 + 128, sc.T)
            pg.dma("pool", g.catT[0:4, :, lo:hi].rearrange("k p t -> p k t"), aoT[ab][:, :, lo - e0:hi - e0],
                   [f"aoT{ab}"], [], key=f"aoT{ab}")


def phase_D(C, g):
    pg, nc, sc, W = C.pg, C.nc, g.c, C.W
    with ExitStack() as es:
        sb, ps = _pools(C, es)
        ksb = [sb(f"ksb{i}", [96, sc.S], BF16) for i in range(2)]
        vsb = [sb(f"vsb{i}", [128, sc.nts, 65], BF16) for i in range(2)]
        qsb = [sb(f"qsb{i}", [96, sc.T], BF16) for i in range(2)]
        PTs = [sb(f"PTs{i}", [128, 512], BF16) for i in range(4)]
        rrow = sb("rrow", [65, 512], F32)
        rbc = sb("rbc", [64, 512], F32)
        on = [sb(f"on{i}", [64, 512], BF16) for i in range(2)]
        pST = [ps(f"pST{i}", [128, 512]) for i in range(3)]
        pOT = [ps(f"pOT{i}", [128, 512]) for i in range(2)]
        pR = ps("pR", [64, 512])
        nqb = sc.T // 512
        it = 0
        for h in range(8):
            hb = h % 2
            pg.dma("sp", ksb[hb][:, :], g.Ks[h, 0:96, :], [], [f"ksb{hb}"], key=f"ksb{hb}")
            pg.dma("sp", vsb[hb][:, :, :], g.Vs[h, :, :, :], [], [f"vsb{hb}"], key=f"vsb{hb}")
            pg.dma("sp", qsb[hb][0:96, :], g.qaug[h, :, KH:KH + sc.T], [], [f"qsb{hb}"], key=f"qsb{hb}")
            for qb in range(nqb):
                ob = (h * nqb + qb) % 2
                cols = slice(qb * 512, (qb + 1) * 512)

                def st(kt, it_):
                    sbuf = it_ % 3
                    pg.mm(pST[sbuf][:, :], ksb[hb][0:96, kt * 128:(kt + 1) * 128], qsb[hb][0:96, cols], True, True,
                          [f"ksb{hb}", f"qsb{hb}"], [f"pST{sbuf}"])

                def ex_pv(kt, it_):
                    sbuf = it_ % 3
                    pbuf = it_ % 4
                    pg.act(PTs[pbuf][:, :], pST[sbuf][:, :], AF.Exp, [f"pST{sbuf}"], [f"PTs{pbuf}"], scale=MLA_SCALE)
                    pg.mm(pOT[ob][0:65, :], vsb[hb][:, kt, 0:65], PTs[pbuf][:, :], kt == 0, kt == sc.nts - 1,
                          [f"vsb{hb}", f"PTs{pbuf}"], [f"pOT{ob}"])

                base = it
                for kt in range(sc.nts + 2):
                    if kt < sc.nts:
                        st(kt, base + kt)
                    if kt - 2 >= 0:
                        ex_pv(kt - 2, base + kt - 2)
                it = base + sc.nts
                pg.add("dve", lambda e, o=rrow[64:65, :], i_=pOT[ob][64:65, :]: e.reciprocal(out=o, in_=i_),
                       [f"pOT{ob}"], ["rrow"])
                pg.mm(pR[0:64, :], C.ones[64:65, 0:64], rrow[64:65, :], True, True, ["ones", "rrow"], ["pR"])
                pg.copy("act", rbc[:, :], pR[0:64, :], ["pR"], ["rbc"])
                pg.tt("dve", on[ob][:, :], pOT[ob][0:64, :], rbc[:, :], ALU.mult, [f"pOT{ob}", "rbc"], [f"on{ob}"])
                pg.dma("pool", g.catT[4 + h // 2, (h % 2) * 64:(h % 2) * 64 + 64, cols], on[ob][:, :], [f"on{ob}"],
                       [], key=f"on{ob}")


def layer_norm_tile(C, r, xn, tmp, gt, bt, st6, mv, lnv, rstd, nmr, vcol, tag, rtag, xtag, ttag=None,
                    defer=False):
    pg = C.pg
    if ttag is None:
        ttag = tag + "tmp"
    for hlf in range(2):
        pg.add("dve", lambda e, o=st6[:, hlf, :], i_=r[:, hlf * 512:(hlf + 1) * 512]: e.bn_stats(out=o, in_=i_),
               [rtag], [tag + f"st{hlf}"])
    pg.add("dve", lambda e: e.bn_aggr(out=mv[:, :], in_=st6[:, :, :].rearrange("p a b -> p (a b)")),
           [tag + "st0", tag + "st1"], [tag + "mv"])
    pg.act(lnv[:, 0:1], mv[:, 1:2], AF.Ln, [tag + "mv"], [tag + "lnv"], bias=C.epsb[:, 0:1])
    pg.act(rstd[:, 0:1], lnv[:, 0:1], AF.Exp, [tag + "lnv"], [tag + "rstd0"], scale=-0.5)
    if vcol is not None:
        pg.tt("dve", rstd[:, 1:2], rstd[:, 0:1], vcol, ALU.mult, [tag + "rstd0", "valid"], [tag + "rstd"])
        rs_ap = rstd[:, 1:2]
    else:
        pg.copy("dve", rstd[:, 1:2], rstd[:, 0:1], [tag + "rstd0"], [tag + "rstd"])
        rs_ap = rstd[:, 1:2]
    pg.stt("dve", nmr[:, 0:1], mv[:, 0:1], -1.0, rs_ap, ALU.mult, ALU.mult, [tag + "mv", tag + "rstd"],
           [tag + "nmr"])
    pg.act(tmp[:, :], r[:, :], AF.Identity, [rtag, tag + "rstd", tag + "nmr"], [ttag + "a", ttag + "b"], scale=rs_ap,
           bias=nmr[:, 0:1])
    if not defer:
        layer_norm_back(C, xn, tmp, gt, bt, vcol, tag, xtag, ttag)


def layer_norm_back(C, xn, tmp, gt, bt, vcol, tag, xtag, ttag):
    pg = C.pg
    pg.tt("pool", tmp[:, 0:384], tmp[:, 0:384], gt[:, 0:384], ALU.mult, [ttag + "a", tag + "g"], [ttag + "a"])
    pg.tt("dve", tmp[:, 384:1024], tmp[:, 384:1024], gt[:, 384:1024], ALU.mult, [ttag + "b", tag + "g"], [ttag + "b"])
    if vcol is not None:
        pg.stt("dve", xn[:, :], bt[:, :], vcol, tmp[:, :], ALU.mult, ALU.add,
               [ttag + "a", ttag + "b", tag + "b", "valid"], [xtag])
    else:
        pg.tt("pool", xn[:, 0:384], tmp[:, 0:384], bt[:, 0:384], ALU.add, [ttag + "a", tag + "b"], [xtag])
        pg.tt("dve", xn[:, 384:1024], tmp[:, 384:1024], bt[:, 384:1024], ALU.add, [ttag + "b", tag + "b"], [xtag])


def phase_E(C, g):
    pg, nc, sc, W = C.pg, C.nc, g.c, C.W
    with ExitStack() as es:
        sb, ps = _pools(C, es)
        wo, fresh = shared_sb(C, "wo_sb", [128, 8, 1024], BF16)
        gt, _ = shared_sb(C, "gt", [128, 1024], F32)
        bt, _ = shared_sb(C, "bt", [128, 1024], F32)
        stage = [sb(f"stg{i}", [128, 2048], F32) for i in range(2)]
        valid = sb("valid", [128, sc.nt], F32)
        C.epsb = sb("epsb", [128, 1], F32)
        cat = [sb(f"cat{i}", [128, 8, 512], BF16) for i in range(2)]
        xs = [sb(f"xs{i}", [128, 1024], F32) for i in range(2)]
        r = [sb(f"r{i}", [128, 1024], F32) for i in range(2)]
        tmp = [sb(f"tmp{i}", [128, 1024], F32) for i in range(2)]
        xn = [sb(f"xn{i}", [128, 1024], F32) for i in range(2)]
        xnT = [sb(f"xnT{i}", [128, 8, 128], BF16) for i in range(2)]
        st6 = sb("st6", [128, 2, 6], F32)
        mv = sb("mv", [128, 2], F32)
        lnv = sb("lnv", [128, 1], F32)
        rstd = sb("rstd", [128, 2], F32)
        nmr = sb("nmr", [128, 1], F32)
        pmix = [ps(f"pmix{i}", [128, 1024]) for i in range(2)]
        psT = ps("psT", [128, 8, 128])
        pg.memset("dve", C.epsb[:], EPS, ["epsb"])
        if fresh:
            load_cast(C, wo, W["wo0"], 8, 1024, stage, "wo")
            pg.dma("sp", gt[:], W["l0_ln1g"], [], ["Eg"], key="Eg")
            pg.dma("sp", bt[:], W["l0_ln1b"], [], ["Eb"], key="Eb")
        pg.dma("sp", valid[:], g.valid, [], ["valid"], key="valid")
        def front(t):
            grp, j = t // 4, t % 4
            cb = grp % 2
            b = t % 2
            if j == 0:
                pg.dma("sp", cat[cb][:], g.catT[:, :, grp * 512:(grp + 1) * 512].rearrange("k p t -> p k t"), [],
                       [f"cat{cb}"], key=f"cat{cb}")
            pg.dma("sp", xs[b][:], g.xext[KH + t * 128:KH + (t + 1) * 128, :], [], [f"xs{b}"], key=f"xs{b}")
            for hlf in range(2):
                for k in range(8):
                    pg.mm(pmix[b][:, hlf * 512:(hlf + 1) * 512], cat[cb][:, k, j * 128:(j + 1) * 128],
                          wo[:, k, hlf * 512:(hlf + 1) * 512], k == 0, k == 7, [f"cat{cb}", f"wo{k}"],
                          [f"pmix{b}_{hlf}"])
                pg.stt("dve", r[b][:, hlf * 512:(hlf + 1) * 512], xs[b][:, hlf * 512:(hlf + 1) * 512], ALPHA,
                       pmix[b][:, hlf * 512:(hlf + 1) * 512], ALU.mult, ALU.add, [f"xs{b}", f"pmix{b}_{hlf}"],
                       [f"r{b}"])
            layer_norm_tile(C, r[b], xn[b], tmp[b], gt, bt, st6, mv, lnv, rstd, nmr, valid[:, t:t + 1], "E",
                            f"r{b}", f"xn{b}", ttag=f"tmp{b}", defer=True)

        def back(t):
            b = t % 2
            layer_norm_back(C, xn[b], tmp[b], gt, bt, valid[:, t:t + 1], "E", f"xn{b}", f"tmp{b}")
            pg.dma("pool", g.xmid[t * 128:(t + 1) * 128, :], xn[b][:], [f"xn{b}"], [], key=f"xn{b}")
            for k in range(8):
                pg.tr(psT[:, k, :], xn[b][:, k * 128:(k + 1) * 128], C.identF[:], [f"xn{b}", "identF"], ["psT"])
            pg.copy("act", xnT[b][:, 0:4, :], psT[:, 0:4, :], ["psT"], [f"xnT{b}"])
            pg.copy("dve", xnT[b][:, 4:8, :], psT[:, 4:8, :], ["psT"], [f"xnT{b}"])
            pg.dma("pool", g.xmidT[:, :, 1 + t * 128:1 + (t + 1) * 128].rearrange("k p t -> p k t"), xnT[b][:],
                   [f"xnT{b}"], [], key=f"xnT{b}")

        for t in range(sc.nt + 1):
            if t < sc.nt:
                front(t)
            if t >= 1:
                back(t - 1)


def phase_FFN(C, g, l):
    pg, nc, sc, W = C.pg, C.nc, g.c, C.W
    p = f"l{l}_"
    if l == 0:
        xT_src, x_src, dst, t_lo, ntok = g.xmidT, g.xmid, g.x1, 0, sc.T
    else:
        xT_src, x_src, dst, t_lo, ntok = g.xn1T, g.xn1, g.out, HALO, sc.NQ
    with ExitStack() as es:
        sb, ps = _pools(C, es)
        wup, fresh = shared_sb(C, "wup", [128, 8, 2 * DFF], BF16)
        wdn, _ = shared_sb(C, "wdn", [128, 22, 1024], BF16)
        cw, _ = shared_sb(C, "cw", [128, 44, 3], F32)
        cbt, _ = shared_sb(C, "cbt", [128, 44], F32)
        gt, _ = shared_sb(C, "gt", [128, 1024], F32)
        bt, _ = shared_sb(C, "bt", [128, 1024], F32)
        stage = [sb(f"stg{i}", [128, 512], F32) for i in range(2)]
        if fresh:
            load_cast(C, wup, W[p + "wup"], 8, 2 * DFF, stage, "wup", cwmax=512)
            load_cast(C, wdn, W[p + "wdn"], 22, 1024, stage, "wdn", cwmax=512)
        C.epsb = sb("epsb", [128, 1], F32)
        xTb = [sb(f"xTb{i}", [128, 8, NB + 2], BF16) for i in range(2)]
        uT = [sb(f"uT{i}", [128, 22, NB], BF16) for i in range(2)]
        cg = [sb(f"cg{i}", [128, NB], F32) for i in range(2)]
        cv = [sb(f"cv{i}", [128, NB], F32) for i in range(2)]
        gg = [sb(f"gg{i}", [128, NB], F32) for i in range(2)]
        xs = [sb(f"xs{i}", [128, 1024], F32) for i in range(2)]
        r = sb("r", [128, 1024], F32)
        tmp = sb("tmp", [128, 1024], F32)
        xo = [sb(f"xo{i}", [128, 1024], F32) for i in range(2)]
        st6 = sb("st6", [128, 2, 6], F32)
        mv = sb("mv", [128, 2], F32)
        lnv = sb("lnv", [128, 1], F32)
        rstd = sb("rstd", [128, 2], F32)
        nmr = sb("nmr", [128, 1], F32)
        ph = [ps(f"ph{i}", [128, 512]) for i in range(4)]
        py = [ps(f"py{i}", [128, 1024]) for i in range(2)]
        pg.memset("dve", C.epsb[:], EPS, ["epsb"])
        if fresh:
            pg.dma("sp", cw[:], W[p + "cw"], [], ["cw"], key="cw")
            pg.dma("sp", cbt[:], W[p + "cb"], [], ["cb"], key="cb")
            pg.dma("sp", gt[:], W[p + "ln2g"], [], ["Fg"], key="Fg")
            pg.dma("sp", bt[:], W[p + "ln2b"], [], ["Fb"], key="Fb")
        wup_r = [f"wup{k}" for k in range(8)]
        NW = NB + 2
        nblk = ntok // NB
        NT = NB // 128

        def down_steps(blk):
            tb = t_lo + blk * NB
            ub = blk % 2
            seq = []
            for j in range(NT):
                tok0 = tb + j * 128
                b = (blk * NT + j) % 2
                pj = py[j % 2]
                for hlf in range(2):
                    for c in range(22):
                        def mmf(j=j, hlf=hlf, c=c, pj=pj, b=b, tok0=tok0):
                            if hlf == 0 and c == 0:
                                pg.dma("sp", xs[b][:], x_src[tok0:tok0 + 128, :], [], [f"xs{b}"], key=f"xs{b}")
                            pg.mm(pj[:, hlf * 512:(hlf + 1) * 512], uT[ub][:, c, j * 128:(j + 1) * 128],
                                  wdn[:, c, hlf * 512:(hlf + 1) * 512], c == 0, c == 21, [f"uT{ub}_{c}", f"wdn{c}"],
                                  [f"py{j % 2}_{hlf}"])
                            if c == 21:
                                pg.stt("dve", r[:, hlf * 512:(hlf + 1) * 512], xs[b][:, hlf * 512:(hlf + 1) * 512],
                                       ALPHA, pj[:, hlf * 512:(hlf + 1) * 512], ALU.mult, ALU.add,
                                       [f"xs{b}", f"py{j % 2}_{hlf}"], ["r"])
                                if hlf == 1:
                                    layer_norm_tile(C, r, xo[b], tmp, gt, bt, st6, mv, lnv, rstd, nmr, None, "F",
                                                    "r", f"xo{b}")
                                    o0 = tok0 - t_lo
                                    pg.dma("pool", dst[o0:o0 + 128, :], xo[b][:], [f"xo{b}"], [], key=f"xo{b}")
                        seq.append(mmf)
            per = (len(seq) + 21) // 22
            return [seq[i * per:(i + 1) * per] for i in range(22)]

        for blk in range(nblk + 1):
            prev = down_steps(blk - 1) if blk >= 1 else None
            if blk < nblk:
                tb = t_lo + blk * NB
                xb = blk % 2
                ub = blk % 2
                pg.dma("sp", xTb[xb][:], xT_src[:, :, tb:tb + NW].rearrange("k p t -> p k t"), [], [f"xTb{xb}"],
                       key=f"xTb{xb}")
                if l == 0 and blk == 0:
                    pg.memset("pool", xTb[xb][:, :, 0:1], 0.0, [f"xTb{xb}"])
                if l == 0 and blk == nblk - 1:
                    pg.memset("pool", xTb[xb][:, :, NW - 1:NW], 0.0, [f"xTb{xb}"])
            for c in range(22):
                i2 = c % 2
                hg, hv = ph[2 * i2], ph[2 * i2 + 1]
                if blk < nblk:
                    for (hp, col0, tg) in ((hg, c * 128, "g"), (hv, DFF + c * 128, "v")):
                        for k in range(8):
                            pg.mm(hp[:, 0:NW], wup[:, k, col0:col0 + 128], xTb[xb][:, k, :], k == 0, k == 7,
                                  [f"xTb{xb}", wup_r[k]], [f"ph{i2}{tg}"])
                if prev is not None:
                    for f in prev[c]:
                        f()
                if blk < nblk:
                    for (hp, ch, dstt, tg) in ((hg, c, cg[i2], "g"), (hv, 22 + c, cv[i2], "v")):
                        pg.act(dstt[:, :], hp[:, 1:NB + 1], AF.Identity, [f"ph{i2}{tg}", "cw", "cb"], [f"c{tg}{i2}"],
                               scale=cw[:, ch, 1:2], bias=cbt[:, ch:ch + 1])
                        pg.stt("dve", dstt[:, :], hp[:, 0:NB], cw[:, ch, 0:1], dstt[:, :], ALU.mult, ALU.add,
                               [f"ph{i2}{tg}", "cw", f"c{tg}{i2}"], [f"c{tg}{i2}"])
                        pg.stt("dve", dstt[:, :], hp[:, 2:NB + 2], cw[:, ch, 2:3], dstt[:, :], ALU.mult, ALU.add,
                               [f"ph{i2}{tg}", "cw", f"c{tg}{i2}"], [f"c{tg}{i2}"])
                    pg.act(gg[i2][:, :], cg[i2][:, :], AF.Gelu, [f"cg{i2}"], [f"gg{i2}"])
                    pg.tt("pool", uT[ub][:, c, :], gg[i2][:, :], cv[i2][:, :], ALU.mult, [f"gg{i2}", f"cv{i2}"],
                          [f"uT{ub}_{c}"])


def phase_G(C, g):
    pg, nc, sc, W = C.pg, C.nc, g.c, C.W
    with ExitStack() as es:
        sb, ps = _pools(C, es)
        w1, fresh = shared_sb(C, "w1_sb", [128, 8, 1856], BF16)
        stage = [sb(f"stg{i}", [128, 2048], F32) for i in range(2)]
        l1cs = sb("l1cs", [128, sc.nt, 32], F32)
        xs = [sb(f"xs{i}", [128, 1024], F32) for i in range(2)]
        xT = [sb(f"xT{i}", [128, 8, 128], BF16) for i in range(2)]
        m1 = sb("m1", [128, 20, 16], F32)
        m2 = sb("m2", [128, 20, 16], F32)
        m3 = sb("m3", [128, 20, 16], F32)
        qk = [sb(f"qk{i}", [128, 20, 64], BF16) for i in range(2)]
        vst = [sb(f"vst{i}", [128, 256], BF16) for i in range(2)]
        qkT = [sb(f"qkT{i}", [128, 10, 128], BF16) for i in range(2)]
        psT = ps("psT", [128, 8, 128])
        pq = ps("pq", [128, 2048])
        pqT = ps("pqT", [128, 10, 128], BF16)
        if fresh:
            load_cast(C, w1, W["w1in"], 8, 1856, stage, "w1")
        pg.dma("sp", l1cs[:], g.l1cs, [], ["l1cs"], key="l1cs")
        w1_r = [f"w1{k}" for k in range(8)]
        for t in range(sc.nt):
            b = t % 2
            load_x_T(C, g.x1[t * 128:(t + 1) * 128, :], xs, xT, psT, b, "xT")
            for (c0, cw_, bank) in ((0, 512, 0), (512, 512, 1), (1024, 512, 2), (1536, 320, 3)):
                for k in range(8):
                    pg.mm(pq[:, bank * 512:bank * 512 + cw_], xT[b][:, k, :], w1[:, k, c0:c0 + cw_], k == 0, k == 7,
                          [f"xT{b}a" if k < 4 else f"xT{b}b", w1_r[k]], [f"pq{bank}"])
            qkv = pq[:, 0:1280].rearrange("p (h d) -> p h d", h=20)
            sw = pq[:, 1536:1856].rearrange("p (h d) -> p h d", h=20)
            cosb = l1cs[:, t, 0:16].unsqueeze(1).broadcast_to([128, 20, 16])
            sinb = l1cs[:, t, 16:32].unsqueeze(1).broadcast_to([128, 20, 16])
            pg.tt("dve", m1[:, :, :], qkv[:, :, 0:16], cosb, ALU.mult, ["pq0", "pq1", "pq2", "l1cs"], ["Gm1"])
            pg.tt("dve", m2[:, :, :], sw, sinb, ALU.mult, ["pq3", "l1cs"], ["Gm2"])
            pg.tt("pool", m3[:, :, :], m1[:, :, :], m2[:, :, :], ALU.add, ["Gm1", "Gm2"], ["Gm3"])
            pg.copy("act", qk[b][:, 0:16, 0:16], m3[:, 0:16, :], ["Gm3"], [f"qk{b}a"], scale=0.125)
            pg.copy("pool", qk[b][:, 16:20, 0:16], m3[:, 16:20, :], ["Gm3"], [f"qk{b}b"])
            pg.copy("act", qk[b][:, 0:16, 16:64], qkv[:, 0:16, 16:64], ["pq0", "pq1"], [f"qk{b}c"], scale=0.125)
            pg.copy("dve", qk[b][:, 16:20, 16:64], qkv[:, 16:20, 16:64], ["pq2"], [f"qk{b}d"])
            pg.copy("dve", vst[b][:, :], pq[:, 1280:1536], ["pq2"], [f"vst{b}"])
            pg.dma("pool", g.v1[t * 128:(t + 1) * 128, :], vst[b][:], [f"vst{b}"], [], key=f"vst{b}")
            qrd = [f"qk{b}a", f"qk{b}b", f"qk{b}c", f"qk{b}d", "identB"]
            for pr in range(10):
                pg.tr(pqT[:, pr, :], qk[b][:, 2 * pr:2 * pr + 2, :].rearrange("p h d -> p (h d)"), C.identB[:], qrd,
                      ["pqT"])
            pg.copy("act", qkT[b][:, 0:5, :], pqT[:, 0:5, :], ["pqT"], [f"qkT{b}"])
            pg.copy("dve", qkT[b][:, 5:10, :], pqT[:, 5:10, :], ["pqT"], [f"qkT{b}"])
            pg.dma("pool", g.q1T[:, :, t * 128:(t + 1) * 128].rearrange("k p t -> p k t"), qkT[b][:, 0:8, :],
                   [f"qkT{b}"], [], key=f"qkT{b}q")
            pg.dma("pool", g.k1T[:, :, t * 128:(t + 1) * 128].rearrange("k p t -> p k t"), qkT[b][:, 8:10, :],
                   [f"qkT{b}"], [], key=f"qkT{b}k")


def phase_H(C, g):
    pg, nc, sc, W = C.pg, C.nc, g.c, C.W
    with ExitStack() as es:
        sb, ps = _pools(C, es)
        wo, fresh = shared_sb(C, "wo_sb", [128, 8, 1024], BF16)
        gt, _ = shared_sb(C, "gt", [128, 1024], F32)
        bt, _ = shared_sb(C, "bt", [128, 1024], F32)
        sinks, _ = shared_sb(C, "sinks", [128, 16], F32)
        band, _ = shared_sb(C, "band", [128, 384], BF16)
        stage = [sb(f"stg{i}", [128, 2048], F32) for i in range(2)]
        valid = sb("valid", [128, sc.nt], F32)
        kpen = sb("kpen", [1, sc.T], BF16)
        C.epsb = sb("epsb", [128, 1], F32)
        k1T = sb("k1T", [128, 2, sc.T], BF16)
        v1 = sb("v1", [128, sc.nt, 256], BF16)
        q1 = [sb(f"q1{i}", [128, 8, 128], BF16) for i in range(2)]
        P = [sb(f"P{i}", [128, 384], BF16) for i in range(2)]
        PT = [sb(f"PT{i}", [128, 3, 128], BF16) for i in range(2)]
        mx = [sb(f"mx{i}", [128, 2], F32) for i in range(2)]
        rs = sb("rs", [128, 16], F32)
        es_ = sb("es_", [128, 16], F32)
        rinv = sb("rinv", [128, 16], F32)
        at = sb("at", [128, 1024], BF16)
        atT = sb("atT", [128, 8, 128], BF16)
        xs = [sb(f"xs{i}", [128, 1024], F32) for i in range(2)]
        r = sb("r", [128, 1024], F32)
        tmp = [sb(f"tmp{i}", [128, 1024], F32) for i in range(2)]
        xn = [sb(f"xn{i}", [128, 1024], F32) for i in range(2)]
        xnT = [sb(f"xnT{i}", [128, 8, 128], BF16) for i in range(2)]
        st6 = sb("st6", [128, 2, 6], F32)
        mv = sb("mv", [128, 2], F32)
        lnv = sb("lnv", [128, 1], F32)
        rstd = sb("rstd", [128, 2], F32)
        nmr = sb("nmr", [128, 1], F32)
        pS = [ps(f"pS{i}", [128, 512]) for i in range(2)]
        pPT = [ps(f"pPT{i}", [128, 8, 128], BF16) for i in range(2)]
        pO = ps("pO", [128, 1024])
        pmix = ps("pmix", [128, 1024])
        pg.memset("dve", C.epsb[:], EPS, ["epsb"])
        if fresh:
            load_cast(C, wo, W["wo1"], 8, 1024, stage, "wo")
            pg.dma("sp", gt[:], W["l1_ln1g"], [], ["Hg"], key="Hg")
            pg.dma("sp", bt[:], W["l1_ln1b"], [], ["Hb"], key="Hb")
            pg.dma("sp", sinks[:], W["sinks"], [], ["sinks"], key="sinks")
            pg.dma("sp", band[:], W["band"], [], ["band"], key="band")
        pg.dma("sp", valid[:], g.valid, [], ["valid"], key="valid")
        pg.dma("sp", kpen[:], g.kpen, [], ["kpen"], key="kpen")
        pg.dma("sp", k1T[:], g.k1T.rearrange("k p t -> p k t"), [], ["k1T"], key="k1T")
        pg.dma("sp", v1[:], g.v1.rearrange("(t p) c -> p t c", p=128), [], ["v1"], key="v1")
        def front(i):
            b = i % 2
            pg.dma("sp", q1[b][:], g.q1T[:, :, i * 128:(i + 1) * 128].rearrange("k p t -> p k t"), [], [f"q1{b}"],
                   key=f"q1{b}")
            pg.dma("sp", xs[b][:], g.x1[i * 128:(i + 1) * 128, :], [], [f"xs{b}"], key=f"xs{b}")
            k0 = (i - 1) * 128

            def s1(hd):
                sbuf = hd % 2
                half = hd % 2
                gk = QPERM[hd] // 4
                o = pS[sbuf][:, 0:384]
                pg.mm(o, q1[b][half * 64:(half + 1) * 64, hd // 2, :],
                      k1T[half * 64:(half + 1) * 64, gk // 2, k0:k0 + 384], True, False, [f"q1{b}", "k1T"],
                      [f"pS{sbuf}"])
                pg.mm(o, C.identB[:], band[:, :], False, False, ["identB", "band"], [f"pS{sbuf}"])
                pg.mm(o, C.onesB[0:1, 0:128], kpen[0:1, k0:k0 + 384], False, True, ["onesB", "kpen"], [f"pS{sbuf}"])

            def s2(hd):
                sbuf = hd % 2
                pg.add("dve", lambda e, o=mx[sbuf][:, 0:1], i_=pS[sbuf][:, 0:384]: e.reduce_max(
                    out=o, in_=i_, axis=AX.X), [f"pS{sbuf}"], [f"mxa{sbuf}"])
                pg.ts("dve", mx[sbuf][:, 1:2], mx[sbuf][:, 0:1], sinks[:, hd:hd + 1], -1.0, ALU.max, ALU.mult,
                      [f"mxa{sbuf}", "sinks"], [f"mx{sbuf}"])
                pg.act(P[sbuf][:, :], pS[sbuf][:, 0:384], AF.Exp, [f"pS{sbuf}", f"mx{sbuf}"], [f"P{sbuf}", f"rs{hd}"],
                       bias=mx[sbuf][:, 1:2], accum_out=rs[:, hd:hd + 1])
                pg.act(es_[:, hd:hd + 1], sinks[:, hd:hd + 1], AF.Exp, ["sinks", f"mx{sbuf}"], [f"es{hd}"],
                       bias=mx[sbuf][:, 1:2])

            def s3(hd):
                sbuf = hd % 2
                for kt in range(3):
                    pg.tr(pPT[sbuf][:, kt, :], P[sbuf][:, kt * 128:(kt + 1) * 128], C.identB[:],
                          [f"P{sbuf}", "identB"], [f"pPT{sbuf}"])
                pg.copy("dve" if hd % 2 else "act", PT[sbuf][:, :, :], pPT[sbuf][:, 0:3, :], [f"pPT{sbuf}"],
                        [f"PT{sbuf}"])

            def s4(hd):
                sbuf = hd % 2
                gk = QPERM[hd] // 4
                for kt in range(3):
                    pg.mm(pO[:, hd * 64:(hd + 1) * 64], PT[sbuf][:, kt, :], v1[:, i - 1 + kt, gk * 64:(gk + 1) * 64],
                          kt == 0, kt == 2, [f"PT{sbuf}", "v1"], [f"pO{hd // 8}"])

            for step in range(16 + 3):
                if step < 16:
                    s1(step)
                if 0 <= step - 1 < 16:
                    s2(step - 1)
                if 0 <= step - 2 < 16:
                    s3(step - 2)
                if 0 <= step - 3 < 16:
                    s4(step - 3)
            pg.tt("dve", rinv[:, :], rs[:, :], es_[:, :], ALU.add, [f"rs{h}" for h in range(16)] +
                  [f"es{h}" for h in range(16)], ["rinv0"])
            pg.add("dve", lambda e: e.reciprocal(out=rinv[:, :], in_=rinv[:, :]), ["rinv0"], ["rinv"])
            pg.tt("dve", at[:, :].rearrange("p (h d) -> p h d", h=16), pO[:, :].rearrange("p (h d) -> p h d", h=16),
                  rinv[:, :].unsqueeze(2).broadcast_to([128, 16, 64]), ALU.mult, ["pO0", "pO1", "rinv"], ["at"])
            for k in range(8):
                pg.tr(pPT[0][:, k, :], at[:, k * 128:(k + 1) * 128], C.identB[:], ["at", "identB"], ["pPT0"])
            pg.copy("act", atT[:, :, :], pPT[0][:, :, :], ["pPT0"], ["atT"])
            for hlf in range(2):
                for k in range(8):
                    pg.mm(pmix[:, hlf * 512:(hlf + 1) * 512], atT[:, k, :], wo[:, k, hlf * 512:(hlf + 1) * 512],
                          k == 0, k == 7, ["atT", f"wo{k}"], [f"pmix{hlf}"])
                pg.stt("dve", r[:, hlf * 512:(hlf + 1) * 512], xs[b][:, hlf * 512:(hlf + 1) * 512], ALPHA,
                       pmix[:, hlf * 512:(hlf + 1) * 512], ALU.mult, ALU.add, [f"xs{b}", f"pmix{hlf}"], ["r"])
            layer_norm_tile(C, r, xn[b], tmp[b], gt, bt, st6, mv, lnv, rstd, nmr, valid[:, i:i + 1], "H", "r",
                            f"xn{b}", ttag=f"tmp{b}", defer=True)

        def back(i):
            b = i % 2
            layer_norm_back(C, xn[b], tmp[b], gt, bt, valid[:, i:i + 1], "H", f"xn{b}", f"tmp{b}")
            pg.dma("pool", g.xn1[i * 128:(i + 1) * 128, :], xn[b][:], [f"xn{b}"], [], key=f"xn{b}")
            psT = pO[:, :].rearrange("p (k t) -> p k t", k=8)
            for k in range(8):
                pg.tr(psT[:, k, :], xn[b][:, k * 128:(k + 1) * 128], C.identF[:], [f"xn{b}", "identF"],
                      ["pO0", "pO1"])
            pg.copy("act", xnT[b][:, 0:4, :], psT[:, 0:4, :], ["pO0"], [f"xnT{b}"])
            pg.copy("dve", xnT[b][:, 4:8, :], psT[:, 4:8, :], ["pO1"], [f"xnT{b}"])
            pg.dma("pool", g.xn1T[:, :, 1 + i * 128:1 + (i + 1) * 128].rearrange("k p t -> p k t"), xnT[b][:],
                   [f"xnT{b}"], [], key=f"xnT{b}")


        for i in range(1, sc.nt):
            if i < sc.nt - 1:
                front(i)
            if i >= 2:
                back(i - 1)


def make_in_maps(cfg, inputs, n_cores=8):
    sh = prep_shared(inputs)
    P, Sg = cfg
    maps = []
    for c in range(n_cores):
        m = dict(sh)
        m.update(prep_seg(P, inputs["x_prompt"][c // 2], (c % 2) * P.NQ))
        m.update(prep_seg(Sg, inputs["x_sample"][0], c * Sg.NQ))
        maps.append(m)
    return maps


def assemble(cfg, res, inputs):
    P, Sg = cfg
    yp = np.zeros(inputs["x_prompt"].shape, np.float32)
    ys = np.zeros(inputs["x_sample"].shape, np.float32)
    for c in range(8):
        yp[c // 2, (c % 2) * P.NQ:(c % 2 + 1) * P.NQ] = res[c]["P_out"]
        ys[0, c * Sg.NQ:(c + 1) * Sg.NQ] = res[c]["S_out"]
    return yp, ys


def kernel(**inputs):
    inputs = {k: np.asarray(v) for k, v in inputs.items()}
    cfg = FULL_CFG
    nc, _ = build(cfg)
    maps = make_in_maps(cfg, inputs)
    res = run_bass_kernel_spmd(nc, maps, core_ids=list(range(8)))
    return assemble(cfg, res.results, inputs)
```

```python
import numpy as np
import ml_dtypes
from contextlib import ExitStack
import concourse.bass as bass
import concourse.mybir as mybir
from concourse.bass_utils import run_bass_kernel_spmd

F32 = mybir.dt.float32
BF16 = mybir.dt.bfloat16
AF = mybir.ActivationFunctionType
ALU = mybir.AluOpType
AX = mybir.AxisListType

D = 1024
DFF = 2816
NEG = -30000.0
ALPHA = float(4 ** 0.25)
EPS = 1e-5
HALO = 256
KH = 512
NB = 256
QPERM = [0, 4, 1, 5, 2, 6, 3, 7, 8, 12, 9, 13, 10, 14, 11, 15]
MLA_SCALE = float(96 ** -0.5)


class SegCfg:
    def __init__(self, name, S, NQ):
        self.name = name
        self.S = S
        self.NQ = NQ
        self.T = NQ + 2 * HALO
        self.T2 = self.T + 2 * KH
        self.nt = self.T // 128
        self.nt2 = self.T2 // 128
        self.nts = S // 128
        self.nqt = self.nt + 1


FULL_CFG = [SegCfg("P", 8192, 4096), SegCfg("S", 16384, 2048)]


DMAQ = {"pool": "sp"}
LOADQ = "act"
HOIST = True
NOCLEAR = False


class _Op:
    __slots__ = ("eng", "fn", "reads", "writes", "dma", "key", "needs_inc", "count", "deps",
                 "sem", "semval", "waits")


class Prog:
    ENGS = ("pe", "act", "dve", "pool", "sp")

    def __init__(self, nc, es, ndma=38):
        self.nc = nc
        self.sets = []
        for s in range(2):
            eng = {e: es.enter_context(nc.semaphore(f"s{s}_{e}")) for e in self.ENGS}
            dma = [es.enter_context(nc.semaphore(f"s{s}_d{i}")) for i in range(ndma)]
            self.sets.append((eng, dma))
        self.phase = 0
        self.ops = []
        self.first = True

    def add(self, eng, fn, reads=(), writes=(), dma=False, key=None):
        op = _Op()
        op.eng = eng
        op.fn = fn
        op.reads = tuple(reads)
        op.writes = tuple(writes)
        op.dma = dma
        op.key = key
        op.needs_inc = False
        op.count = 0
        self.ops.append(op)
        return op

    def mm(self, out, lhsT, rhs, start, stop, reads, writes):
        self.add("pe", lambda h: h.matmul(out, lhsT, rhs, start=start, stop=stop), reads, writes)

    def tr(self, out, in_, ident, reads, writes):
        self.add("pe", lambda h: h.transpose(out, in_, ident), reads, writes)

    def act(self, out, in_, func, reads, writes, **kw):
        self.add("act", lambda h: h.activation(out=out, in_=in_, func=func, **kw), reads, writes)

    def copy(self, eng, out, in_, reads, writes, scale=None):
        if eng == "act":
            if scale is None:
                self.add("act", lambda h: h.activation(out=out, in_=in_, func=AF.Copy), reads, writes)
            else:
                self.add("act", lambda h: h.activation(out=out, in_=in_, func=AF.Copy, scale=scale),
                         reads, writes)
        else:
            if scale is None:
                self.add(eng, lambda h: h.tensor_copy(out=out, in_=in_), reads, writes)
            else:
                self.add(eng, lambda h: h.tensor_scalar(out=out, in0=in_, scalar1=scale, scalar2=None,
                                                        op0=ALU.mult), reads, writes)

    def tt(self, eng, out, in0, in1, op, reads, writes):
        self.add(eng, lambda h: h.tensor_tensor(out=out, in0=in0, in1=in1, op=op), reads, writes)

    def ts(self, eng, out, in0, s1, s2, op0, op1, reads, writes):
        if op1 is None:
            self.add(eng, lambda h: h.tensor_scalar(out=out, in0=in0, scalar1=s1, scalar2=None, op0=op0),
                     reads, writes)
        else:
            self.add(eng, lambda h: h.tensor_scalar(out=out, in0=in0, scalar1=s1, scalar2=s2, op0=op0,
                                                    op1=op1), reads, writes)

    def stt(self, eng, out, in0, scalar, in1, op0, op1, reads, writes):
        self.add(eng, lambda h: h.scalar_tensor_tensor(out=out, in0=in0, scalar=scalar, in1=in1,
                                                       op0=op0, op1=op1), reads, writes)

    def memset(self, eng, ap, val, writes):
        self.add(eng, lambda h: h.memset(ap, val), (), writes)

    def dma(self, q, out, in_, reads, writes, key):
        q = DMAQ.get(q, q)
        if writes and LOADQ:
            q = LOADQ
        self.add(q, lambda h: h.dma_start(out=out, in_=in_), reads, writes, dma=True, key=key)

    def flush(self):
        nc = self.nc
        engsem, dmapool = self.sets[self.phase % 2]
        oeng, odma = self.sets[(self.phase + 1) % 2]
        ops = self.ops
        if HOIST:
            touch = {}
            keyed = []
            for i, op in enumerate(ops):
                k = float(i)
                if op.dma and op.writes:
                    prev = [touch[b] for b in op.writes if b in touch]
                    if prev:
                        k = max(prev) + 0.5 + 1e-6 * (i % 1000)
                keyed.append((k, i, op))
                for b in op.reads + op.writes:
                    touch[b] = max(touch.get(b, -1.0), k)
            keyed.sort(key=lambda x: (x[0], x[1]))
            ops = [x[2] for x in keyed]
        last_w = {}
        readers = {}
        dmakey = {}
        for op in ops:
            deps = []
            for b in op.reads:
                w = last_w.get(b)
                if w is not None:
                    deps.append((w, 0))
            for b in op.writes:
                w = last_w.get(b)
                if w is not None:
                    deps.append((w, 1))
                for r in readers.get(b, {}).values():
                    deps.append((r, 2))
            for b in op.writes:
                last_w[b] = op
                readers[b] = {}
            for b in op.reads:
                readers.setdefault(b, {})[(op.eng, op.key if op.dma else None)] = op
            op.deps = []
            for (s, kind) in deps:
                if s is op:
                    continue
                if (not s.dma) and (not op.dma) and s.eng == op.eng:
                    if kind != 0 or op.eng == "pe":
                        continue
                op.deps.append(s)
                if not s.dma:
                    s.needs_inc = True
            if op.dma:
                ent = dmakey.get(op.key)
                if ent is None:
                    assert len(dmakey) < len(dmapool), "out of dma semaphores"
                    ent = [dmapool[len(dmakey)], 0, None]
                    dmakey[op.key] = ent
                if ent[2] is not None:
                    op.deps.append(ent[2])
                ent[2] = op
                ent[1] += 16
                op.sem = ent[0]
                op.semval = ent[1]
        cnt = {e: 0 for e in self.ENGS}
        for op in ops:
            if (not op.dma) and op.needs_inc:
                cnt[op.eng] += 1
                op.count = cnt[op.eng]
        waited = {e: {} for e in self.ENGS}
        for op in ops:
            need = {}
            for s in op.deps:
                if s.dma:
                    sid, sem, val = ("d", id(s.sem)), s.sem, s.semval
                else:
                    sid, sem, val = ("e", s.eng), engsem[s.eng], s.count
                if sid not in need or need[sid][1] < val:
                    need[sid] = (sem, val)
            op.waits = []
            wd = waited[op.eng]
            for sid, (sem, val) in need.items():
                if wd.get(sid, 0) >= val:
                    continue
                wd[sid] = val
                op.waits.append((sem, val))
        per = {e: [] for e in self.ENGS}
        for op in ops:
            per[op.eng].append(op)
        finals = [(ent[0], ent[1]) for ent in dmakey.values()]
        first = self.first

        def mk(e):
            def body(h):
                if e == "sp" and not first and not NOCLEAR:
                    for s_ in list(oeng.values()) + list(odma):
                        h.sem_clear(s_)
                for op in per[e]:
                    for (sem, val) in op.waits:
                        h.wait_ge(sem, val)
                    ins = op.fn(h)
                    if op.dma:
                        ins.then_inc(op.sem, 16)
                    elif op.needs_inc:
                        ins.then_inc(engsem[e], 1)
                if e == "sp":
                    for (sem, val) in finals:
                        h.wait_ge(sem, val)
            return body

        with nc.Block() as block:
            block.tensor(mk("pe"))
            block.scalar(mk("act"))
            block.vector(mk("dve"))
            block.gpsimd(mk("pool"))
            block.sync(mk("sp"))
        self.n_ops = getattr(self, "n_ops", 0) + len(ops)
        self.ops = []
        self.phase += 1
        self.first = False


def _rope_tab(pos, theta, rot_dim):
    half = rot_dim // 2
    inv = (np.float32(1.0) / (np.float32(theta) ** (np.arange(0, rot_dim, 2, dtype=np.float32)
                                                      / np.float32(rot_dim)))).astype(np.float32)
    ang = pos.astype(np.float32)[:, None] * inv[None, :]
    cos = np.cos(ang).astype(np.float32)
    sin = np.sin(ang).astype(np.float32)
    c2 = np.concatenate([cos, cos], axis=1)
    s2 = np.concatenate([-sin, sin], axis=1)
    return c2, s2


def _bc(v, n=128):
    return np.ascontiguousarray(np.broadcast_to(np.asarray(v, np.float32)[None, :], (n, v.shape[0])))


def prep_shared(inp):
    sh = {}
    w = inp["l0_w_in"]
    kr = w[:, 2176:2208]
    sh["wa"] = np.ascontiguousarray(np.concatenate([w[:, 1920:2208], kr[:, 16:32], kr[:, 0:16]], axis=1))
    sh["wb"] = np.ascontiguousarray(w[:, 0:1920])
    wq = inp["l0_w_q_up"].reshape(384, 8, 96)
    sh["wq"] = np.ascontiguousarray(wq.reshape(384, 768))
    wqs = np.concatenate([wq[:, :, 0:64], wq[:, :, 80:96], wq[:, :, 64:80]], axis=2)
    sh["wqs"] = np.ascontiguousarray(wqs.reshape(384, 768))
    wkv = inp["l0_w_kv_up"].reshape(256, 8, 128)
    sh["wkvK"] = np.ascontiguousarray(wkv[:, :, 0:64].reshape(256, 512))
    sh["wkvV"] = np.ascontiguousarray(wkv[:, :, 64:128].reshape(256, 512))
    sh["gq"] = _bc(inp["l0_g_q_norm"])
    sh["gkv"] = _bc(inp["l0_g_kv_norm"])
    sh["wo0"] = np.ascontiguousarray(inp["l0_w_out"])
    rpb = inp["l0_rpb"]
    B = np.full((8, 2, 64, 16, 64), NEG, np.float32)
    c = np.arange(64)
    cs = np.clip(c - 8, 0, 48)
    for i in range(2):
        for wv in range(16):
            dr = wv - i
            if dr < 0 or dr > 14:
                continue
            for cc in range(64):
                kc = cs[cc] + np.arange(16)
                B[:, i, cc, wv, kc] = rpb[:, dr, kc - cc + 15]
    sh["nab"] = np.ascontiguousarray(B.reshape(8, 128, 1024).transpose(1, 0, 2))
    for l in (0, 1):
        p = f"l{l}_"
        sh[p + "ln1g"] = _bc(inp[p + "ln1_g"])
        sh[p + "ln1b"] = _bc(inp[p + "ln1_b"])
        sh[p + "ln2g"] = _bc(inp[p + "ln2_g"])
        sh[p + "ln2b"] = _bc(inp[p + "ln2_b"])
        sh[p + "wup"] = np.ascontiguousarray(inp[p + "ffn_w_up"])
        sh[p + "wdn"] = np.ascontiguousarray(inp[p + "ffn_w_down"])
        cw = inp[p + "ffn_conv_w"]
        sh[p + "cw"] = np.ascontiguousarray(cw.reshape(3, 44, 128).transpose(2, 1, 0))
        sh[p + "cb"] = np.ascontiguousarray(inp[p + "ffn_conv_b"].reshape(44, 128).T)
    w1 = inp["l1_w_in"]
    q = w1[:, 0:1024].reshape(D, 16, 64)[:, QPERM, :]
    k = w1[:, 1024:1280].reshape(D, 4, 64)
    qk = np.concatenate([q, k], axis=1)
    sw = np.concatenate([qk[:, :, 8:16], qk[:, :, 0:8]], axis=2)
    sh["w1in"] = np.ascontiguousarray(np.concatenate([qk.reshape(D, 1280), w1[:, 1280:1536],
                                                      sw.reshape(D, 320)], axis=1))
    sh["wo1"] = np.ascontiguousarray(inp["l1_w_out"].reshape(16, 64, D)[QPERM].reshape(1024, D))
    sh["sinks"] = _bc(inp["l1_sinks"][QPERM])
    sh["ident"] = np.eye(128, dtype=np.float32)
    a = np.arange(128)[:, None]
    j = np.arange(384)[None, :]
    sh["band"] = np.where((j >= a) & (j <= a + 256), 0.0, NEG).astype(ml_dtypes.bfloat16)
    qs = np.zeros((2, 128), np.float32)
    qs[0, 0:64] = 1.0
    qs[1, 64:128] = 1.0
    sh["qsel"] = qs.astype(ml_dtypes.bfloat16)
    return sh


def prep_seg(seg, x_seq, a):
    S, T, T2 = seg.S, seg.T, seg.T2
    n = seg.name
    m = {}
    m[n + "_xseq"] = np.ascontiguousarray(x_seq)
    base2 = a - HALO - KH
    xe = np.zeros((T2, D), np.float32)
    lo, hi = max(base2, 0), min(base2 + T2, S)
    xe[lo - base2:hi - base2] = x_seq[lo:hi]
    m[n + "_xext"] = xe
    pos_e = np.arange(a - HALO, a - HALO + T)
    valid = ((pos_e >= 0) & (pos_e < S)).astype(np.float32)
    m[n + "_valid"] = np.ascontiguousarray(valid.reshape(seg.nt, 128).T)
    c2, s2 = _rope_tab(np.arange(S), 10000.0, 32)
    kcs = np.concatenate([c2, s2], axis=1).reshape(seg.nts, 128, 64).transpose(1, 0, 2)
    m[n + "_kcs"] = np.ascontiguousarray(kcs)
    pos2 = np.clip(np.arange(base2, base2 + T2), 0, S - 1)
    c2, s2 = _rope_tab(pos2, 10000.0, 32)
    m[n + "_qcsT"] = np.ascontiguousarray(np.concatenate([c2.T, s2.T], axis=0))
    c2, s2 = _rope_tab(np.clip(pos_e, 0, S - 1), 500000.0, 16)
    l1cs = np.concatenate([c2, s2], axis=1).reshape(seg.nt, 128, 32).transpose(1, 0, 2)
    m[n + "_l1cs"] = np.ascontiguousarray(l1cs)
    m[n + "_kpen"] = np.where(valid > 0, 0.0, NEG).astype(ml_dtypes.bfloat16)[None, :]
    rows = S // 64
    pen = np.zeros((seg.nqt, 2, 16, 64), np.float32)
    for j in range(seg.nqt):
        r_abs = (a - HALO) // 64 + 2 * j - 1
        for i in range(2):
            rq = r_abs + i
            if rq < 0 or rq >= rows:
                continue
            rs = min(max(rq - 4, 0), rows - 8)
            for wv in range(16):
                krow = r_abs - 7 + wv
                if not (rs <= krow < rs + 8):
                    pen[j, i, wv, :] = NEG
    m[n + "_pen"] = pen.reshape(seg.nqt, 2, 1024).astype(ml_dtypes.bfloat16)
    return m


class Ctx:
    pass


def build(cfg, debug=()):
    nc = bass.Bass("TRN2", target_bir_lowering=False)
    C = Ctx()
    C.nc = nc
    C.debug = set(debug)

    def din(name, shape, dt=F32):
        return nc.dram_tensor(name, list(shape), dt, kind="ExternalInput").ap()

    def dscr(name, shape, dt):
        if name in C.debug:
            return nc.dram_tensor(name, list(shape), dt, kind="ExternalOutput").ap()
        return nc.dram_tensor(name, list(shape), dt).ap()

    W = {}
    W["wa"] = din("wa", [D, 320])
    W["wb"] = din("wb", [D, 1920])
    W["wq"] = din("wq", [384, 768])
    W["wqs"] = din("wqs", [384, 768])
    W["wkvK"] = din("wkvK", [256, 512])
    W["wkvV"] = din("wkvV", [256, 512])
    W["gq"] = din("gq", [128, 384])
    W["gkv"] = din("gkv", [128, 256])
    W["wo0"] = din("wo0", [D, D])
    W["nab"] = din("nab", [128, 8, 1024])
    for l in (0, 1):
        p = f"l{l}_"
        for nm in ("ln1g", "ln1b", "ln2g", "ln2b"):
            W[p + nm] = din(p + nm, [128, D])
        W[p + "wup"] = din(p + "wup", [D, 2 * DFF])
        W[p + "wdn"] = din(p + "wdn", [DFF, D])
        W[p + "cw"] = din(p + "cw", [128, 44, 3])
        W[p + "cb"] = din(p + "cb", [128, 44])
    W["w1in"] = din("w1in", [D, 1856])
    W["wo1"] = din("wo1", [D, D])
    W["sinks"] = din("sinks", [128, 16])
    W["ident"] = din("ident", [128, 128])
    W["band"] = din("band", [128, 384], BF16)
    W["qsel"] = din("qsel", [2, 128], BF16)
    C.W = W

    segs = []
    for sc in cfg:
        n = sc.name
        g = Ctx()
        g.c = sc
        g.xseq = din(n + "_xseq", [sc.S, D])
        g.xext = din(n + "_xext", [sc.T2, D])
        g.valid = din(n + "_valid", [128, sc.nt])
        g.kcs = din(n + "_kcs", [128, sc.nts, 64])
        g.qcsT = din(n + "_qcsT", [64, sc.T2])
        g.l1cs = din(n + "_l1cs", [128, sc.nt, 32])
        g.kpen = din(n + "_kpen", [1, sc.T], BF16)
        g.pen = din(n + "_pen", [sc.nqt, 2, 1024], BF16)
        g.Ks = dscr(n + "_Ks", [8, 97, sc.S], BF16)
        g.Vs = dscr(n + "_Vs", [8, 128, sc.nts, 65], BF16)
        g.qaug = dscr(n + "_qaug", [8, 96, sc.T2], BF16)
        g.qaT = dscr(n + "_qaT", [4, 128, sc.T2], BF16)
        g.kaT = dscr(n + "_kaT", [4, 128, sc.T2], BF16)
        g.va = dscr(n + "_va", [sc.T2, 512], BF16)
        g.catT = dscr(n + "_catT", [8, 128, sc.T], BF16)
        g.xmid = dscr(n + "_xmid", [sc.T, D], F32)
        g.xmidT = dscr(n + "_xmidT", [8, 128, sc.T + 2], BF16)
        g.x1 = dscr(n + "_x1", [sc.T, D], F32)
        g.q1T = dscr(n + "_q1T", [8, 128, sc.T], BF16)
        g.k1T = dscr(n + "_k1T", [2, 128, sc.T], BF16)
        g.v1 = dscr(n + "_v1", [sc.T, 256], BF16)
        g.xn1 = dscr(n + "_xn1", [sc.T, D], F32)
        g.xn1T = dscr(n + "_xn1T", [8, 128, sc.T + 2], BF16)
        g.out = nc.dram_tensor(n + "_out", [sc.NQ, D], F32, kind="ExternalOutput").ap()
        segs.append(g)

    with ExitStack() as es:
        pg = Prog(nc, es)
        C.pg = pg
        C.identF = es.enter_context(nc.sbuf_tensor("identF", [128, 128], F32))
        C.identB = es.enter_context(nc.sbuf_tensor("identB", [128, 128], BF16))
        C.ones = es.enter_context(nc.sbuf_tensor("ones", [128, 64], F32))
        C.onesB = es.enter_context(nc.sbuf_tensor("onesB", [128, 128], BF16))
        C.zerosB = es.enter_context(nc.sbuf_tensor("zerosB", [128, 8], BF16))
        pg.dma("sp", C.identF[:], W["ident"], [], ["identF"], key="identF")
        pg.copy("dve", C.identB[:], C.identF[:], ["identF"], ["identB"])
        pg.memset("pool", C.ones[:], 1.0, ["ones"])
        pg.memset("pool", C.onesB[:], 1.0, ["onesB"])
        pg.memset("pool", C.zerosB[:], 0.0, ["zerosB"])
        pg.flush()
        stop_after = C.debug_stop = [d for d in C.debug if d.startswith("stop:")]
        stop = stop_after[0][5:] if stop_after else None
        phases = [("A", phase_A), ("B", phase_B), ("C", phase_C), ("D", phase_D), ("E", phase_E),
                  ("F", lambda C_, g_: phase_FFN(C_, g_, 0)), ("G", phase_G), ("H", phase_H),
                  ("I", lambda C_, g_: phase_FFN(C_, g_, 1))]
        only = [d[5:] for d in C.debug if d.startswith("only:")]
        for (pn, fn) in phases:
            with ExitStack() as pes:
                C.pes = pes
                C.shared = {}
                for g in segs:
                    if only and pn not in only[0]:
                        continue
                    fn(C, g)
                    pg.flush()
            if stop == pn:
                break
    C.n_ops = pg.n_ops
    return nc, C


_UID = [0]


def shared_sb(C, name, shp, dt):
    if name in C.shared:
        return C.shared[name], False
    _UID[0] += 1
    t = C.pes.enter_context(C.nc.sbuf_tensor(f"sh{_UID[0]}_{name}", list(shp), dt))
    C.shared[name] = t
    return t, True


def _pools(C, es):
    nc = C.nc
    _UID[0] += 1
    u = _UID[0]

    def sb(n, shp, dt):
        return es.enter_context(nc.sbuf_tensor(f"sb{u}_{n}", list(shp), dt))

    def ps(n, shp, dt=F32):
        return es.enter_context(nc.psum_tensor(f"ps{u}_{n}", list(shp), dt))

    return sb, ps


def load_cast(C, dst, src, K, N, stage, name, engs=("pool", "act", "dve"), ctr=[0], cwmax=2048):
    pg = C.pg
    for k in range(K):
        for c0 in range(0, N, cwmax):
            cw = min(cwmax, N - c0)
            i = ctr[0]
            ctr[0] += 1
            b = i % 2
            pg.dma("sp", stage[b][:, 0:cw], src[k * 128:(k + 1) * 128, c0:c0 + cw], [], [f"stg{b}"],
                   key=f"stg{b}")
            pg.copy(engs[i % len(engs)], dst[:, k, c0:c0 + cw], stage[b][:, 0:cw], [f"stg{b}"],
                    [f"{name}{k}"])


def load_x_T(C, src_rows, xs, xT, psT, b, tagx):
    pg = C.pg
    pg.dma("sp", xs[b][:], src_rows, [], [f"xs{b}"], key=f"xs{b}")
    for k in range(8):
        pg.tr(psT[:, k, :], xs[b][:, k * 128:(k + 1) * 128], C.identF[:], [f"xs{b}", "identF"], ["psT"])
    pg.copy("act", xT[b][:, 0:4, :], psT[:, 0:4, :], ["psT"], [f"{tagx}{b}a"])
    pg.copy("dve", xT[b][:, 4:8, :], psT[:, 4:8, :], ["psT"], [f"{tagx}{b}b"])


def rms_rstd(C, ss, lnv, rstd, n, tag):
    pg = C.pg
    pg.act(lnv[:, 0:1], ss[:, 0:1], AF.Ln, [tag + "ss"], [tag + "lnv"], scale=1.0 / n, bias=C.epsb[:, 0:1])
    pg.act(rstd[:, 0:1], lnv[:, 0:1], AF.Exp, [tag + "lnv"], [tag + "rstd"], scale=-0.5)


def phase_A(C, g):
    pg, nc, sc, W = C.pg, C.nc, g.c, C.W
    with ExitStack() as es:
        sb, ps = _pools(C, es)
        wa = sb("wa_sb", [128, 8, 320], BF16)
        wkK = sb("wkK", [128, 2, 512], BF16)
        wkV = sb("wkV", [128, 2, 512], BF16)
        gkv = sb("gkv", [128, 256], F32)
        kcs = sb("kcs", [128, sc.nts, 64], F32)
        C.epsb = sb("epsb", [128, 1], F32)
        stage = [sb(f"stg{i}", [128, 2048], F32) for i in range(2)]
        xs = [sb(f"xs{i}", [128, 1024], F32) for i in range(2)]
        xT = [sb(f"xT{i}", [128, 8, 128], BF16) for i in range(2)]
        junk = sb("junk", [128, 256], F32)
        ss = sb("ss", [128, 1], F32)
        lnv = sb("lnv", [128, 1], F32)
        rstd = sb("rstd", [128, 1], F32)
        ckr = sb("ckr", [128, 352], BF16)
        m1 = sb("m1", [128, 32], F32)
        m2 = sb("m2", [128, 32], F32)
        cTg = sb("cTg", [128, 2, 512], BF16)
        kst = sb("kst", [64, 8, 512], BF16)
        krT = sb("krT", [32, 512], BF16)
        vst = [sb(f"vst{i}", [128, 8, 65], BF16) for i in range(2)]
        psT = ps("psT", [128, 8, 128])
        pkv = ps("pkv", [128, 512])
        pcT = ps("pcT", [128, 3, 128], BF16)
        pV = ps("pV", [128, 512])
        pK = [ps(f"pK{i}", [128, 512]) for i in range(2)]

        pg.memset("dve", C.epsb[:], EPS, ["epsb"])
        load_cast(C, wa, W["wa"], 8, 320, stage, "wa")
        load_cast(C, wkK, W["wkvK"], 2, 512, stage, "wkK")
        load_cast(C, wkV, W["wkvV"], 2, 512, stage, "wkV")
        pg.dma("sp", gkv[:], W["gkv"], [], ["gkv"], key="gkv")
        pg.dma("sp", kcs[:], g.kcs, [], ["kcs"], key="kcs")
        pg.memset("pool", ckr[:], 0.0, ["ckr_c", "ckr_r"])
        for i in range(2):
            pg.memset("pool", vst[i][:], 1.0, [f"vst{i}"])
        wa_r = [f"wa{k}" for k in range(8)]
        import os
        CUT = int(os.environ.get("CUT", "9"))
        if CUT <= 1:
            return
        for grp in range(sc.nts // 4):
            for j in range(4):
                t = grp * 4 + j
                b = t % 2
                load_x_T(C, g.xseq[t * 128:(t + 1) * 128, :], xs, xT, psT, b, "xT")
                if CUT <= 2:
                    continue
                for k in range(8):
                    pg.mm(pkv[:, 0:320], xT[b][:, k, :], wa[:, k, :], k == 0, k == 7,
                          [f"xT{b}a" if k < 4 else f"xT{b}b", wa_r[k]], ["pkv"])
                pg.act(junk[:, 0:256], pkv[:, 0:256], AF.Square, ["pkv"], ["junk", "Ass"], accum_out=ss[:, 0:1])
                rms_rstd(C, ss, lnv, rstd, 256, "A")
                pg.stt("dve", ckr[:, 0:256], pkv[:, 0:256], rstd[:, 0:1], gkv[:], ALU.mult, ALU.mult,
                       ["pkv", "Arstd", "gkv"], ["ckr_c"])
                if CUT <= 3:
                    continue
                pg.tt("dve", m1[:], pkv[:, 256:288], kcs[:, t, 0:32], ALU.mult, ["pkv", "kcs"], ["m1"])
                pg.tt("dve", m2[:], pkv[:, 288:320], kcs[:, t, 32:64], ALU.mult, ["pkv", "kcs"], ["m2"])
                pg.tt("pool", ckr[:, 320:352], m1[:], m2[:], ALU.add, ["m1", "m2"], ["ckr_r"])
                CUT2 = int(os.environ.get("CUT2", "9"))
                if CUT2 <= 0:
                    continue
                pg.tr(pcT[:, 0, :], ckr[:, 0:128], C.identB[:], ["ckr_c", "identB"], ["pcT"])
                pg.tr(pcT[:, 1, :], ckr[:, 128:256], C.identB[:], ["ckr_c", "identB"], ["pcT"])
                pg.tr(pcT[0:32, 2, :], ckr[:, 320:352], C.identB[:], ["ckr_r", "identB"], ["pcT"])
                pg.copy("act", cTg[:, :, j * 128:(j + 1) * 128], pcT[:, 0:2, :], ["pcT"], [f"cTg{j}"])
                pg.copy("dve", krT[0:32, j * 128:(j + 1) * 128], pcT[0:32, 2, :], ["pcT"], [f"krT{j}"])
                if CUT <= 4:
                    continue
                vb = t % 2
                for k in range(2):
                    pg.mm(pV[:, :], cTg[:, k, j * 128:(j + 1) * 128], wkV[:, k, :], k == 0, k == 1,
                          [f"cTg{j}", f"wkV{k}"], ["pV"])
                pg.copy("dve", vst[vb][:, :, 0:64], pV[:, :].rearrange("p (h d) -> p h d", h=8), ["pV"],
                        [f"vst{vb}"])
                pg.dma("pool", g.Vs[:, :, t, :].rearrange("h p c -> p h c"), vst[vb][:], [f"vst{vb}"], [],
                       key=f"vst{vb}")
            if CUT <= 5:
                continue
            ctg_r = [f"cTg{j}" for j in range(4)]
            for h in range(8):
                pb = h % 2
                for k in range(2):
                    pg.mm(pK[pb][0:64, :], wkK[:, k, h * 64:(h + 1) * 64], cTg[:, k, :], k == 0, k == 1,
                          ctg_r + [f"wkK{k}"], [f"pK{pb}"])
                pg.copy("act" if h % 2 else "dve", kst[0:64, h, :], pK[pb][0:64, :], [f"pK{pb}"], [f"kstn{h}"])
            if int(os.environ.get("CUT3", "9")) <= 0:
                continue
            pg.dma("pool", g.Ks[:, 0:64, grp * 512:(grp + 1) * 512].rearrange("h d c -> d h c"), kst[:],
                   [f"kstn{h}" for h in range(8)], [], key="kst")
            for h in range(8 if int(os.environ.get("CUT3", "9")) >= 2 else 0):
                pg.dma("pool", g.Ks[h, 64:96, grp * 512:(grp + 1) * 512], krT[:, :], [f"krT{j}" for j in range(4)], [],
                       key=f"krT{h}")


def phase_B(C, g):
    pg, nc, sc, W = C.pg, C.nc, g.c, C.W
    with ExitStack() as es:
        sb, ps = _pools(C, es)
        wb, fresh = shared_sb(C, "wb_sb", [128, 8, 1920], BF16)
        wq, _ = shared_sb(C, "wq_sb", [128, 3, 768], BF16)
        wqs, _ = shared_sb(C, "wqs_sb", [128, 3, 768], BF16)
        gq, _ = shared_sb(C, "gq", [128, 384], F32)
        C.epsb = sb("epsb", [128, 1], F32)
        ctab = sb("ctab", [96, 512], F32)
        stab = sb("stab", [96, 512], F32)
        stage = [sb(f"stg{i}", [128, 2048], F32) for i in range(2)]
        xs = [sb(f"xs{i}", [128, 1024], F32) for i in range(2)]
        xTg = sb("xTg", [128, 8, 512], BF16)
        junk = sb("junk", [128, 384], F32)
        ss = sb("ss", [128, 1], F32)
        lnv = sb("lnv", [128, 1], F32)
        rstd = sb("rstd", [128, 1], F32)
        cq = sb("cq", [128, 384], BF16)
        cqT = sb("cqT", [128, 3, 512], BF16)
        qkst = [sb(f"qkst{i}", [128, 512], BF16) for i in range(2)]
        vast = [sb(f"vast{i}", [128, 512], BF16) for i in range(2)]
        qst = [sb(f"qst{i}", [96, 512], BF16) for i in range(2)]
        r1 = sb("r1", [96, 512], F32)
        r2 = sb("r2", [96, 512], F32)
        psT = ps("psT", [128, 8, 128])
        pA = [ps(f"pA{i}", [128, 512]) for i in range(2)]
        pB = [ps(f"pB{i}", [128, 512]) for i in range(2)]
        pcq = ps("pcq", [128, 3, 128], BF16)

        pg.memset("dve", C.epsb[:], EPS, ["epsb"])
        if fresh:
            load_cast(C, wb, W["wb"], 8, 1920, stage, "wb")
            load_cast(C, wq, W["wq"], 3, 768, stage, "wq")
            load_cast(C, wqs, W["wqs"], 3, 768, stage, "wqs")
            pg.dma("sp", gq[:], W["gq"], [], ["gq"], key="gq")
        wb_r = [f"wb{k}" for k in range(8)]
        ngrp = sc.nt2 // 4
        for grp in range(ngrp):
            c0 = grp * 512
            xr = []
            for j in range(4):
                t = grp * 4 + j
                b = t % 2
                pg.dma("sp", xs[b][:], g.xext[t * 128:(t + 1) * 128, :], [], [f"xs{b}"], key=f"xs{b}")
                for k in range(8):
                    pg.tr(psT[:, k, :], xs[b][:, k * 128:(k + 1) * 128], C.identF[:], [f"xs{b}", "identF"],
                          ["psT"])
                pg.copy("act", xTg[:, 0:4, j * 128:(j + 1) * 128], psT[:, 0:4, :], ["psT"], [f"xTg{j}"])
                pg.copy("dve", xTg[:, 4:8, j * 128:(j + 1) * 128], psT[:, 4:8, :], ["psT"], [f"xTg{j}"])
                xr.append(f"xTg{j}")
            for which, col0, dst, scl in (("q", 0, g.qaT, 0.125), ("k", 512, g.kaT, None)):
                for p in range(4):
                    pb = p % 2
                    for k in range(8):
                        pg.mm(pA[pb][:, :], wb[:, k, col0 + p * 128: col0 + (p + 1) * 128], xTg[:, k, :],
                              k == 0, k == 7, xr + [wb_r[k]], [f"pA{pb}"])
                    pg.copy("act" if pb else "dve", qkst[pb][:, :], pA[pb][:, :], [f"pA{pb}"], [f"qkst{pb}"],
                            scale=scl)
                    pg.dma("pool", dst[p, :, c0:c0 + 512], qkst[pb][:], [f"qkst{pb}"], [], key=f"qkst{pb}")
            for j in range(4):
                t = grp * 4 + j
                pb = j % 2
                for k in range(8):
                    pg.mm(pA[pb][:, :], xTg[:, k, j * 128:(j + 1) * 128], wb[:, k, 1024:1536], k == 0, k == 7,
                          [f"xTg{j}", wb_r[k]], [f"pA{pb}"])
                pg.copy("act", vast[pb][:, :], pA[pb][:, :], [f"pA{pb}"], [f"vast{pb}"])
                pg.dma("pool", g.va[t * 128:(t + 1) * 128, :], vast[pb][:], [f"vast{pb}"], [], key=f"vast{pb}")
                for k in range(8):
                    pg.mm(pB[pb][:, 0:384], xTg[:, k, j * 128:(j + 1) * 128], wb[:, k, 1536:1920], k == 0,
                          k == 7, [f"xTg{j}", wb_r[k]], [f"pB{pb}"])
                pg.act(junk[:, 0:384], pB[pb][:, 0:384], AF.Square, [f"pB{pb}"], ["junk", "Bss"],
                       accum_out=ss[:, 0:1])
                rms_rstd(C, ss, lnv, rstd, 384, "B")
                pg.stt("dve", cq[:, :], pB[pb][:, 0:384], rstd[:, 0:1], gq[:], ALU.mult, ALU.mult,
                       [f"pB{pb}", "Brstd", "gq"], ["cq"])
                for k in range(3):
                    pg.tr(pcq[:, k, :], cq[:, k * 128:(k + 1) * 128], C.identB[:], ["cq", "identB"], ["pcq"])
                pg.copy("act", cqT[:, :, j * 128:(j + 1) * 128], pcq[:, :, :], ["pcq"], [f"cqT{j}"])
            pg.dma("sp", ctab[64:96, :], g.qcsT[0:32, c0:c0 + 512], [], ["ctab"], key="ctab")
            pg.dma("sp", stab[64:96, :], g.qcsT[32:64, c0:c0 + 512], [], ["stab"], key="stab")
            cq_r = [f"cqT{j}" for j in range(4)]
            for h in range(8):
                pb = h % 2
                for k in range(3):
                    pg.mm(pA[pb][0:96, :], wq[:, k, h * 96:(h + 1) * 96], cqT[:, k, :], k == 0, k == 2,
                          cq_r + [f"wq{k}"], [f"pA{pb}"])
                for k in range(3):
                    pg.mm(pB[pb][0:96, :], wqs[:, k, h * 96:(h + 1) * 96], cqT[:, k, :], k == 0, k == 2,
                          cq_r + [f"wqs{k}"], [f"pB{pb}"])
                pg.copy("act", qst[pb][0:64, :], pA[pb][0:64, :], [f"pA{pb}"], [f"qst{pb}n"])
                pg.tt("dve", r1[64:96, :], pA[pb][64:96, :], ctab[64:96, :], ALU.mult, [f"pA{pb}", "ctab"], ["r1"])
                pg.tt("dve", r2[64:96, :], pB[pb][64:96, :], stab[64:96, :], ALU.mult, [f"pB{pb}", "stab"], ["r2"])
                pg.tt("pool", qst[pb][64:96, :], r1[64:96, :], r2[64:96, :], ALU.add, ["r1", "r2"], [f"qst{pb}r"])
                pg.dma("pool", g.qaug[h, :, c0:c0 + 512], qst[pb][:], [f"qst{pb}n", f"qst{pb}r"], [],
                       key=f"qst{pb}")


def phase_C(C, g):
    pg, nc, sc, W = C.pg, C.nc, g.c, C.W
    with ExitStack() as es:
        sb, ps = _pools(C, es)
        nab, fresh = shared_sb(C, "nab", [128, 8, 1024], BF16)
        qsel, _ = shared_sb(C, "qsel", [2, 128], BF16)
        stage = [sb(f"stg{i}", [128, 2048], F32) for i in range(2)]
        kaT = sb("kaT", [128, 4, sc.T2], BF16)
        va = sb("va", [128, sc.nt2, 512], BF16)
        qa = [sb(f"qa{i}", [128, 4, 128], BF16) for i in range(2)]
        pen = [sb(f"pen{i}", [2, 1024], BF16) for i in range(2)]
        P = [sb(f"P{i}", [128, 1024], BF16) for i in range(2)]
        PT = [sb(f"PT{i}", [128, 8, 128], BF16) for i in range(2)]
        mx = [sb(f"mx{i}", [128, 2], F32) for i in range(2)]
        rs = sb("rs", [128, 8], F32)
        rinv = sb("rinv", [128, 8], F32)
        ao = [sb(f"ao{i}", [128, 512], BF16) for i in range(2)]
        aoT = [sb(f"aoT{i}", [128, 4, 128], BF16) for i in range(2)]
        pS = [ps(f"pS{i}", [128, 1024]) for i in range(2)]
        pPT = [ps(f"pPT{i}", [128, 8, 128], BF16) for i in range(2)]
        pO = ps("pO", [128, 512])
        paT = ps("paT", [128, 4, 128], BF16)

        for h in range(8 if fresh else 0):
            b = h % 2
            pg.dma("sp", stage[b][:, 0:1024], W["nab"][:, h, :], [], [f"stg{b}"], key=f"stg{b}")
            pg.copy("pool" if b else "dve", nab[:, h, :], stage[b][:, 0:1024], [f"stg{b}"], [f"nab{h}"])
        if fresh:
            pg.dma("sp", qsel[:], W["qsel"], [], ["qsel"], key="qsel")
        for p in range(4):
            pg.dma("sp", kaT[:, p, :], g.kaT[p, :, :], [], [f"kaT{p}"], key=f"kaT{p}")
        pg.dma("sp", va[:], g.va.rearrange("(t p) c -> p t c", p=128), [], ["va"], key="va")

        nq = sc.nqt
        for j in range(nq):
            qb = j % 2
            q0 = KH + 128 * j - 64
            pg.dma("sp", qa[qb][:], g.qaT[:, :, q0:q0 + 128].rearrange("k p t -> p k t"), [], [f"qa{qb}"],
                   key=f"qa{qb}")
            pg.dma("sp", pen[qb][:], g.pen[j, :, :], [], [f"pen{qb}"], key=f"pen{qb}")
            k0 = 128 * j

            def s1(h):
                sbuf = h % 2
                half = h % 2
                pr = h // 2
                for n2 in range(2):
                    o = pS[sbuf][:, n2 * 512:(n2 + 1) * 512]
                    pg.mm(o, qa[qb][half * 64:(half + 1) * 64, pr, :],
                          kaT[half * 64:(half + 1) * 64, pr, k0 + n2 * 512:k0 + (n2 + 1) * 512], True, False,
                          [f"qa{qb}", f"kaT{pr}"], [f"pS{sbuf}_{n2}"])
                    pg.mm(o, C.identB[:], nab[:, h, n2 * 512:(n2 + 1) * 512], False, False,
                          ["identB", f"nab{h}"], [f"pS{sbuf}_{n2}"])
                    pg.mm(o, qsel[0:2, :], pen[qb][0:2, n2 * 512:(n2 + 1) * 512], False, True,
                          ["qsel", f"pen{qb}"], [f"pS{sbuf}_{n2}"])

            def s2(h):
                sbuf = h % 2
                rd = [f"pS{sbuf}_0", f"pS{sbuf}_1"]
                pg.add("dve", lambda e, o=mx[sbuf][:, 0:1], i_=pS[sbuf][:, :]: e.reduce_max(
                    out=o, in_=i_, axis=AX.X), rd, [f"mxa{sbuf}"])
                pg.ts("dve", mx[sbuf][:, 1:2], mx[sbuf][:, 0:1], -1.0, None, ALU.mult, None, [f"mxa{sbuf}"],
                      [f"mx{sbuf}"])
                pg.act(P[sbuf][:, :], pS[sbuf][:, :], AF.Exp, rd + [f"mx{sbuf}"], [f"P{sbuf}", f"rs{h}"],
                       bias=mx[sbuf][:, 1:2], accum_out=rs[:, h:h + 1])

            def s3(h):
                sbuf = h % 2
                for kt in range(8):
                    pg.tr(pPT[sbuf][:, kt, :], P[sbuf][:, kt * 128:(kt + 1) * 128], C.identB[:],
                          [f"P{sbuf}", "identB"], [f"pPT{sbuf}"])
                pg.copy("dve" if h % 2 else "act", PT[sbuf][:, :, :], pPT[sbuf][:, :, :], [f"pPT{sbuf}"],
                        [f"PT{sbuf}"])

            def s4(h):
                sbuf = h % 2
                for kt in range(8):
                    pg.mm(pO[:, h * 64:(h + 1) * 64], PT[sbuf][:, kt, :], va[:, j + kt, h * 64:(h + 1) * 64],
                          kt == 0, kt == 7, [f"PT{sbuf}", "va"], ["pO"])

            for step in range(8 + 3):
                if step < 8:
                    s1(step)
                if 0 <= step - 1 < 8:
                    s2(step - 1)
                if 0 <= step - 2 < 8:
                    s3(step - 2)
                if 0 <= step - 3 < 8:
                    s4(step - 3)
            pg.add("dve", lambda e: e.reciprocal(out=rinv[:, :], in_=rs[:, :]), [f"rs{h}" for h in range(8)],
                   ["rinv"])
            ab = j % 2
            pg.tt("dve", ao[ab][:, :].rearrange("p (h d) -> p h d", h=8),
                  pO[:, :].rearrange("p (h d) -> p h d", h=8),
                  rinv[:, :].unsqueeze(2).broadcast_to([128, 8, 64]), ALU.mult, ["pO", "rinv"], [f"ao{ab}"])
            for k in range(4):
                pg.tr(paT[:, k, :], ao[ab][:, k * 128:(k + 1) * 128], C.identB[:], [f"ao{ab}", "identB"], ["paT"])
            pg.copy("act", aoT[ab][:, :, :], paT[:, :, :], ["paT"], [f"aoT{ab}"])
            e0 = 128 * j - 64
            lo, hi = max(e0, 0), min(e0 + 128, sc.T)
            pg.dma("pool", g.catT[0:4, :, lo:hi].rearrange("k p t -> p k t"), aoT[ab][:, :, lo - e0:hi - e0],
                   [f"aoT{ab}"], [], key=f"aoT{ab}")


def phase_D(C, g):
    pg, nc, sc, W = C.pg, C.nc, g.c, C.W
    with ExitStack() as es:
        sb, ps = _pools(C, es)
        ksb = [sb(f"ksb{i}", [96, sc.S], BF16) for i in range(2)]
        vsb = [sb(f"vsb{i}", [128, sc.nts, 65], BF16) for i in range(2)]
        qsb = [sb(f"qsb{i}", [96, sc.T], BF16) for i in range(2)]
        PTs = [sb(f"PTs{i}", [128, 512], BF16) for i in range(4)]
        rrow = sb("rrow", [65, 512], F32)
        rbc = sb("rbc", [64, 512], F32)
        on = [sb(f"on{i}", [64, 512], BF16) for i in range(2)]
        pST = [ps(f"pST{i}", [128, 512]) for i in range(3)]
        pOT = [ps(f"pOT{i}", [128, 512]) for i in range(2)]
        pR = ps("pR", [64, 512])
        nqb = sc.T // 512
        it = 0
        for h in range(8):
            hb = h % 2
            pg.dma("sp", ksb[hb][:, :], g.Ks[h, 0:96, :], [], [f"ksb{hb}"], key=f"ksb{hb}")
            pg.dma("sp", vsb[hb][:, :, :], g.Vs[h, :, :, :], [], [f"vsb{hb}"], key=f"vsb{hb}")
            pg.dma("sp", qsb[hb][0:96, :], g.qaug[h, :, KH:KH + sc.T], [], [f"qsb{hb}"], key=f"qsb{hb}")
            for qb in range(nqb):
                ob = (h * nqb + qb) % 2
                cols = slice(qb * 512, (qb + 1) * 512)

                def st(kt, it_):
                    sbuf = it_ % 3
                    pg.mm(pST[sbuf][:, :], ksb[hb][0:96, kt * 128:(kt + 1) * 128], qsb[hb][0:96, cols], True, True,
                          [f"ksb{hb}", f"qsb{hb}"], [f"pST{sbuf}"])

                def ex_pv(kt, it_):
                    sbuf = it_ % 3
                    pbuf = it_ % 4
                    pg.act(PTs[pbuf][:, :], pST[sbuf][:, :], AF.Exp, [f"pST{sbuf}"], [f"PTs{pbuf}"], scale=MLA_SCALE)
                    pg.mm(pOT[ob][0:65, :], vsb[hb][:, kt, 0:65], PTs[pbuf][:, :], kt == 0, kt == sc.nts - 1,
                          [f"vsb{hb}", f"PTs{pbuf}"], [f"pOT{ob}"])

                base = it
                for kt in range(sc.nts + 2):
                    if kt < sc.nts:
                        st(kt, base + kt)
                    if kt - 2 >= 0:
                        ex_pv(kt - 2, base + kt - 2)
                it = base + sc.nts
                pg.add("dve", lambda e, o=rrow[64:65, :], i_=pOT[ob][64:65, :]: e.reciprocal(out=o, in_=i_),
                       [f"pOT{ob}"], ["rrow"])
                pg.mm(pR[0:64, :], C.ones[64:65, 0:64], rrow[64:65, :], True, True, ["ones", "rrow"], ["pR"])
                pg.copy("act", rbc[:, :], pR[0:64, :], ["pR"], ["rbc"])
                pg.tt("dve", on[ob][:, :], pOT[ob][0:64, :], rbc[:, :], ALU.mult, [f"pOT{ob}", "rbc"], [f"on{ob}"])
                pg.dma("pool", g.catT[4 + h // 2, (h % 2) * 64:(h % 2) * 64 + 64, cols], on[ob][:, :], [f"on{ob}"],
                       [], key=f"on{ob}")


def layer_norm_tile(C, r, xn, tmp, gt, bt, st6, mv, lnv, rstd, nmr, vcol, tag, rtag, xtag, ttag=None,
                    defer=False):
    pg = C.pg
    if ttag is None:
        ttag = tag + "tmp"
    for hlf in range(2):
        pg.add("dve", lambda e, o=st6[:, hlf, :], i_=r[:, hlf * 512:(hlf + 1) * 512]: e.bn_stats(out=o, in_=i_),
               [rtag], [tag + f"st{hlf}"])
    pg.add("dve", lambda e: e.bn_aggr(out=mv[:, :], in_=st6[:, :, :].rearrange("p a b -> p (a b)")),
           [tag + "st0", tag + "st1"], [tag + "mv"])
    pg.act(lnv[:, 0:1], mv[:, 1:2], AF.Ln, [tag + "mv"], [tag + "lnv"], bias=C.epsb[:, 0:1])
    pg.act(rstd[:, 0:1], lnv[:, 0:1], AF.Exp, [tag + "lnv"], [tag + "rstd0"], scale=-0.5)
    if vcol is not None:
        pg.tt("dve", rstd[:, 1:2], rstd[:, 0:1], vcol, ALU.mult, [tag + "rstd0", "valid"], [tag + "rstd"])
        rs_ap = rstd[:, 1:2]
    else:
        pg.copy("dve", rstd[:, 1:2], rstd[:, 0:1], [tag + "rstd0"], [tag + "rstd"])
        rs_ap = rstd[:, 1:2]
    pg.stt("dve", nmr[:, 0:1], mv[:, 0:1], -1.0, rs_ap, ALU.mult, ALU.mult, [tag + "mv", tag + "rstd"],
           [tag + "nmr"])
    pg.act(tmp[:, :], r[:, :], AF.Identity, [rtag, tag + "rstd", tag + "nmr"], [ttag + "a", ttag + "b"], scale=rs_ap,
           bias=nmr[:, 0:1])
    if not defer:
        layer_norm_back(C, xn, tmp, gt, bt, vcol, tag, xtag, ttag)


def layer_norm_back(C, xn, tmp, gt, bt, vcol, tag, xtag, ttag):
    pg = C.pg
    if vcol is not None:
        pg.tt("pool", tmp[:, 0:384], tmp[:, 0:384], gt[:, 0:384], ALU.mult, [ttag + "a", tag + "g"], [ttag + "a"])
        pg.tt("dve", tmp[:, 384:1024], tmp[:, 384:1024], gt[:, 384:1024], ALU.mult, [ttag + "b", tag + "g"],
              [ttag + "b"])
        pg.stt("dve", xn[:, :], bt[:, :], vcol, tmp[:, :], ALU.mult, ALU.add,
               [ttag + "a", ttag + "b", tag + "b", "valid"], [xtag])
    else:
        pg.tt("pool", tmp[:, :], tmp[:, :], gt[:, :], ALU.mult, [ttag + "a", ttag + "b", tag + "g"],
              [ttag + "a", ttag + "b"])
        pg.tt("pool", xn[:, :], tmp[:, :], bt[:, :], ALU.add, [ttag + "a", ttag + "b", tag + "b"], [xtag])


def phase_E(C, g):
    pg, nc, sc, W = C.pg, C.nc, g.c, C.W
    with ExitStack() as es:
        sb, ps = _pools(C, es)
        wo, fresh = shared_sb(C, "wo_sb", [128, 8, 1024], BF16)
        gt, _ = shared_sb(C, "gt", [128, 1024], F32)
        bt, _ = shared_sb(C, "bt", [128, 1024], F32)
        stage = [sb(f"stg{i}", [128, 2048], F32) for i in range(2)]
        valid = sb("valid", [128, sc.nt], F32)
        C.epsb = sb("epsb", [128, 1], F32)
        cat = [sb(f"cat{i}", [128, 8, 512], BF16) for i in range(2)]
        xs = [sb(f"xs{i}", [128, 1024], F32) for i in range(2)]
        r = [sb(f"r{i}", [128, 1024], F32) for i in range(2)]
        tmp = [sb(f"tmp{i}", [128, 1024], F32) for i in range(2)]
        xn = [sb(f"xn{i}", [128, 1024], F32) for i in range(2)]
        xnT = [sb(f"xnT{i}", [128, 8, 128], BF16) for i in range(2)]
        st6 = sb("st6", [128, 2, 6], F32)
        mv = sb("mv", [128, 2], F32)
        lnv = sb("lnv", [128, 1], F32)
        rstd = sb("rstd", [128, 2], F32)
        nmr = sb("nmr", [128, 1], F32)
        pmix = [ps(f"pmix{i}", [128, 1024]) for i in range(2)]
        psT = ps("psT", [128, 8, 128])
        pg.memset("dve", C.epsb[:], EPS, ["epsb"])
        if fresh:
            load_cast(C, wo, W["wo0"], 8, 1024, stage, "wo")
            pg.dma("sp", gt[:], W["l0_ln1g"], [], ["Eg"], key="Eg")
            pg.dma("sp", bt[:], W["l0_ln1b"], [], ["Eb"], key="Eb")
        pg.dma("sp", valid[:], g.valid, [], ["valid"], key="valid")
        def front(t):
            grp, j = t // 4, t % 4
            cb = grp % 2
            b = t % 2
            if j == 0:
                pg.dma("sp", cat[cb][:], g.catT[:, :, grp * 512:(grp + 1) * 512].rearrange("k p t -> p k t"), [],
                       [f"cat{cb}"], key=f"cat{cb}")
            pg.dma("sp", xs[b][:], g.xext[KH + t * 128:KH + (t + 1) * 128, :], [], [f"xs{b}"], key=f"xs{b}")
            for hlf in range(2):
                for k in range(8):
                    pg.mm(pmix[b][:, hlf * 512:(hlf + 1) * 512], cat[cb][:, k, j * 128:(j + 1) * 128],
                          wo[:, k, hlf * 512:(hlf + 1) * 512], k == 0, k == 7, [f"cat{cb}", f"wo{k}"],
                          [f"pmix{b}_{hlf}"])
                pg.stt("dve", r[b][:, hlf * 512:(hlf + 1) * 512], xs[b][:, hlf * 512:(hlf + 1) * 512], ALPHA,
                       pmix[b][:, hlf * 512:(hlf + 1) * 512], ALU.mult, ALU.add, [f"xs{b}", f"pmix{b}_{hlf}"],
                       [f"r{b}"])
            layer_norm_tile(C, r[b], xn[b], tmp[b], gt, bt, st6, mv, lnv, rstd, nmr, valid[:, t:t + 1], "E",
                            f"r{b}", f"xn{b}", ttag=f"tmp{b}", defer=True)

        def back(t):
            b = t % 2
            layer_norm_back(C, xn[b], tmp[b], gt, bt, valid[:, t:t + 1], "E", f"xn{b}", f"tmp{b}")
            pg.dma("pool", g.xmid[t * 128:(t + 1) * 128, :], xn[b][:], [f"xn{b}"], [], key=f"xn{b}")
            for k in range(8):
                pg.tr(psT[:, k, :], xn[b][:, k * 128:(k + 1) * 128], C.identF[:], [f"xn{b}", "identF"], ["psT"])
            pg.copy("act", xnT[b][:, 0:4, :], psT[:, 0:4, :], ["psT"], [f"xnT{b}"])
            pg.copy("dve", xnT[b][:, 4:8, :], psT[:, 4:8, :], ["psT"], [f"xnT{b}"])
            pg.dma("pool", g.xmidT[:, :, 1 + t * 128:1 + (t + 1) * 128].rearrange("k p t -> p k t"), xnT[b][:],
                   [f"xnT{b}"], [], key=f"xnT{b}")

        for t in range(sc.nt + 1):
            if t < sc.nt:
                front(t)
            if t >= 1:
                back(t - 1)


def phase_FFN(C, g, l):
    pg, nc, sc, W = C.pg, C.nc, g.c, C.W
    p = f"l{l}_"
    if l == 0:
        xT_src, x_src, dst, t_lo, ntok = g.xmidT, g.xmid, g.x1, 0, sc.T
    else:
        xT_src, x_src, dst, t_lo, ntok = g.xn1T, g.xn1, g.out, HALO, sc.NQ
    with ExitStack() as es:
        sb, ps = _pools(C, es)
        wup, fresh = shared_sb(C, "wup", [128, 8, 2 * DFF], BF16)
        wdn, _ = shared_sb(C, "wdn", [128, 22, 1024], BF16)
        cw, _ = shared_sb(C, "cw", [128, 44, 3], F32)
        cbt, _ = shared_sb(C, "cbt", [128, 44], F32)
        gt, _ = shared_sb(C, "gt", [128, 1024], F32)
        bt, _ = shared_sb(C, "bt", [128, 1024], F32)
        stage = [sb(f"stg{i}", [128, 512], F32) for i in range(2)]
        if fresh:
            load_cast(C, wup, W[p + "wup"], 8, 2 * DFF, stage, "wup", cwmax=512)
            load_cast(C, wdn, W[p + "wdn"], 22, 1024, stage, "wdn", cwmax=512)
        C.epsb = sb("epsb", [128, 1], F32)
        xTb = [sb(f"xTb{i}", [128, 8, NB + 2], BF16) for i in range(2)]
        uT = [sb(f"uT{i}", [128, 22, NB], BF16) for i in range(2)]
        cg = [sb(f"cg{i}", [128, NB], F32) for i in range(2)]
        cv = [sb(f"cv{i}", [128, NB], F32) for i in range(2)]
        gg = [sb(f"gg{i}", [128, NB], F32) for i in range(2)]
        xs = [sb(f"xs{i}", [128, 1024], F32) for i in range(2)]
        r = sb("r", [128, 1024], F32)
        tmp = sb("tmp", [128, 1024], F32)
        xo = [sb(f"xo{i}", [128, 1024], F32) for i in range(2)]
        st6 = sb("st6", [128, 2, 6], F32)
        mv = sb("mv", [128, 2], F32)
        lnv = sb("lnv", [128, 1], F32)
        rstd = sb("rstd", [128, 2], F32)
        nmr = sb("nmr", [128, 1], F32)
        ph = [ps(f"ph{i}", [128, 512]) for i in range(4)]
        py = [ps(f"py{i}", [128, 1024]) for i in range(2)]
        pg.memset("dve", C.epsb[:], EPS, ["epsb"])
        if fresh:
            pg.dma("sp", cw[:], W[p + "cw"], [], ["cw"], key="cw")
            pg.dma("sp", cbt[:], W[p + "cb"], [], ["cb"], key="cb")
            pg.dma("sp", gt[:], W[p + "ln2g"], [], ["Fg"], key="Fg")
            pg.dma("sp", bt[:], W[p + "ln2b"], [], ["Fb"], key="Fb")
        wup_r = [f"wup{k}" for k in range(8)]
        NW = NB + 2
        nblk = ntok // NB
        NT = NB // 128

        def down_steps(blk):
            tb = t_lo + blk * NB
            ub = blk % 2
            seq = []
            for j in range(NT):
                tok0 = tb + j * 128
                b = (blk * NT + j) % 2
                pj = py[j % 2]
                for hlf in range(2):
                    for c in range(22):
                        def mmf(j=j, hlf=hlf, c=c, pj=pj, b=b, tok0=tok0):
                            if hlf == 0 and c == 0:
                                pg.dma("sp", xs[b][:], x_src[tok0:tok0 + 128, :], [], [f"xs{b}"], key=f"xs{b}")
                            pg.mm(pj[:, hlf * 512:(hlf + 1) * 512], uT[ub][:, c, j * 128:(j + 1) * 128],
                                  wdn[:, c, hlf * 512:(hlf + 1) * 512], c == 0, c == 21, [f"uT{ub}_{c}", f"wdn{c}"],
                                  [f"py{j % 2}_{hlf}"])
                            if c == 21:
                                pg.stt("dve", r[:, hlf * 512:(hlf + 1) * 512], xs[b][:, hlf * 512:(hlf + 1) * 512],
                                       ALPHA, pj[:, hlf * 512:(hlf + 1) * 512], ALU.mult, ALU.add,
                                       [f"xs{b}", f"py{j % 2}_{hlf}"], ["r"])
                                if hlf == 1:
                                    layer_norm_tile(C, r, xo[b], tmp, gt, bt, st6, mv, lnv, rstd, nmr, None, "F",
                                                    "r", f"xo{b}")
                                    o0 = tok0 - t_lo
                                    pg.dma("pool", dst[o0:o0 + 128, :], xo[b][:], [f"xo{b}"], [], key=f"xo{b}")
                        seq.append(mmf)
            per = (len(seq) + 21) // 22
            return [seq[i * per:(i + 1) * per] for i in range(22)]

        for blk in range(nblk + 1):
            prev = down_steps(blk - 1) if blk >= 1 else None
            if blk < nblk:
                tb = t_lo + blk * NB
                xb = blk % 2
                ub = blk % 2
                pg.dma("sp", xTb[xb][:], xT_src[:, :, tb:tb + NW].rearrange("k p t -> p k t"), [], [f"xTb{xb}"],
                       key=f"xTb{xb}")
                if l == 0 and blk == 0:
                    pg.memset("pool", xTb[xb][:, :, 0:1], 0.0, [f"xTb{xb}"])
                if l == 0 and blk == nblk - 1:
                    pg.memset("pool", xTb[xb][:, :, NW - 1:NW], 0.0, [f"xTb{xb}"])
            for c in range(22):
                i2 = c % 2
                hg, hv = ph[2 * i2], ph[2 * i2 + 1]
                if blk < nblk:
                    for (hp, col0, tg) in ((hg, c * 128, "g"), (hv, DFF + c * 128, "v")):
                        for k in range(8):
                            pg.mm(hp[:, 0:NW], wup[:, k, col0:col0 + 128], xTb[xb][:, k, :], k == 0, k == 7,
                                  [f"xTb{xb}", wup_r[k]], [f"ph{i2}{tg}"])
                if prev is not None:
                    for f in prev[c]:
                        f()
                if blk < nblk:
                    for (hp, ch, dstt, tg) in ((hg, c, cg[i2], "g"), (hv, 22 + c, cv[i2], "v")):
                        pg.act(dstt[:, :], hp[:, 1:NB + 1], AF.Identity, [f"ph{i2}{tg}", "cw", "cb"], [f"c{tg}{i2}"],
                               scale=cw[:, ch, 1:2], bias=cbt[:, ch:ch + 1])
                        pg.stt("dve", dstt[:, :], hp[:, 0:NB], cw[:, ch, 0:1], dstt[:, :], ALU.mult, ALU.add,
                               [f"ph{i2}{tg}", "cw", f"c{tg}{i2}"], [f"c{tg}{i2}"])
                        pg.stt("dve", dstt[:, :], hp[:, 2:NB + 2], cw[:, ch, 2:3], dstt[:, :], ALU.mult, ALU.add,
                               [f"ph{i2}{tg}", "cw", f"c{tg}{i2}"], [f"c{tg}{i2}"])
                    pg.act(gg[i2][:, :], cg[i2][:, :], AF.Gelu, [f"cg{i2}"], [f"gg{i2}"])
                    pg.tt("pool", uT[ub][:, c, :], gg[i2][:, :], cv[i2][:, :], ALU.mult, [f"gg{i2}", f"cv{i2}"],
                          [f"uT{ub}_{c}"])


def phase_G(C, g):
    pg, nc, sc, W = C.pg, C.nc, g.c, C.W
    with ExitStack() as es:
        sb, ps = _pools(C, es)
        w1, fresh = shared_sb(C, "w1_sb", [128, 8, 1856], BF16)
        stage = [sb(f"stg{i}", [128, 2048], F32) for i in range(2)]
        l1cs = sb("l1cs", [128, sc.nt, 32], F32)
        xs = [sb(f"xs{i}", [128, 1024], F32) for i in range(2)]
        xT = [sb(f"xT{i}", [128, 8, 128], BF16) for i in range(2)]
        m1 = sb("m1", [128, 20, 16], F32)
        m2 = sb("m2", [128, 20, 16], F32)
        m3 = sb("m3", [128, 20, 16], F32)
        qk = [sb(f"qk{i}", [128, 20, 64], BF16) for i in range(2)]
        vst = [sb(f"vst{i}", [128, 256], BF16) for i in range(2)]
        qkT = [sb(f"qkT{i}", [128, 10, 128], BF16) for i in range(2)]
        psT = ps("psT", [128, 8, 128])
        pq = ps("pq", [128, 2048])
        pqT = ps("pqT", [128, 10, 128], BF16)
        if fresh:
            load_cast(C, w1, W["w1in"], 8, 1856, stage, "w1")
        pg.dma("sp", l1cs[:], g.l1cs, [], ["l1cs"], key="l1cs")
        w1_r = [f"w1{k}" for k in range(8)]
        for t in range(sc.nt):
            b = t % 2
            load_x_T(C, g.x1[t * 128:(t + 1) * 128, :], xs, xT, psT, b, "xT")
            for (c0, cw_, bank) in ((0, 512, 0), (512, 512, 1), (1024, 512, 2), (1536, 320, 3)):
                for k in range(8):
                    pg.mm(pq[:, bank * 512:bank * 512 + cw_], xT[b][:, k, :], w1[:, k, c0:c0 + cw_], k == 0, k == 7,
                          [f"xT{b}a" if k < 4 else f"xT{b}b", w1_r[k]], [f"pq{bank}"])
            qkv = pq[:, 0:1280].rearrange("p (h d) -> p h d", h=20)
            sw = pq[:, 1536:1856].rearrange("p (h d) -> p h d", h=20)
            cosb = l1cs[:, t, 0:16].unsqueeze(1).broadcast_to([128, 20, 16])
            sinb = l1cs[:, t, 16:32].unsqueeze(1).broadcast_to([128, 20, 16])
            pg.tt("dve", m1[:, :, :], qkv[:, :, 0:16], cosb, ALU.mult, ["pq0", "pq1", "pq2", "l1cs"], ["Gm1"])
            pg.tt("dve", m2[:, :, :], sw, sinb, ALU.mult, ["pq3", "l1cs"], ["Gm2"])
            pg.tt("pool", m3[:, :, :], m1[:, :, :], m2[:, :, :], ALU.add, ["Gm1", "Gm2"], ["Gm3"])
            pg.copy("act", qk[b][:, 0:16, 0:16], m3[:, 0:16, :], ["Gm3"], [f"qk{b}a"], scale=0.125)
            pg.copy("pool", qk[b][:, 16:20, 0:16], m3[:, 16:20, :], ["Gm3"], [f"qk{b}b"])
            pg.copy("act", qk[b][:, 0:16, 16:64], qkv[:, 0:16, 16:64], ["pq0", "pq1"], [f"qk{b}c"], scale=0.125)
            pg.copy("dve", qk[b][:, 16:20, 16:64], qkv[:, 16:20, 16:64], ["pq2"], [f"qk{b}d"])
            pg.copy("dve", vst[b][:, :], pq[:, 1280:1536], ["pq2"], [f"vst{b}"])
            pg.dma("pool", g.v1[t * 128:(t + 1) * 128, :], vst[b][:], [f"vst{b}"], [], key=f"vst{b}")
            qrd = [f"qk{b}a", f"qk{b}b", f"qk{b}c", f"qk{b}d", "identB"]
            for pr in range(10):
                pg.tr(pqT[:, pr, :], qk[b][:, 2 * pr:2 * pr + 2, :].rearrange("p h d -> p (h d)"), C.identB[:], qrd,
                      ["pqT"])
            pg.copy("act", qkT[b][:, 0:5, :], pqT[:, 0:5, :], ["pqT"], [f"qkT{b}"])
            pg.copy("dve", qkT[b][:, 5:10, :], pqT[:, 5:10, :], ["pqT"], [f"qkT{b}"])
            pg.dma("pool", g.q1T[:, :, t * 128:(t + 1) * 128].rearrange("k p t -> p k t"), qkT[b][:, 0:8, :],
                   [f"qkT{b}"], [], key=f"qkT{b}q")
            pg.dma("pool", g.k1T[:, :, t * 128:(t + 1) * 128].rearrange("k p t -> p k t"), qkT[b][:, 8:10, :],
                   [f"qkT{b}"], [], key=f"qkT{b}k")


def phase_H(C, g):
    pg, nc, sc, W = C.pg, C.nc, g.c, C.W
    with ExitStack() as es:
        sb, ps = _pools(C, es)
        wo, fresh = shared_sb(C, "wo_sb", [128, 8, 1024], BF16)
        gt, _ = shared_sb(C, "gt", [128, 1024], F32)
        bt, _ = shared_sb(C, "bt", [128, 1024], F32)
        sinks, _ = shared_sb(C, "sinks", [128, 16], F32)
        band, _ = shared_sb(C, "band", [128, 384], BF16)
        stage = [sb(f"stg{i}", [128, 2048], F32) for i in range(2)]
        valid = sb("valid", [128, sc.nt], F32)
        kpen = sb("kpen", [1, sc.T], BF16)
        C.epsb = sb("epsb", [128, 1], F32)
        k1T = sb("k1T", [128, 2, sc.T], BF16)
        v1 = sb("v1", [128, sc.nt, 256], BF16)
        q1 = [sb(f"q1{i}", [128, 8, 128], BF16) for i in range(2)]
        P = [sb(f"P{i}", [128, 384], BF16) for i in range(2)]
        PT = [sb(f"PT{i}", [128, 3, 128], BF16) for i in range(2)]
        mx = [sb(f"mx{i}", [128, 2], F32) for i in range(2)]
        rs = sb("rs", [128, 16], F32)
        es_ = sb("es_", [128, 16], F32)
        rinv = sb("rinv", [128, 16], F32)
        at = sb("at", [128, 1024], BF16)
        atT = sb("atT", [128, 8, 128], BF16)
        xs = [sb(f"xs{i}", [128, 1024], F32) for i in range(2)]
        r = sb("r", [128, 1024], F32)
        tmp = [sb(f"tmp{i}", [128, 1024], F32) for i in range(2)]
        xn = [sb(f"xn{i}", [128, 1024], F32) for i in range(2)]
        xnT = [sb(f"xnT{i}", [128, 8, 128], BF16) for i in range(2)]
        st6 = sb("st6", [128, 2, 6], F32)
        mv = sb("mv", [128, 2], F32)
        lnv = sb("lnv", [128, 1], F32)
        rstd = sb("rstd", [128, 2], F32)
        nmr = sb("nmr", [128, 1], F32)
        pS = [ps(f"pS{i}", [128, 512]) for i in range(2)]
        pPT = [ps(f"pPT{i}", [128, 8, 128], BF16) for i in range(2)]
        pO = ps("pO", [128, 1024])
        pmix = ps("pmix", [128, 1024])
        pg.memset("dve", C.epsb[:], EPS, ["epsb"])
        if fresh:
            load_cast(C, wo, W["wo1"], 8, 1024, stage, "wo")
            pg.dma("sp", gt[:], W["l1_ln1g"], [], ["Hg"], key="Hg")
            pg.dma("sp", bt[:], W["l1_ln1b"], [], ["Hb"], key="Hb")
            pg.dma("sp", sinks[:], W["sinks"], [], ["sinks"], key="sinks")
            pg.dma("sp", band[:], W["band"], [], ["band"], key="band")
        pg.dma("sp", valid[:], g.valid, [], ["valid"], key="valid")
        pg.dma("sp", kpen[:], g.kpen, [], ["kpen"], key="kpen")
        pg.dma("sp", k1T[:], g.k1T.rearrange("k p t -> p k t"), [], ["k1T"], key="k1T")
        pg.dma("sp", v1[:], g.v1.rearrange("(t p) c -> p t c", p=128), [], ["v1"], key="v1")
        def front(i):
            b = i % 2
            pg.dma("sp", q1[b][:], g.q1T[:, :, i * 128:(i + 1) * 128].rearrange("k p t -> p k t"), [], [f"q1{b}"],
                   key=f"q1{b}")
            pg.dma("sp", xs[b][:], g.x1[i * 128:(i + 1) * 128, :], [], [f"xs{b}"], key=f"xs{b}")
            k0 = (i - 1) * 128

            def s1(hd):
                sbuf = hd % 2
                half = hd % 2
                gk = QPERM[hd] // 4
                o = pS[sbuf][:, 0:384]
                pg.mm(o, q1[b][half * 64:(half + 1) * 64, hd // 2, :],
                      k1T[half * 64:(half + 1) * 64, gk // 2, k0:k0 + 384], True, False, [f"q1{b}", "k1T"],
                      [f"pS{sbuf}"])
                pg.mm(o, C.identB[:], band[:, :], False, False, ["identB", "band"], [f"pS{sbuf}"])
                pg.mm(o, C.onesB[0:1, 0:128], kpen[0:1, k0:k0 + 384], False, True, ["onesB", "kpen"], [f"pS{sbuf}"])

            def s2(hd):
                sbuf = hd % 2
                pg.add("dve", lambda e, o=mx[sbuf][:, 0:1], i_=pS[sbuf][:, 0:384]: e.reduce_max(
                    out=o, in_=i_, axis=AX.X), [f"pS{sbuf}"], [f"mxa{sbuf}"])
                pg.ts("dve", mx[sbuf][:, 1:2], mx[sbuf][:, 0:1], sinks[:, hd:hd + 1], -1.0, ALU.max, ALU.mult,
                      [f"mxa{sbuf}", "sinks"], [f"mx{sbuf}"])
                pg.act(P[sbuf][:, :], pS[sbuf][:, 0:384], AF.Exp, [f"pS{sbuf}", f"mx{sbuf}"], [f"P{sbuf}", f"rs{hd}"],
                       bias=mx[sbuf][:, 1:2], accum_out=rs[:, hd:hd + 1])
                pg.act(es_[:, hd:hd + 1], sinks[:, hd:hd + 1], AF.Exp, ["sinks", f"mx{sbuf}"], [f"es{hd}"],
                       bias=mx[sbuf][:, 1:2])

            def s3(hd):
                sbuf = hd % 2
                for kt in range(3):
                    pg.tr(pPT[sbuf][:, kt, :], P[sbuf][:, kt * 128:(kt + 1) * 128], C.identB[:],
                          [f"P{sbuf}", "identB"], [f"pPT{sbuf}"])
                pg.copy("dve" if hd % 2 else "act", PT[sbuf][:, :, :], pPT[sbuf][:, 0:3, :], [f"pPT{sbuf}"],
                        [f"PT{sbuf}"])

            def s4(hd):
                sbuf = hd % 2
                gk = QPERM[hd] // 4
                for kt in range(3):
                    pg.mm(pO[:, hd * 64:(hd + 1) * 64], PT[sbuf][:, kt, :], v1[:, i - 1 + kt, gk * 64:(gk + 1) * 64],
                          kt == 0, kt == 2, [f"PT{sbuf}", "v1"], [f"pO{hd // 8}"])

            for step in range(16 + 3):
                if step < 16:
                    s1(step)
                if 0 <= step - 1 < 16:
                    s2(step - 1)
                if 0 <= step - 2 < 16:
                    s3(step - 2)
                if 0 <= step - 3 < 16:
                    s4(step - 3)
            pg.tt("dve", rinv[:, :], rs[:, :], es_[:, :], ALU.add, [f"rs{h}" for h in range(16)] +
                  [f"es{h}" for h in range(16)], ["rinv0"])
            pg.add("dve", lambda e: e.reciprocal(out=rinv[:, :], in_=rinv[:, :]), ["rinv0"], ["rinv"])
            pg.tt("dve", at[:, :].rearrange("p (h d) -> p h d", h=16), pO[:, :].rearrange("p (h d) -> p h d", h=16),
                  rinv[:, :].unsqueeze(2).broadcast_to([128, 16, 64]), ALU.mult, ["pO0", "pO1", "rinv"], ["at"])
            for k in range(8):
                pg.tr(pPT[0][:, k, :], at[:, k * 128:(k + 1) * 128], C.identB[:], ["at", "identB"], ["pPT0"])
            pg.copy("act", atT[:, :, :], pPT[0][:, :, :], ["pPT0"], ["atT"])
            for hlf in range(2):
                for k in range(8):
                    pg.mm(pmix[:, hlf * 512:(hlf + 1) * 512], atT[:, k, :], wo[:, k, hlf * 512:(hlf + 1) * 512],
                          k == 0, k == 7, ["atT", f"wo{k}"], [f"pmix{hlf}"])
                pg.stt("dve", r[:, hlf * 512:(hlf + 1) * 512], xs[b][:, hlf * 512:(hlf + 1) * 512], ALPHA,
                       pmix[:, hlf * 512:(hlf + 1) * 512], ALU.mult, ALU.add, [f"xs{b}", f"pmix{hlf}"], ["r"])
            layer_norm_tile(C, r, xn[b], tmp[b], gt, bt, st6, mv, lnv, rstd, nmr, valid[:, i:i + 1], "H", "r",
                            f"xn{b}", ttag=f"tmp{b}", defer=True)

        def back(i):
            b = i % 2
            layer_norm_back(C, xn[b], tmp[b], gt, bt, valid[:, i:i + 1], "H", f"xn{b}", f"tmp{b}")
            pg.dma("pool", g.xn1[i * 128:(i + 1) * 128, :], xn[b][:], [f"xn{b}"], [], key=f"xn{b}")
            psT = pO[:, :].rearrange("p (k t) -> p k t", k=8)
            for k in range(8):
                pg.tr(psT[:, k, :], xn[b][:, k * 128:(k + 1) * 128], C.identF[:], [f"xn{b}", "identF"],
                      ["pO0", "pO1"])
            pg.copy("act", xnT[b][:, 0:4, :], psT[:, 0:4, :], ["pO0"], [f"xnT{b}"])
            pg.copy("dve", xnT[b][:, 4:8, :], psT[:, 4:8, :], ["pO1"], [f"xnT{b}"])
            pg.dma("pool", g.xn1T[:, :, 1 + i * 128:1 + (i + 1) * 128].rearrange("k p t -> p k t"), xnT[b][:],
                   [f"xnT{b}"], [], key=f"xnT{b}")


        for i in range(1, sc.nt):
            if i < sc.nt - 1:
                front(i)
            if i >= 2:
                back(i - 1)


def make_in_maps(cfg, inputs, n_cores=8):
    sh = prep_shared(inputs)
    P, Sg = cfg
    maps = []
    for c in range(n_cores):
        m = dict(sh)
        m.update(prep_seg(P, inputs["x_prompt"][c // 2], (c % 2) * P.NQ))
        m.update(prep_seg(Sg, inputs["x_sample"][0], c * Sg.NQ))
        maps.append(m)
    return maps


def assemble(cfg, res, inputs):
    P, Sg = cfg
    yp = np.zeros(inputs["x_prompt"].shape, np.float32)
    ys = np.zeros(inputs["x_sample"].shape, np.float32)
    for c in range(8):
        yp[c // 2, (c % 2) * P.NQ:(c % 2 + 1) * P.NQ] = res[c]["P_out"]
        ys[0, c * Sg.NQ:(c + 1) * Sg.NQ] = res[c]["S_out"]
    return yp, ys


def kernel(**inputs):
    inputs = {k: np.asarray(v) for k, v in inputs.items()}
    cfg = FULL_CFG
    nc, _ = build(cfg)
    maps = make_in_maps(cfg, inputs)
    res = run_bass_kernel_spmd(nc, maps, core_ids=list(range(8)))
    return assemble(cfg, res.results, inputs)
```

```python
import numpy as np
import ml_dtypes
from contextlib import ExitStack
import concourse.bass as bass
import concourse.mybir as mybir
from concourse.bass_utils import run_bass_kernel_spmd

F32 = mybir.dt.float32
BF16 = mybir.dt.bfloat16
AF = mybir.ActivationFunctionType
ALU = mybir.AluOpType
AX = mybir.AxisListType

D = 1024
DFF = 2816
NEG = -30000.0
ALPHA = float(4 ** 0.25)
EPS = 1e-5
HALO = 256
KH = 512
NB = 256
QPERM = [0, 4, 1, 5, 2, 6, 3, 7, 8, 12, 9, 13, 10, 14, 11, 15]
MLA_SCALE = float(96 ** -0.5)


class SegCfg:
    def __init__(self, name, S, NQ):
        self.name = name
        self.S = S
        self.NQ = NQ
        self.T = NQ + 2 * HALO
        self.T2 = self.T + 2 * KH
        self.nt = self.T // 128
        self.nt2 = self.T2 // 128
        self.nts = S // 128
        self.nqt = self.nt + 1


FULL_CFG = [SegCfg("P", 8192, 4096), SegCfg("S", 16384, 2048)]


DMAQ = {"pool": "sp"}
LOADQ = "act"
HOIST = True
NOCLEAR = False


class _Op:
    __slots__ = ("eng", "fn", "reads", "writes", "dma", "key", "needs_inc", "count", "deps",
                 "sem", "semval", "waits")


class Prog:
    ENGS = ("pe", "act", "dve", "pool", "sp")

    def __init__(self, nc, es, ndma=38):
        self.nc = nc
        self.sets = []
        for s in range(2):
            eng = {e: es.enter_context(nc.semaphore(f"s{s}_{e}")) for e in self.ENGS}
            dma = [es.enter_context(nc.semaphore(f"s{s}_d{i}")) for i in range(ndma)]
            self.sets.append((eng, dma))
        self.phase = 0
        self.ops = []
        self.first = True

    def add(self, eng, fn, reads=(), writes=(), dma=False, key=None):
        op = _Op()
        op.eng = eng
        op.fn = fn
        op.reads = tuple(reads)
        op.writes = tuple(writes)
        op.dma = dma
        op.key = key
        op.needs_inc = False
        op.count = 0
        self.ops.append(op)
        return op

    def mm(self, out, lhsT, rhs, start, stop, reads, writes):
        self.add("pe", lambda h: h.matmul(out, lhsT, rhs, start=start, stop=stop), reads, writes)

    def tr(self, out, in_, ident, reads, writes):
        self.add("pe", lambda h: h.transpose(out, in_, ident), reads, writes)

    def act(self, out, in_, func, reads, writes, **kw):
        self.add("act", lambda h: h.activation(out=out, in_=in_, func=func, **kw), reads, writes)

    def copy(self, eng, out, in_, reads, writes, scale=None):
        if eng == "act":
            if scale is None:
                self.add("act", lambda h: h.activation(out=out, in_=in_, func=AF.Copy), reads, writes)
            else:
                self.add("act", lambda h: h.activation(out=out, in_=in_, func=AF.Copy, scale=scale),
                         reads, writes)
        else:
            if scale is None:
                self.add(eng, lambda h: h.tensor_copy(out=out, in_=in_), reads, writes)
            else:
                self.add(eng, lambda h: h.tensor_scalar(out=out, in0=in_, scalar1=scale, scalar2=None,
                                                        op0=ALU.mult), reads, writes)

    def tt(self, eng, out, in0, in1, op, reads, writes):
        self.add(eng, lambda h: h.tensor_tensor(out=out, in0=in0, in1=in1, op=op), reads, writes)

    def ts(self, eng, out, in0, s1, s2, op0, op1, reads, writes):
        if op1 is None:
            self.add(eng, lambda h: h.tensor_scalar(out=out, in0=in0, scalar1=s1, scalar2=None, op0=op0),
                     reads, writes)
        else:
            self.add(eng, lambda h: h.tensor_scalar(out=out, in0=in0, scalar1=s1, scalar2=s2, op0=op0,
                                                    op1=op1), reads, writes)

    def stt(self, eng, out, in0, scalar, in1, op0, op1, reads, writes):
        self.add(eng, lambda h: h.scalar_tensor_tensor(out=out, in0=in0, scalar=scalar, in1=in1,
                                                       op0=op0, op1=op1), reads, writes)

    def memset(self, eng, ap, val, writes):
        self.add(eng, lambda h: h.memset(ap, val), (), writes)

    def dma(self, q, out, in_, reads, writes, key):
        q = DMAQ.get(q, q)
        if writes and LOADQ:
            q = LOADQ
        self.add(q, lambda h: h.dma_start(out=out, in_=in_), reads, writes, dma=True, key=key)

    def flush(self):
        nc = self.nc
        engsem, dmapool = self.sets[self.phase % 2]
        oeng, odma = self.sets[(self.phase + 1) % 2]
        ops = self.ops
        if HOIST:
            touch = {}
            keyed = []
            for i, op in enumerate(ops):
                k = float(i)
                if op.dma and op.writes:
                    prev = [touch[b] for b in op.writes if b in touch]
                    if prev:
                        k = max(prev) + 0.5 + 1e-6 * (i % 1000)
                keyed.append((k, i, op))
                for b in op.reads + op.writes:
                    touch[b] = max(touch.get(b, -1.0), k)
            keyed.sort(key=lambda x: (x[0], x[1]))
            ops = [x[2] for x in keyed]
        last_w = {}
        readers = {}
        dmakey = {}
        for op in ops:
            deps = []
            for b in op.reads:
                w = last_w.get(b)
                if w is not None:
                    deps.append((w, 0))
                if b[0] == "p" and b[:3] != "pen":
                    for r in readers.get(b, {}).values():
                        if r.eng != op.eng:
                            deps.append((r, 2))
            for b in op.writes:
                w = last_w.get(b)
                if w is not None:
                    deps.append((w, 1))
                for r in readers.get(b, {}).values():
                    deps.append((r, 2))
            for b in op.writes:
                last_w[b] = op
                readers[b] = {}
            for b in op.reads:
                readers.setdefault(b, {})[(op.eng, op.key if op.dma else None)] = op
            op.deps = []
            for (s, kind) in deps:
                if s is op:
                    continue
                if (not s.dma) and (not op.dma) and s.eng == op.eng:
                    if kind != 0 or op.eng == "pe":
                        continue
                op.deps.append(s)
                if not s.dma:
                    s.needs_inc = True
            if op.dma:
                ent = dmakey.get(op.key)
                if ent is None:
                    assert len(dmakey) < len(dmapool), "out of dma semaphores"
                    ent = [dmapool[len(dmakey)], 0, None]
                    dmakey[op.key] = ent
                if ent[2] is not None:
                    op.deps.append(ent[2])
                ent[2] = op
                ent[1] += 16
                op.sem = ent[0]
                op.semval = ent[1]
        cnt = {e: 0 for e in self.ENGS}
        for op in ops:
            if (not op.dma) and op.needs_inc:
                cnt[op.eng] += 1
                op.count = cnt[op.eng]
        waited = {e: {} for e in self.ENGS}
        for op in ops:
            need = {}
            for s in op.deps:
                if s.dma:
                    sid, sem, val = ("d", id(s.sem)), s.sem, s.semval
                else:
                    sid, sem, val = ("e", s.eng), engsem[s.eng], s.count
                if sid not in need or need[sid][1] < val:
                    need[sid] = (sem, val)
            op.waits = []
            wd = waited[op.eng]
            for sid, (sem, val) in need.items():
                if wd.get(sid, 0) >= val:
                    continue
                wd[sid] = val
                op.waits.append((sem, val))
        per = {e: [] for e in self.ENGS}
        for op in ops:
            per[op.eng].append(op)
        finals = [(ent[0], ent[1]) for ent in dmakey.values()]
        first = self.first

        def mk(e):
            def body(h):
                if e == "sp" and not first and not NOCLEAR:
                    for s_ in list(oeng.values()) + list(odma):
                        h.sem_clear(s_)
                for op in per[e]:
                    for (sem, val) in op.waits:
                        h.wait_ge(sem, val)
                    ins = op.fn(h)
                    if op.dma:
                        ins.then_inc(op.sem, 16)
                    elif op.needs_inc:
                        ins.then_inc(engsem[e], 1)
                if e == "sp":
                    for (sem, val) in finals:
                        h.wait_ge(sem, val)
            return body

        with nc.Block() as block:
            block.tensor(mk("pe"))
            block.scalar(mk("act"))
            block.vector(mk("dve"))
            block.gpsimd(mk("pool"))
            block.sync(mk("sp"))
        self.n_ops = getattr(self, "n_ops", 0) + len(ops)
        self.ops = []
        self.phase += 1
        self.first = False


def _rope_tab(pos, theta, rot_dim):
    half = rot_dim // 2
    inv = (np.float32(1.0) / (np.float32(theta) ** (np.arange(0, rot_dim, 2, dtype=np.float32)
                                                      / np.float32(rot_dim)))).astype(np.float32)
    ang = pos.astype(np.float32)[:, None] * inv[None, :]
    cos = np.cos(ang).astype(np.float32)
    sin = np.sin(ang).astype(np.float32)
    c2 = np.concatenate([cos, cos], axis=1)
    s2 = np.concatenate([-sin, sin], axis=1)
    return c2, s2


def _bc(v, n=128):
    return np.ascontiguousarray(np.broadcast_to(np.asarray(v, np.float32)[None, :], (n, v.shape[0])))


def prep_shared(inp):
    sh = {}
    w = inp["l0_w_in"]
    kr = w[:, 2176:2208]
    sh["wa"] = np.ascontiguousarray(np.concatenate([w[:, 1920:2208], kr[:, 16:32], kr[:, 0:16]], axis=1))
    sh["wb"] = np.ascontiguousarray(w[:, 0:1920])
    wq = inp["l0_w_q_up"].reshape(384, 8, 96)
    sh["wq"] = np.ascontiguousarray(wq.reshape(384, 768))
    wqs = np.concatenate([wq[:, :, 0:64], wq[:, :, 80:96], wq[:, :, 64:80]], axis=2)
    sh["wqs"] = np.ascontiguousarray(wqs.reshape(384, 768))
    wkv = inp["l0_w_kv_up"].reshape(256, 8, 128)
    sh["wkvK"] = np.ascontiguousarray(wkv[:, :, 0:64].reshape(256, 512))
    sh["wkvV"] = np.ascontiguousarray(wkv[:, :, 64:128].reshape(256, 512))
    sh["gq"] = _bc(inp["l0_g_q_norm"])
    sh["gkv"] = _bc(inp["l0_g_kv_norm"])
    sh["wo0"] = np.ascontiguousarray(inp["l0_w_out"])
    rpb = inp["l0_rpb"]
    B = np.full((8, 2, 64, 16, 64), NEG, np.float32)
    c = np.arange(64)
    cs = np.clip(c - 8, 0, 48)
    for i in range(2):
        for wv in range(16):
            dr = wv - i
            if dr < 0 or dr > 14:
                continue
            for cc in range(64):
                kc = cs[cc] + np.arange(16)
                B[:, i, cc, wv, kc] = rpb[:, dr, kc - cc + 15]
    sh["nab"] = np.ascontiguousarray(B.reshape(8, 128, 1024).transpose(1, 0, 2))
    for l in (0, 1):
        p = f"l{l}_"
        sh[p + "ln1g"] = _bc(inp[p + "ln1_g"])
        sh[p + "ln1b"] = _bc(inp[p + "ln1_b"])
        sh[p + "ln2g"] = _bc(inp[p + "ln2_g"])
        sh[p + "ln2b"] = _bc(inp[p + "ln2_b"])
        sh[p + "wup"] = np.ascontiguousarray(inp[p + "ffn_w_up"])
        sh[p + "wdn"] = np.ascontiguousarray(inp[p + "ffn_w_down"])
        cw = inp[p + "ffn_conv_w"]
        sh[p + "cw"] = np.ascontiguousarray(cw.reshape(3, 44, 128).transpose(2, 1, 0))
        sh[p + "cb"] = np.ascontiguousarray(inp[p + "ffn_conv_b"].reshape(44, 128).T)
    w1 = inp["l1_w_in"]
    q = w1[:, 0:1024].reshape(D, 16, 64)[:, QPERM, :]
    k = w1[:, 1024:1280].reshape(D, 4, 64)
    qk = np.concatenate([q, k], axis=1)
    sw = np.concatenate([qk[:, :, 8:16], qk[:, :, 0:8]], axis=2)
    sh["w1in"] = np.ascontiguousarray(np.concatenate([qk.reshape(D, 1280), w1[:, 1280:1536],
                                                      sw.reshape(D, 320)], axis=1))
    sh["wo1"] = np.ascontiguousarray(inp["l1_w_out"].reshape(16, 64, D)[QPERM].reshape(1024, D))
    sh["sinks"] = _bc(inp["l1_sinks"][QPERM])
    sh["ident"] = np.eye(128, dtype=np.float32)
    a = np.arange(128)[:, None]
    j = np.arange(384)[None, :]
    sh["band"] = np.where((j >= a) & (j <= a + 256), 0.0, NEG).astype(ml_dtypes.bfloat16)
    qs = np.zeros((2, 128), np.float32)
    qs[0, 0:64] = 1.0
    qs[1, 64:128] = 1.0
    sh["qsel"] = qs.astype(ml_dtypes.bfloat16)
    return sh


def prep_seg(seg, x_seq, a):
    S, T, T2 = seg.S, seg.T, seg.T2
    n = seg.name
    m = {}
    m[n + "_xseq"] = np.ascontiguousarray(x_seq)
    base2 = a - HALO - KH
    xe = np.zeros((T2, D), np.float32)
    lo, hi = max(base2, 0), min(base2 + T2, S)
    xe[lo - base2:hi - base2] = x_seq[lo:hi]
    m[n + "_xext"] = xe
    pos_e = np.arange(a - HALO, a - HALO + T)
    valid = ((pos_e >= 0) & (pos_e < S)).astype(np.float32)
    m[n + "_valid"] = np.ascontiguousarray(valid.reshape(seg.nt, 128).T)
    c2, s2 = _rope_tab(np.arange(S), 10000.0, 32)
    kcs = np.concatenate([c2, s2], axis=1).reshape(seg.nts, 128, 64).transpose(1, 0, 2)
    m[n + "_kcs"] = np.ascontiguousarray(kcs)
    pos2 = np.clip(np.arange(base2, base2 + T2), 0, S - 1)
    c2, s2 = _rope_tab(pos2, 10000.0, 32)
    m[n + "_qcsT"] = np.ascontiguousarray(np.concatenate([c2.T, s2.T], axis=0))
    c2, s2 = _rope_tab(np.clip(pos_e, 0, S - 1), 500000.0, 16)
    l1cs = np.concatenate([c2, s2], axis=1).reshape(seg.nt, 128, 32).transpose(1, 0, 2)
    m[n + "_l1cs"] = np.ascontiguousarray(l1cs)
    m[n + "_kpen"] = np.where(valid > 0, 0.0, NEG).astype(ml_dtypes.bfloat16)[None, :]
    rows = S // 64
    pen = np.zeros((seg.nqt, 2, 16, 64), np.float32)
    for j in range(seg.nqt):
        r_abs = (a - HALO) // 64 + 2 * j - 1
        for i in range(2):
            rq = r_abs + i
            if rq < 0 or rq >= rows:
                continue
            rs = min(max(rq - 4, 0), rows - 8)
            for wv in range(16):
                krow = r_abs - 7 + wv
                if not (rs <= krow < rs + 8):
                    pen[j, i, wv, :] = NEG
    m[n + "_pen"] = pen.reshape(seg.nqt, 2, 1024).astype(ml_dtypes.bfloat16)
    return m


class Ctx:
    pass


def build(cfg, debug=()):
    nc = bass.Bass("TRN2", target_bir_lowering=False)
    C = Ctx()
    C.nc = nc
    C.debug = set(debug)

    def din(name, shape, dt=F32):
        return nc.dram_tensor(name, list(shape), dt, kind="ExternalInput").ap()

    def dscr(name, shape, dt):
        if name in C.debug:
            return nc.dram_tensor(name, list(shape), dt, kind="ExternalOutput").ap()
        return nc.dram_tensor(name, list(shape), dt).ap()

    W = {}
    W["wa"] = din("wa", [D, 320])
    W["wb"] = din("wb", [D, 1920])
    W["wq"] = din("wq", [384, 768])
    W["wqs"] = din("wqs", [384, 768])
    W["wkvK"] = din("wkvK", [256, 512])
    W["wkvV"] = din("wkvV", [256, 512])
    W["gq"] = din("gq", [128, 384])
    W["gkv"] = din("gkv", [128, 256])
    W["wo0"] = din("wo0", [D, D])
    W["nab"] = din("nab", [128, 8, 1024])
    for l in (0, 1):
        p = f"l{l}_"
        for nm in ("ln1g", "ln1b", "ln2g", "ln2b"):
            W[p + nm] = din(p + nm, [128, D])
        W[p + "wup"] = din(p + "wup", [D, 2 * DFF])
        W[p + "wdn"] = din(p + "wdn", [DFF, D])
        W[p + "cw"] = din(p + "cw", [128, 44, 3])
        W[p + "cb"] = din(p + "cb", [128, 44])
    W["w1in"] = din("w1in", [D, 1856])
    W["wo1"] = din("wo1", [D, D])
    W["sinks"] = din("sinks", [128, 16])
    W["ident"] = din("ident", [128, 128])
    W["band"] = din("band", [128, 384], BF16)
    W["qsel"] = din("qsel", [2, 128], BF16)
    C.W = W

    segs = []
    for sc in cfg:
        n = sc.name
        g = Ctx()
        g.c = sc
        g.xseq = din(n + "_xseq", [sc.S, D])
        g.xext = din(n + "_xext", [sc.T2, D])
        g.valid = din(n + "_valid", [128, sc.nt])
        g.kcs = din(n + "_kcs", [128, sc.nts, 64])
        g.qcsT = din(n + "_qcsT", [64, sc.T2])
        g.l1cs = din(n + "_l1cs", [128, sc.nt, 32])
        g.kpen = din(n + "_kpen", [1, sc.T], BF16)
        g.pen = din(n + "_pen", [sc.nqt, 2, 1024], BF16)
        g.Ks = dscr(n + "_Ks", [8, 97, sc.S], BF16)
        g.Vs = dscr(n + "_Vs", [8, 128, sc.nts, 65], BF16)
        g.qaug = dscr(n + "_qaug", [8, 96, sc.T2], BF16)
        g.qaT = dscr(n + "_qaT", [4, 128, sc.T2], BF16)
        g.kaT = dscr(n + "_kaT", [4, 128, sc.T2], BF16)
        g.va = dscr(n + "_va", [sc.T2, 512], BF16)
        g.catT = dscr(n + "_catT", [8, 128, sc.T], BF16)
        g.xmid = dscr(n + "_xmid", [sc.T, D], F32)
        g.xmidT = dscr(n + "_xmidT", [8, 128, sc.T + 2], BF16)
        g.x1 = dscr(n + "_x1", [sc.T, D], F32)
        g.q1T = dscr(n + "_q1T", [8, 128, sc.T], BF16)
        g.k1T = dscr(n + "_k1T", [2, 128, sc.T], BF16)
        g.v1 = dscr(n + "_v1", [sc.T, 256], BF16)
        g.xn1 = dscr(n + "_xn1", [sc.T, D], F32)
        g.xn1T = dscr(n + "_xn1T", [8, 128, sc.T + 2], BF16)
        g.out = nc.dram_tensor(n + "_out", [sc.NQ, D], F32, kind="ExternalOutput").ap()
        segs.append(g)

    with ExitStack() as es:
        pg = Prog(nc, es)
        C.pg = pg
        C.identF = es.enter_context(nc.sbuf_tensor("identF", [128, 128], F32))
        C.identB = es.enter_context(nc.sbuf_tensor("identB", [128, 128], BF16))
        C.ones = es.enter_context(nc.sbuf_tensor("ones", [128, 64], F32))
        C.onesB = es.enter_context(nc.sbuf_tensor("onesB", [128, 128], BF16))
        C.zerosB = es.enter_context(nc.sbuf_tensor("zerosB", [128, 8], BF16))
        pg.dma("sp", C.identF[:], W["ident"], [], ["identF"], key="identF")
        pg.copy("dve", C.identB[:], C.identF[:], ["identF"], ["identB"])
        pg.memset("pool", C.ones[:], 1.0, ["ones"])
        pg.memset("pool", C.onesB[:], 1.0, ["onesB"])
        pg.memset("pool", C.zerosB[:], 0.0, ["zerosB"])
        pg.flush()
        stop_after = C.debug_stop = [d for d in C.debug if d.startswith("stop:")]
        stop = stop_after[0][5:] if stop_after else None
        phases = [("A", phase_A), ("B", phase_B), ("C", phase_C), ("D", phase_D), ("E", phase_E),
                  ("F", lambda C_, g_: phase_FFN(C_, g_, 0)), ("G", phase_G), ("H", phase_H),
                  ("I", lambda C_, g_: phase_FFN(C_, g_, 1))]
        only = [d[5:] for d in C.debug if d.startswith("only:")]
        for (pn, fn) in phases:
            with ExitStack() as pes:
                C.pes = pes
                C.shared = {}
                for g in segs:
                    if only and pn not in only[0]:
                        continue
                    fn(C, g)
                    pg.flush()
            if stop == pn:
                break
    C.n_ops = pg.n_ops
    return nc, C


_UID = [0]


def shared_sb(C, name, shp, dt):
    if name in C.shared:
        return C.shared[name], False
    _UID[0] += 1
    t = C.pes.enter_context(C.nc.sbuf_tensor(f"sh{_UID[0]}_{name}", list(shp), dt))
    C.shared[name] = t
    return t, True


def _pools(C, es):
    nc = C.nc
    _UID[0] += 1
    u = _UID[0]

    def sb(n, shp, dt):
        return es.enter_context(nc.sbuf_tensor(f"sb{u}_{n}", list(shp), dt))

    def ps(n, shp, dt=F32):
        return es.enter_context(nc.psum_tensor(f"ps{u}_{n}", list(shp), dt))

    return sb, ps


def load_cast(C, dst, src, K, N, stage, name, engs=("pool", "act", "dve"), ctr=[0], cwmax=2048):
    pg = C.pg
    for k in range(K):
        for c0 in range(0, N, cwmax):
            cw = min(cwmax, N - c0)
            i = ctr[0]
            ctr[0] += 1
            b = i % 2
            pg.dma("sp", stage[b][:, 0:cw], src[k * 128:(k + 1) * 128, c0:c0 + cw], [], [f"stg{b}"],
                   key=f"stg{b}")
            pg.copy(engs[i % len(engs)], dst[:, k, c0:c0 + cw], stage[b][:, 0:cw], [f"stg{b}"],
                    [f"{name}{k}"])


def load_x_T(C, src_rows, xs, xT, psT, b, tagx):
    pg = C.pg
    pg.dma("sp", xs[b][:], src_rows, [], [f"xs{b}"], key=f"xs{b}")
    for k in range(8):
        pg.tr(psT[:, k, :], xs[b][:, k * 128:(k + 1) * 128], C.identF[:], [f"xs{b}", "identF"],
              ["psTa" if k < 4 else "psTb"])
    pg.copy("act", xT[b][:, 0:4, :], psT[:, 0:4, :], ["psTa"], [f"{tagx}{b}a"])
    pg.copy("dve", xT[b][:, 4:8, :], psT[:, 4:8, :], ["psTb"], [f"{tagx}{b}b"])


def rms_rstd(C, ss, lnv, rstd, n, tag):
    pg = C.pg
    pg.act(lnv[:, 0:1], ss[:, 0:1], AF.Ln, [tag + "ss"], [tag + "lnv"], scale=1.0 / n, bias=C.epsb[:, 0:1])
    pg.act(rstd[:, 0:1], lnv[:, 0:1], AF.Exp, [tag + "lnv"], [tag + "rstd"], scale=-0.5)


def phase_A(C, g):
    pg, nc, sc, W = C.pg, C.nc, g.c, C.W
    with ExitStack() as es:
        sb, ps = _pools(C, es)
        wa = sb("wa_sb", [128, 8, 320], BF16)
        wkK = sb("wkK", [128, 2, 512], BF16)
        wkV = sb("wkV", [128, 2, 512], BF16)
        gkv = sb("gkv", [128, 256], F32)
        kcs = sb("kcs", [128, sc.nts, 64], F32)
        C.epsb = sb("epsb", [128, 1], F32)
        stage = [sb(f"stg{i}", [128, 2048], F32) for i in range(2)]
        xs = [sb(f"xs{i}", [128, 1024], F32) for i in range(2)]
        xT = [sb(f"xT{i}", [128, 8, 128], BF16) for i in range(2)]
        junk = sb("junk", [128, 256], F32)
        ss = sb("ss", [128, 1], F32)
        lnv = sb("lnv", [128, 1], F32)
        rstd = sb("rstd", [128, 1], F32)
        ckr = [sb(f"ckr{i}", [128, 352], BF16) for i in range(2)]
        m1 = sb("m1", [128, 32], F32)
        m2 = sb("m2", [128, 32], F32)
        cTg = sb("cTg", [128, 2, 512], BF16)
        kst = sb("kst", [64, 8, 512], BF16)
        krT = sb("krT", [32, 512], BF16)
        vst = [sb(f"vst{i}", [128, 8, 65], BF16) for i in range(2)]
        psT = ps("psT", [128, 8, 128])
        pkv = ps("pkv", [128, 512])
        pcT = ps("pcT", [128, 3, 128], BF16)
        pV = ps("pV", [128, 512])
        pK = [ps(f"pK{i}", [128, 512]) for i in range(2)]

        pg.memset("dve", C.epsb[:], EPS, ["epsb"])
        load_cast(C, wa, W["wa"], 8, 320, stage, "wa")
        load_cast(C, wkK, W["wkvK"], 2, 512, stage, "wkK")
        load_cast(C, wkV, W["wkvV"], 2, 512, stage, "wkV")
        pg.dma("sp", gkv[:], W["gkv"], [], ["gkv"], key="gkv")
        pg.dma("sp", kcs[:], g.kcs, [], ["kcs"], key="kcs")
        for i in range(2):
            pg.memset("pool", ckr[i][:], 0.0, [f"ckr_c{i}", f"ckr_r{i}"])
        for i in range(2):
            pg.memset("pool", vst[i][:], 1.0, [f"vst{i}"])
        wa_r = [f"wa{k}" for k in range(8)]

        def front(t):
            b = t % 2
            ck = ckr[b]
            load_x_T(C, g.xseq[t * 128:(t + 1) * 128, :], xs, xT, psT, b, "xT")
            for k in range(8):
                pg.mm(pkv[:, 0:320], xT[b][:, k, :], wa[:, k, :], k == 0, k == 7,
                      [f"xT{b}a" if k < 4 else f"xT{b}b", wa_r[k]], ["pkv"])
            pg.act(junk[:, 0:256], pkv[:, 0:256], AF.Square, ["pkv"], ["junk", "Ass"], accum_out=ss[:, 0:1])
            pg.tt("dve", m1[:], pkv[:, 256:288], kcs[:, t, 0:32], ALU.mult, ["pkv", "kcs"], ["m1"])
            pg.tt("dve", m2[:], pkv[:, 288:320], kcs[:, t, 32:64], ALU.mult, ["pkv", "kcs"], ["m2"])
            pg.tt("pool", ck[:, 320:352], m1[:], m2[:], ALU.add, ["m1", "m2"], [f"ckr_r{b}"])
            rms_rstd(C, ss, lnv, rstd, 256, "A")
            pg.stt("dve", ck[:, 0:256], pkv[:, 0:256], rstd[:, 0:1], gkv[:], ALU.mult, ALU.mult,
                   ["pkv", "Arstd", "gkv"], [f"ckr_c{b}"])

        def back(t):
            b = t % 2
            ck = ckr[b]
            grp, j = t // 4, t % 4
            pg.tr(pcT[:, 0, :], ck[:, 0:128], C.identB[:], [f"ckr_c{b}", "identB"], ["pcT"])
            pg.tr(pcT[:, 1, :], ck[:, 128:256], C.identB[:], [f"ckr_c{b}", "identB"], ["pcT"])
            pg.tr(pcT[0:32, 2, :], ck[:, 320:352], C.identB[:], [f"ckr_r{b}", "identB"], ["pcT"])
            pg.copy("act", cTg[:, :, j * 128:(j + 1) * 128], pcT[:, 0:2, :], ["pcT"], [f"cTg{j}"])
            pg.copy("dve", krT[0:32, j * 128:(j + 1) * 128], pcT[0:32, 2, :], ["pcT"], [f"krT{j}"])
            vb = t % 2
            for k in range(2):
                pg.mm(pV[:, :], cTg[:, k, j * 128:(j + 1) * 128], wkV[:, k, :], k == 0, k == 1,
                      [f"cTg{j}", f"wkV{k}"], ["pV"])
            pg.copy("dve", vst[vb][:, :, 0:64], pV[:, :].rearrange("p (h d) -> p h d", h=8), ["pV"], [f"vst{vb}"])
            pg.dma("pool", g.Vs[:, :, t, :].rearrange("h p c -> p h c"), vst[vb][:], [f"vst{vb}"], [],
                   key=f"vst{vb}")
            if j == 3:
                ctg_r = [f"cTg{jj}" for jj in range(4)]
                for h in range(8):
                    pb = h % 2
                    for k in range(2):
                        pg.mm(pK[pb][0:64, :], wkK[:, k, h * 64:(h + 1) * 64], cTg[:, k, :], k == 0, k == 1,
                              ctg_r + [f"wkK{k}"], [f"pK{pb}"])
                    pg.copy("act" if h % 2 else "dve", kst[0:64, h, :], pK[pb][0:64, :], [f"pK{pb}"], [f"kstn{h}"])
                pg.dma("pool", g.Ks[:, 0:64, grp * 512:(grp + 1) * 512].rearrange("h d c -> d h c"), kst[:],
                       [f"kstn{h}" for h in range(8)], [], key="kst")
                for h in range(8):
                    pg.dma("pool", g.Ks[h, 64:96, grp * 512:(grp + 1) * 512], krT[:, :],
                           [f"krT{jj}" for jj in range(4)], [], key=f"krT{h}")

        for t in range(sc.nts + 1):
            if t < sc.nts:
                front(t)
            if t >= 1:
                back(t - 1)


def phase_B(C, g):
    pg, nc, sc, W = C.pg, C.nc, g.c, C.W
    with ExitStack() as es:
        sb, ps = _pools(C, es)
        wb, fresh = shared_sb(C, "wb_sb", [128, 8, 1920], BF16)
        wq, _ = shared_sb(C, "wq_sb", [128, 3, 768], BF16)
        wqs, _ = shared_sb(C, "wqs_sb", [128, 3, 768], BF16)
        gq, _ = shared_sb(C, "gq", [128, 384], F32)
        C.epsb = sb("epsb", [128, 1], F32)
        ctab = sb("ctab", [96, 512], F32)
        stab = sb("stab", [96, 512], F32)
        stage = [sb(f"stg{i}", [128, 2048], F32) for i in range(2)]
        xs = [sb(f"xs{i}", [128, 1024], F32) for i in range(2)]
        xTg = sb("xTg", [128, 8, 512], BF16)
        junk = sb("junk", [128, 384], F32)
        ss = sb("ss", [128, 1], F32)
        lnv = sb("lnv", [128, 1], F32)
        rstd = sb("rstd", [128, 1], F32)
        cq = sb("cq", [128, 384], BF16)
        cqT = sb("cqT", [128, 3, 512], BF16)
        qkst = [sb(f"qkst{i}", [128, 512], BF16) for i in range(2)]
        vast = [sb(f"vast{i}", [128, 512], BF16) for i in range(2)]
        qst = [sb(f"qst{i}", [96, 512], BF16) for i in range(2)]
        r1 = sb("r1", [96, 512], F32)
        r2 = sb("r2", [96, 512], F32)
        psT = ps("psT", [128, 8, 128])
        pA = [ps(f"pA{i}", [128, 512]) for i in range(2)]
        pB = [ps(f"pB{i}", [128, 512]) for i in range(2)]
        pcq = ps("pcq", [128, 3, 128], BF16)

        pg.memset("dve", C.epsb[:], EPS, ["epsb"])
        if fresh:
            load_cast(C, wb, W["wb"], 8, 1920, stage, "wb")
            load_cast(C, wq, W["wq"], 3, 768, stage, "wq")
            load_cast(C, wqs, W["wqs"], 3, 768, stage, "wqs")
            pg.dma("sp", gq[:], W["gq"], [], ["gq"], key="gq")
        wb_r = [f"wb{k}" for k in range(8)]
        ngrp = sc.nt2 // 4
        for grp in range(ngrp):
            c0 = grp * 512
            xr = []
            for j in range(4):
                t = grp * 4 + j
                b = t % 2
                pg.dma("sp", xs[b][:], g.xext[t * 128:(t + 1) * 128, :], [], [f"xs{b}"], key=f"xs{b}")
                for k in range(8):
                    pg.tr(psT[:, k, :], xs[b][:, k * 128:(k + 1) * 128], C.identF[:], [f"xs{b}", "identF"],
                          ["psTa" if k < 4 else "psTb"])
                pg.copy("act", xTg[:, 0:4, j * 128:(j + 1) * 128], psT[:, 0:4, :], ["psTa"], [f"xTg{j}"])
                pg.copy("dve", xTg[:, 4:8, j * 128:(j + 1) * 128], psT[:, 4:8, :], ["psTb"], [f"xTg{j}"])
                xr.append(f"xTg{j}")
            for which, col0, dst, scl in (("q", 0, g.qaT, 0.125), ("k", 512, g.kaT, None)):
                for p in range(4):
                    pb = p % 2
                    for k in range(8):
                        pg.mm(pA[pb][:, :], wb[:, k, col0 + p * 128: col0 + (p + 1) * 128], xTg[:, k, :],
                              k == 0, k == 7, xr + [wb_r[k]], [f"pA{pb}"])
                    pg.copy("act" if pb else "dve", qkst[pb][:, :], pA[pb][:, :], [f"pA{pb}"], [f"qkst{pb}"],
                            scale=scl)
                    pg.dma("pool", dst[p, :, c0:c0 + 512], qkst[pb][:], [f"qkst{pb}"], [], key=f"qkst{pb}")
            for j in range(4):
                t = grp * 4 + j
                pb = j % 2
                for k in range(8):
                    pg.mm(pA[pb][:, :], xTg[:, k, j * 128:(j + 1) * 128], wb[:, k, 1024:1536], k == 0, k == 7,
                          [f"xTg{j}", wb_r[k]], [f"pA{pb}"])
                pg.copy("act", vast[pb][:, :], pA[pb][:, :], [f"pA{pb}"], [f"vast{pb}"])
                pg.dma("pool", g.va[t * 128:(t + 1) * 128, :], vast[pb][:], [f"vast{pb}"], [], key=f"vast{pb}")
                for k in range(8):
                    pg.mm(pB[pb][:, 0:384], xTg[:, k, j * 128:(j + 1) * 128], wb[:, k, 1536:1920], k == 0,
                          k == 7, [f"xTg{j}", wb_r[k]], [f"pB{pb}"])
                pg.act(junk[:, 0:384], pB[pb][:, 0:384], AF.Square, [f"pB{pb}"], ["junk", "Bss"],
                       accum_out=ss[:, 0:1])
                rms_rstd(C, ss, lnv, rstd, 384, "B")
                pg.stt("dve", cq[:, :], pB[pb][:, 0:384], rstd[:, 0:1], gq[:], ALU.mult, ALU.mult,
                       [f"pB{pb}", "Brstd", "gq"], ["cq"])
                for k in range(3):
                    pg.tr(pcq[:, k, :], cq[:, k * 128:(k + 1) * 128], C.identB[:], ["cq", "identB"], ["pcq"])
                pg.copy("act", cqT[:, :, j * 128:(j + 1) * 128], pcq[:, :, :], ["pcq"], [f"cqT{j}"])
            pg.dma("sp", ctab[64:96, :], g.qcsT[0:32, c0:c0 + 512], [], ["ctab"], key="ctab")
            pg.dma("sp", stab[64:96, :], g.qcsT[32:64, c0:c0 + 512], [], ["stab"], key="stab")
            cq_r = [f"cqT{j}" for j in range(4)]
            for h in range(8):
                pb = h % 2
                for k in range(3):
                    pg.mm(pA[pb][0:96, :], wq[:, k, h * 96:(h + 1) * 96], cqT[:, k, :], k == 0, k == 2,
                          cq_r + [f"wq{k}"], [f"pA{pb}"])
                for k in range(3):
                    pg.mm(pB[pb][0:96, :], wqs[:, k, h * 96:(h + 1) * 96], cqT[:, k, :], k == 0, k == 2,
                          cq_r + [f"wqs{k}"], [f"pB{pb}"])
                pg.copy("act", qst[pb][0:64, :], pA[pb][0:64, :], [f"pA{pb}"], [f"qst{pb}n"])
                pg.tt("dve", r1[64:96, :], pA[pb][64:96, :], ctab[64:96, :], ALU.mult, [f"pA{pb}", "ctab"], ["r1"])
                pg.tt("dve", r2[64:96, :], pB[pb][64:96, :], stab[64:96, :], ALU.mult, [f"pB{pb}", "stab"], ["r2"])
                pg.tt("pool", qst[pb][64:96, :], r1[64:96, :], r2[64:96, :], ALU.add, ["r1", "r2"], [f"qst{pb}r"])
                pg.dma("pool", g.qaug[h, :, c0:c0 + 512], qst[pb][:], [f"qst{pb}n", f"qst{pb}r"], [],
                       key=f"qst{pb}")


def phase_C(C, g):
    pg, nc, sc, W = C.pg, C.nc, g.c, C.W
    with ExitStack() as es:
        sb, ps = _pools(C, es)
        nab, fresh = shared_sb(C, "nab", [128, 8, 1024], BF16)
        qsel, _ = shared_sb(C, "qsel", [2, 128], BF16)
        stage = [sb(f"stg{i}", [128, 2048], F32) for i in range(2)]
        kaT = sb("kaT", [128, 4, sc.T2], BF16)
        va = sb("va", [128, sc.nt2, 512], BF16)
        qa = [sb(f"qa{i}", [128, 4, 128], BF16) for i in range(2)]
        pen = [sb(f"pen{i}", [2, 1024], BF16) for i in range(2)]
        P = [sb(f"P{i}", [128, 1024], BF16) for i in range(2)]
        PT = [sb(f"PT{i}", [128, 8, 128], BF16) for i in range(2)]
        mx = [sb(f"mx{i}", [128, 2], F32) for i in range(2)]
        rs = sb("rs", [128, 8], F32)
        rinv = sb("rinv", [128, 8], F32)
        ao = [sb(f"ao{i}", [128, 512], BF16) for i in range(2)]
        aoT = [sb(f"aoT{i}", [128, 4, 128], BF16) for i in range(2)]
        pS = [ps(f"pS{i}", [128, 1024]) for i in range(2)]
        pPT = [ps(f"pPT{i}", [128, 8, 128], BF16) for i in range(2)]
        pO = ps("pO", [128, 512])
        paT = ps("paT", [128, 4, 128], BF16)

        for h in range(8 if fresh else 0):
            b = h % 2
            pg.dma("sp", stage[b][:, 0:1024], W["nab"][:, h, :], [], [f"stg{b}"], key=f"stg{b}")
            pg.copy("pool" if b else "dve", nab[:, h, :], stage[b][:, 0:1024], [f"stg{b}"], [f"nab{h}"])
        if fresh:
            pg.dma("sp", qsel[:], W["qsel"], [], ["qsel"], key="qsel")
        for p in range(4):
            pg.dma("sp", kaT[:, p, :], g.kaT[p, :, :], [], [f"kaT{p}"], key=f"kaT{p}")
        pg.dma("sp", va[:], g.va.rearrange("(t p) c -> p t c", p=128), [], ["va"], key="va")

        nq = sc.nqt
        for j in range(nq):
            qb = j % 2
            q0 = KH + 128 * j - 64
            pg.dma("sp", qa[qb][:], g.qaT[:, :, q0:q0 + 128].rearrange("k p t -> p k t"), [], [f"qa{qb}"],
                   key=f"qa{qb}")
            pg.dma("sp", pen[qb][:], g.pen[j, :, :], [], [f"pen{qb}"], key=f"pen{qb}")
            k0 = 128 * j

            def s1(h):
                sbuf = h % 2
                half = h % 2
                pr = h // 2
                for n2 in range(2):
                    o = pS[sbuf][:, n2 * 512:(n2 + 1) * 512]
                    pg.mm(o, qa[qb][half * 64:(half + 1) * 64, pr, :],
                          kaT[half * 64:(half + 1) * 64, pr, k0 + n2 * 512:k0 + (n2 + 1) * 512], True, False,
                          [f"qa{qb}", f"kaT{pr}"], [f"pS{sbuf}_{n2}"])
                    pg.mm(o, C.identB[:], nab[:, h, n2 * 512:(n2 + 1) * 512], False, False,
                          ["identB", f"nab{h}"], [f"pS{sbuf}_{n2}"])
                    pg.mm(o, qsel[0:2, :], pen[qb][0:2, n2 * 512:(n2 + 1) * 512], False, True,
                          ["qsel", f"pen{qb}"], [f"pS{sbuf}_{n2}"])

            def s2(h):
                sbuf = h % 2
                rd = [f"pS{sbuf}_0", f"pS{sbuf}_1"]
                pg.add("dve", lambda e, o=mx[sbuf][:, 0:1], i_=pS[sbuf][:, :]: e.reduce_max(
                    out=o, in_=i_, axis=AX.X), rd, [f"mxa{sbuf}"])
                pg.ts("dve", mx[sbuf][:, 1:2], mx[sbuf][:, 0:1], -1.0, None, ALU.mult, None, [f"mxa{sbuf}"],
                      [f"mx{sbuf}"])
                pg.act(P[sbuf][:, :], pS[sbuf][:, :], AF.Exp, rd + [f"mx{sbuf}"], [f"P{sbuf}", f"rs{h}"],
                       bias=mx[sbuf][:, 1:2], accum_out=rs[:, h:h + 1])

            def s3(h):
                sbuf = h % 2
                for kt in range(8):
                    pg.tr(pPT[sbuf][:, kt, :], P[sbuf][:, kt * 128:(kt + 1) * 128], C.identB[:],
                          [f"P{sbuf}", "identB"], [f"pPT{sbuf}"])
                pg.copy("dve" if h % 2 else "act", PT[sbuf][:, :, :], pPT[sbuf][:, :, :], [f"pPT{sbuf}"],
                        [f"PT{sbuf}"])

            def s4(h):
                sbuf = h % 2
                for kt in range(8):
                    pg.mm(pO[:, h * 64:(h + 1) * 64], PT[sbuf][:, kt, :], va[:, j + kt, h * 64:(h + 1) * 64],
                          kt == 0, kt == 7, [f"PT{sbuf}", "va"], ["pO"])

            for step in range(8 + 3):
                if step < 8:
                    s1(step)
                if 0 <= step - 1 < 8:
                    s2(step - 1)
                if 0 <= step - 2 < 8:
                    s3(step - 2)
                if 0 <= step - 3 < 8:
                    s4(step - 3)
            pg.add("dve", lambda e: e.reciprocal(out=rinv[:, :], in_=rs[:, :]), [f"rs{h}" for h in range(8)],
                   ["rinv"])
            ab = j % 2
            pg.tt("dve", ao[ab][:, :].rearrange("p (h d) -> p h d", h=8),
                  pO[:, :].rearrange("p (h d) -> p h d", h=8),
                  rinv[:, :].unsqueeze(2).broadcast_to([128, 8, 64]), ALU.mult, ["pO", "rinv"], [f"ao{ab}"])
            for k in range(4):
                pg.tr(paT[:, k, :], ao[ab][:, k * 128:(k + 1) * 128], C.identB[:], [f"ao{ab}", "identB"], ["paT"])
            pg.copy("act", aoT[ab][:, :, :], paT[:, :, :], ["paT"], [f"aoT{ab}"])
            e0 = 128 * j - 64
            lo, hi = max(e0, 0), min(e0 + 128, sc.T)
            pg.dma("pool", g.catT[0:4, :, lo:hi].rearrange("k p t -> p k t"), aoT[ab][:, :, lo - e0:hi - e0],
                   [f"aoT{ab}"], [], key=f"aoT{ab}")


def phase_D(C, g):
    pg, nc, sc, W = C.pg, C.nc, g.c, C.W
    with ExitStack() as es:
        sb, ps = _pools(C, es)
        ksb = [sb(f"ksb{i}", [96, sc.S], BF16) for i in range(2)]
        vsb = [sb(f"vsb{i}", [128, sc.nts, 65], BF16) for i in range(2)]
        qsb = [sb(f"qsb{i}", [96, sc.T], BF16) for i in range(2)]
        PTs = [sb(f"PTs{i}", [128, 512], BF16) for i in range(4)]
        rrow = sb("rrow", [65, 512], F32)
        rbc = sb("rbc", [64, 512], F32)
        on = [sb(f"on{i}", [64, 512], BF16) for i in range(2)]
        pST = [ps(f"pST{i}", [128, 512]) for i in range(3)]
        pOT = [ps(f"pOT{i}", [128, 512]) for i in range(2)]
        pR = ps("pR", [64, 512])
        nqb = sc.T // 512
        it = 0
        for h in range(8):
            hb = h % 2
            pg.dma("sp", ksb[hb][:, :], g.Ks[h, 0:96, :], [], [f"ksb{hb}"], key=f"ksb{hb}")
            pg.dma("sp", vsb[hb][:, :, :], g.Vs[h, :, :, :], [], [f"vsb{hb}"], key=f"vsb{hb}")
            pg.dma("sp", qsb[hb][0:96, :], g.qaug[h, :, KH:KH + sc.T], [], [f"qsb{hb}"], key=f"qsb{hb}")
            for qb in range(nqb):
                ob = (h * nqb + qb) % 2
                cols = slice(qb * 512, (qb + 1) * 512)

                def st(kt, it_):
                    sbuf = it_ % 3
                    pg.mm(pST[sbuf][:, :], ksb[hb][0:96, kt * 128:(kt + 1) * 128], qsb[hb][0:96, cols], True, True,
                          [f"ksb{hb}", f"qsb{hb}"], [f"pST{sbuf}"])

                def ex_pv(kt, it_):
                    sbuf = it_ % 3
                    pbuf = it_ % 4
                    pg.act(PTs[pbuf][:, :], pST[sbuf][:, :], AF.Exp, [f"pST{sbuf}"], [f"PTs{pbuf}"], scale=MLA_SCALE)
                    pg.mm(pOT[ob][0:65, :], vsb[hb][:, kt, 0:65], PTs[pbuf][:, :], kt == 0, kt == sc.nts - 1,
                          [f"vsb{hb}", f"PTs{pbuf}"], [f"pOT{ob}"])

                base = it
                for kt in range(sc.nts + 2):
                    if kt < sc.nts:
                        st(kt, base + kt)
                    if kt - 2 >= 0:
                        ex_pv(kt - 2, base + kt - 2)
                it = base + sc.nts
                pg.add("dve", lambda e, o=rrow[64:65, :], i_=pOT[ob][64:65, :]: e.reciprocal(out=o, in_=i_),
                       [f"pOT{ob}"], ["rrow"])
                pg.mm(pR[0:64, :], C.ones[64:65, 0:64], rrow[64:65, :], True, True, ["ones", "rrow"], ["pR"])
                pg.copy("act", rbc[:, :], pR[0:64, :], ["pR"], ["rbc"])
                pg.tt("dve", on[ob][:, :], pOT[ob][0:64, :], rbc[:, :], ALU.mult, [f"pOT{ob}", "rbc"], [f"on{ob}"])
                pg.dma("pool", g.catT[4 + h // 2, (h % 2) * 64:(h % 2) * 64 + 64, cols], on[ob][:, :], [f"on{ob}"],
                       [], key=f"on{ob}")


def layer_norm_tile(C, r, xn, tmp, gt, bt, st6, mv, lnv, rstd, nmr, vcol, tag, rtag, xtag, ttag=None,
                    defer=False):
    pg = C.pg
    if ttag is None:
        ttag = tag + "tmp"
    for hlf in range(2):
        pg.add("dve", lambda e, o=st6[:, hlf, :], i_=r[:, hlf * 512:(hlf + 1) * 512]: e.bn_stats(out=o, in_=i_),
               [rtag], [tag + f"st{hlf}"])
    pg.add("dve", lambda e: e.bn_aggr(out=mv[:, :], in_=st6[:, :, :].rearrange("p a b -> p (a b)")),
           [tag + "st0", tag + "st1"], [tag + "mv"])
    pg.act(lnv[:, 0:1], mv[:, 1:2], AF.Ln, [tag + "mv"], [tag + "lnv"], bias=C.epsb[:, 0:1])
    pg.act(rstd[:, 0:1], lnv[:, 0:1], AF.Exp, [tag + "lnv"], [tag + "rstd0"], scale=-0.5)
    if vcol is not None:
        pg.tt("dve", rstd[:, 1:2], rstd[:, 0:1], vcol, ALU.mult, [tag + "rstd0", "valid"], [tag + "rstd"])
        rs_ap = rstd[:, 1:2]
    else:
        pg.copy("dve", rstd[:, 1:2], rstd[:, 0:1], [tag + "rstd0"], [tag + "rstd"])
        rs_ap = rstd[:, 1:2]
    pg.stt("dve", nmr[:, 0:1], mv[:, 0:1], -1.0, rs_ap, ALU.mult, ALU.mult, [tag + "mv", tag + "rstd"],
           [tag + "nmr"])
    pg.act(tmp[:, :], r[:, :], AF.Identity, [rtag, tag + "rstd", tag + "nmr"], [ttag + "a", ttag + "b"], scale=rs_ap,
           bias=nmr[:, 0:1])
    if not defer:
        layer_norm_back(C, xn, tmp, gt, bt, vcol, tag, xtag, ttag)


def layer_norm_back(C, xn, tmp, gt, bt, vcol, tag, xtag, ttag):
    pg = C.pg
    if vcol is not None:
        pg.tt("pool", tmp[:, 0:384], tmp[:, 0:384], gt[:, 0:384], ALU.mult, [ttag + "a", tag + "g"], [ttag + "a"])
        pg.tt("dve", tmp[:, 384:1024], tmp[:, 384:1024], gt[:, 384:1024], ALU.mult, [ttag + "b", tag + "g"],
              [ttag + "b"])
        pg.stt("dve", xn[:, :], bt[:, :], vcol, tmp[:, :], ALU.mult, ALU.add,
               [ttag + "a", ttag + "b", tag + "b", "valid"], [xtag])
    else:
        pg.tt("pool", tmp[:, :], tmp[:, :], gt[:, :], ALU.mult, [ttag + "a", ttag + "b", tag + "g"],
              [ttag + "a", ttag + "b"])
        pg.tt("pool", xn[:, :], tmp[:, :], bt[:, :], ALU.add, [ttag + "a", ttag + "b", tag + "b"], [xtag])


def phase_E(C, g):
    pg, nc, sc, W = C.pg, C.nc, g.c, C.W
    with ExitStack() as es:
        sb, ps = _pools(C, es)
        wo, fresh = shared_sb(C, "wo_sb", [128, 8, 1024], BF16)
        gt, _ = shared_sb(C, "gt", [128, 1024], F32)
        bt, _ = shared_sb(C, "bt", [128, 1024], F32)
        stage = [sb(f"stg{i}", [128, 2048], F32) for i in range(2)]
        valid = sb("valid", [128, sc.nt], F32)
        C.epsb = sb("epsb", [128, 1], F32)
        cat = [sb(f"cat{i}", [128, 8, 512], BF16) for i in range(2)]
        xs = [sb(f"xs{i}", [128, 1024], F32) for i in range(2)]
        r = [sb(f"r{i}", [128, 1024], F32) for i in range(2)]
        tmp = [sb(f"tmp{i}", [128, 1024], F32) for i in range(2)]
        xn = [sb(f"xn{i}", [128, 1024], F32) for i in range(2)]
        xnT = [sb(f"xnT{i}", [128, 8, 128], BF16) for i in range(2)]
        st6 = sb("st6", [128, 2, 6], F32)
        mv = sb("mv", [128, 2], F32)
        lnv = sb("lnv", [128, 1], F32)
        rstd = sb("rstd", [128, 2], F32)
        nmr = sb("nmr", [128, 1], F32)
        pmix = [ps(f"pmix{i}", [128, 1024]) for i in range(2)]
        psT = ps("psT", [128, 8, 128])
        pg.memset("dve", C.epsb[:], EPS, ["epsb"])
        if fresh:
            load_cast(C, wo, W["wo0"], 8, 1024, stage, "wo")
            pg.dma("sp", gt[:], W["l0_ln1g"], [], ["Eg"], key="Eg")
            pg.dma("sp", bt[:], W["l0_ln1b"], [], ["Eb"], key="Eb")
        pg.dma("sp", valid[:], g.valid, [], ["valid"], key="valid")
        def front(t):
            grp, j = t // 4, t % 4
            cb = grp % 2
            b = t % 2
            if j == 0:
                pg.dma("sp", cat[cb][:], g.catT[:, :, grp * 512:(grp + 1) * 512].rearrange("k p t -> p k t"), [],
                       [f"cat{cb}"], key=f"cat{cb}")
            pg.dma("sp", xs[b][:], g.xext[KH + t * 128:KH + (t + 1) * 128, :], [], [f"xs{b}"], key=f"xs{b}")
            for hlf in range(2):
                for k in range(8):
                    pg.mm(pmix[b][:, hlf * 512:(hlf + 1) * 512], cat[cb][:, k, j * 128:(j + 1) * 128],
                          wo[:, k, hlf * 512:(hlf + 1) * 512], k == 0, k == 7, [f"cat{cb}", f"wo{k}"],
                          [f"pmix{b}_{hlf}"])
                pg.stt("dve", r[b][:, hlf * 512:(hlf + 1) * 512], xs[b][:, hlf * 512:(hlf + 1) * 512], ALPHA,
                       pmix[b][:, hlf * 512:(hlf + 1) * 512], ALU.mult, ALU.add, [f"xs{b}", f"pmix{b}_{hlf}"],
                       [f"r{b}"])
            layer_norm_tile(C, r[b], xn[b], tmp[b], gt, bt, st6, mv, lnv, rstd, nmr, valid[:, t:t + 1], "E",
                            f"r{b}", f"xn{b}", ttag=f"tmp{b}", defer=True)

        def back(t):
            b = t % 2
            layer_norm_back(C, xn[b], tmp[b], gt, bt, valid[:, t:t + 1], "E", f"xn{b}", f"tmp{b}")
            pg.dma("pool", g.xmid[t * 128:(t + 1) * 128, :], xn[b][:], [f"xn{b}"], [], key=f"xn{b}")
            for k in range(8):
                pg.tr(psT[:, k, :], xn[b][:, k * 128:(k + 1) * 128], C.identF[:], [f"xn{b}", "identF"],
                      ["psTa" if k < 4 else "psTb"])
            pg.copy("act", xnT[b][:, 0:4, :], psT[:, 0:4, :], ["psTa"], [f"xnT{b}"])
            pg.copy("dve", xnT[b][:, 4:8, :], psT[:, 4:8, :], ["psTb"], [f"xnT{b}"])
            pg.dma("pool", g.xmidT[:, :, 1 + t * 128:1 + (t + 1) * 128].rearrange("k p t -> p k t"), xnT[b][:],
                   [f"xnT{b}"], [], key=f"xnT{b}")

        for t in range(sc.nt + 1):
            if t < sc.nt:
                front(t)
            if t >= 1:
                back(t - 1)


def phase_FFN(C, g, l):
    pg, nc, sc, W = C.pg, C.nc, g.c, C.W
    p = f"l{l}_"
    if l == 0:
        xT_src, x_src, dst, t_lo, ntok = g.xmidT, g.xmid, g.x1, 0, sc.T
    else:
        xT_src, x_src, dst, t_lo, ntok = g.xn1T, g.xn1, g.out, HALO, sc.NQ
    with ExitStack() as es:
        sb, ps = _pools(C, es)
        wup, fresh = shared_sb(C, "wup", [128, 8, 2 * DFF], BF16)
        wdn, _ = shared_sb(C, "wdn", [128, 22, 1024], BF16)
        cw, _ = shared_sb(C, "cw", [128, 44, 3], F32)
        cbt, _ = shared_sb(C, "cbt", [128, 44], F32)
        gt, _ = shared_sb(C, "gt", [128, 1024], F32)
        bt, _ = shared_sb(C, "bt", [128, 1024], F32)
        stage = [sb(f"stg{i}", [128, 512], F32) for i in range(2)]
        if fresh:
            load_cast(C, wup, W[p + "wup"], 8, 2 * DFF, stage, "wup", cwmax=512)
            load_cast(C, wdn, W[p + "wdn"], 22, 1024, stage, "wdn", cwmax=512)
        C.epsb = sb("epsb", [128, 1], F32)
        xTb = [sb(f"xTb{i}", [128, 8, NB + 2], BF16) for i in range(2)]
        uT = [sb(f"uT{i}", [128, 22, NB], BF16) for i in range(2)]
        cg = [sb(f"cg{i}", [128, NB], F32) for i in range(2)]
        cv = [sb(f"cv{i}", [128, NB], F32) for i in range(2)]
        gg = [sb(f"gg{i}", [128, NB], F32) for i in range(2)]
        xs = [sb(f"xs{i}", [128, 1024], F32) for i in range(2)]
        r = sb("r", [128, 1024], F32)
        tmp = sb("tmp", [128, 1024], F32)
        xo = [sb(f"xo{i}", [128, 1024], F32) for i in range(2)]
        st6 = sb("st6", [128, 2, 6], F32)
        mv = sb("mv", [128, 2], F32)
        lnv = sb("lnv", [128, 1], F32)
        rstd = sb("rstd", [128, 2], F32)
        nmr = sb("nmr", [128, 1], F32)
        ph = [ps(f"ph{i}", [128, 512]) for i in range(4)]
        py = [ps(f"py{i}", [128, 1024]) for i in range(2)]
        pg.memset("dve", C.epsb[:], EPS, ["epsb"])
        if fresh:
            pg.dma("sp", cw[:], W[p + "cw"], [], ["cw"], key="cw")
            pg.dma("sp", cbt[:], W[p + "cb"], [], ["cb"], key="cb")
            pg.dma("sp", gt[:], W[p + "ln2g"], [], ["Fg"], key="Fg")
            pg.dma("sp", bt[:], W[p + "ln2b"], [], ["Fb"], key="Fb")
        wup_r = [f"wup{k}" for k in range(8)]
        NW = NB + 2
        nblk = ntok // NB
        NT = NB // 128

        def down_steps(blk):
            tb = t_lo + blk * NB
            ub = blk % 2
            seq = []
            for j in range(NT):
                tok0 = tb + j * 128
                b = (blk * NT + j) % 2
                pj = py[j % 2]
                for hlf in range(2):
                    for c in range(22):
                        def mmf(j=j, hlf=hlf, c=c, pj=pj, b=b, tok0=tok0):
                            if hlf == 0 and c == 0:
                                pg.dma("sp", xs[b][:], x_src[tok0:tok0 + 128, :], [], [f"xs{b}"], key=f"xs{b}")
                            pg.mm(pj[:, hlf * 512:(hlf + 1) * 512], uT[ub][:, c, j * 128:(j + 1) * 128],
                                  wdn[:, c, hlf * 512:(hlf + 1) * 512], c == 0, c == 21, [f"uT{ub}_{c}", f"wdn{c}"],
                                  [f"py{j % 2}_{hlf}"])
                            if c == 21:
                                pg.stt("dve", r[:, hlf * 512:(hlf + 1) * 512], xs[b][:, hlf * 512:(hlf + 1) * 512],
                                       ALPHA, pj[:, hlf * 512:(hlf + 1) * 512], ALU.mult, ALU.add,
                                       [f"xs{b}", f"py{j % 2}_{hlf}"], ["r"])
                                if hlf == 1:
                                    layer_norm_tile(C, r, xo[b], tmp, gt, bt, st6, mv, lnv, rstd, nmr, None, "F",
                                                    "r", f"xo{b}")
                                    o0 = tok0 - t_lo
                                    pg.dma("pool", dst[o0:o0 + 128, :], xo[b][:], [f"xo{b}"], [], key=f"xo{b}")
                        seq.append(mmf)
            per = (len(seq) + 21) // 22
            return [seq[i * per:(i + 1) * per] for i in range(22)]

        for blk in range(nblk + 1):
            prev = down_steps(blk - 1) if blk >= 1 else None
            if blk < nblk:
                tb = t_lo + blk * NB
                xb = blk % 2
                ub = blk % 2
                pg.dma("sp", xTb[xb][:], xT_src[:, :, tb:tb + NW].rearrange("k p t -> p k t"), [], [f"xTb{xb}"],
                       key=f"xTb{xb}")
                if l == 0 and blk == 0:
                    pg.memset("pool", xTb[xb][:, :, 0:1], 0.0, [f"xTb{xb}"])
                if l == 0 and blk == nblk - 1:
                    pg.memset("pool", xTb[xb][:, :, NW - 1:NW], 0.0, [f"xTb{xb}"])
            for c in range(22):
                i2 = c % 2
                hg, hv = ph[2 * i2], ph[2 * i2 + 1]
                if blk < nblk:
                    for (hp, col0, tg) in ((hg, c * 128, "g"), (hv, DFF + c * 128, "v")):
                        for k in range(8):
                            pg.mm(hp[:, 0:NW], wup[:, k, col0:col0 + 128], xTb[xb][:, k, :], k == 0, k == 7,
                                  [f"xTb{xb}", wup_r[k]], [f"ph{i2}{tg}"])
                if prev is not None:
                    for f in prev[c]:
                        f()
                if blk < nblk:
                    for (hp, ch, dstt, tg) in ((hg, c, cg[i2], "g"), (hv, 22 + c, cv[i2], "v")):
                        pg.act(dstt[:, :], hp[:, 1:NB + 1], AF.Identity, [f"ph{i2}{tg}", "cw", "cb"], [f"c{tg}{i2}"],
                               scale=cw[:, ch, 1:2], bias=cbt[:, ch:ch + 1])
                        pg.stt("dve", dstt[:, :], hp[:, 0:NB], cw[:, ch, 0:1], dstt[:, :], ALU.mult, ALU.add,
                               [f"ph{i2}{tg}", "cw", f"c{tg}{i2}"], [f"c{tg}{i2}"])
                        pg.stt("dve", dstt[:, :], hp[:, 2:NB + 2], cw[:, ch, 2:3], dstt[:, :], ALU.mult, ALU.add,
                               [f"ph{i2}{tg}", "cw", f"c{tg}{i2}"], [f"c{tg}{i2}"])
                    pg.act(gg[i2][:, :], cg[i2][:, :], AF.Gelu, [f"cg{i2}"], [f"gg{i2}"])
                    pg.tt("pool", uT[ub][:, c, :], gg[i2][:, :], cv[i2][:, :], ALU.mult, [f"gg{i2}", f"cv{i2}"],
                          [f"uT{ub}_{c}"])


def phase_G(C, g):
    pg, nc, sc, W = C.pg, C.nc, g.c, C.W
    with ExitStack() as es:
        sb, ps = _pools(C, es)
        w1, fresh = shared_sb(C, "w1_sb", [128, 8, 1856], BF16)
        stage = [sb(f"stg{i}", [128, 2048], F32) for i in range(2)]
        l1cs = sb("l1cs", [128, sc.nt, 32], F32)
        xs = [sb(f"xs{i}", [128, 1024], F32) for i in range(2)]
        xT = [sb(f"xT{i}", [128, 8, 128], BF16) for i in range(2)]
        m1 = sb("m1", [128, 20, 16], F32)
        m2 = sb("m2", [128, 20, 16], F32)
        m3 = sb("m3", [128, 20, 16], F32)
        qk = [sb(f"qk{i}", [128, 20, 64], BF16) for i in range(2)]
        vst = [sb(f"vst{i}", [128, 256], BF16) for i in range(2)]
        qkT = [sb(f"qkT{i}", [128, 10, 128], BF16) for i in range(2)]
        psT = ps("psT", [128, 8, 128])
        pq = ps("pq", [128, 2048])
        pqT = ps("pqT", [128, 10, 128], BF16)
        if fresh:
            load_cast(C, w1, W["w1in"], 8, 1856, stage, "w1")
        pg.dma("sp", l1cs[:], g.l1cs, [], ["l1cs"], key="l1cs")
        w1_r = [f"w1{k}" for k in range(8)]
        for t in range(sc.nt):
            b = t % 2
            load_x_T(C, g.x1[t * 128:(t + 1) * 128, :], xs, xT, psT, b, "xT")
            for (c0, cw_, bank) in ((0, 512, 0), (512, 512, 1), (1024, 512, 2), (1536, 320, 3)):
                for k in range(8):
                    pg.mm(pq[:, bank * 512:bank * 512 + cw_], xT[b][:, k, :], w1[:, k, c0:c0 + cw_], k == 0, k == 7,
                          [f"xT{b}a" if k < 4 else f"xT{b}b", w1_r[k]], [f"pq{bank}"])
            qkv = pq[:, 0:1280].rearrange("p (h d) -> p h d", h=20)
            sw = pq[:, 1536:1856].rearrange("p (h d) -> p h d", h=20)
            cosb = l1cs[:, t, 0:16].unsqueeze(1).broadcast_to([128, 20, 16])
            sinb = l1cs[:, t, 16:32].unsqueeze(1).broadcast_to([128, 20, 16])
            pg.tt("dve", m1[:, :, :], qkv[:, :, 0:16], cosb, ALU.mult, ["pq0", "pq1", "pq2", "l1cs"], ["Gm1"])
            pg.tt("dve", m2[:, :, :], sw, sinb, ALU.mult, ["pq3", "l1cs"], ["Gm2"])
            pg.tt("pool", m3[:, :, :], m1[:, :, :], m2[:, :, :], ALU.add, ["Gm1", "Gm2"], ["Gm3"])
            pg.copy("act", qk[b][:, 0:16, 0:16], m3[:, 0:16, :], ["Gm3"], [f"qk{b}a"], scale=0.125)
            pg.copy("pool", qk[b][:, 16:20, 0:16], m3[:, 16:20, :], ["Gm3"], [f"qk{b}b"])
            pg.copy("act", qk[b][:, 0:16, 16:64], qkv[:, 0:16, 16:64], ["pq0", "pq1"], [f"qk{b}c"], scale=0.125)
            pg.copy("dve", qk[b][:, 16:20, 16:64], qkv[:, 16:20, 16:64], ["pq2"], [f"qk{b}d"])
            pg.copy("dve", vst[b][:, :], pq[:, 1280:1536], ["pq2"], [f"vst{b}"])
            pg.dma("pool", g.v1[t * 128:(t + 1) * 128, :], vst[b][:], [f"vst{b}"], [], key=f"vst{b}")
            qrd = [f"qk{b}a", f"qk{b}b", f"qk{b}c", f"qk{b}d", "identB"]
            for pr in range(10):
                pg.tr(pqT[:, pr, :], qk[b][:, 2 * pr:2 * pr + 2, :].rearrange("p h d -> p (h d)"), C.identB[:], qrd,
                      ["pqT"])
            pg.copy("act", qkT[b][:, 0:5, :], pqT[:, 0:5, :], ["pqT"], [f"qkT{b}"])
            pg.copy("dve", qkT[b][:, 5:10, :], pqT[:, 5:10, :], ["pqT"], [f"qkT{b}"])
            pg.dma("pool", g.q1T[:, :, t * 128:(t + 1) * 128].rearrange("k p t -> p k t"), qkT[b][:, 0:8, :],
                   [f"qkT{b}"], [], key=f"qkT{b}q")
            pg.dma("pool", g.k1T[:, :, t * 128:(t + 1) * 128].rearrange("k p t -> p k t"), qkT[b][:, 8:10, :],
                   [f"qkT{b}"], [], key=f"qkT{b}k")


def phase_H(C, g):
    pg, nc, sc, W = C.pg, C.nc, g.c, C.W
    with ExitStack() as es:
        sb, ps = _pools(C, es)
        wo, fresh = shared_sb(C, "wo_sb", [128, 8, 1024], BF16)
        gt, _ = shared_sb(C, "gt", [128, 1024], F32)
        bt, _ = shared_sb(C, "bt", [128, 1024], F32)
        sinks, _ = shared_sb(C, "sinks", [128, 16], F32)
        band, _ = shared_sb(C, "band", [128, 384], BF16)
        stage = [sb(f"stg{i}", [128, 2048], F32) for i in range(2)]
        valid = sb("valid", [128, sc.nt], F32)
        kpen = sb("kpen", [1, sc.T], BF16)
        C.epsb = sb("epsb", [128, 1], F32)
        k1T = sb("k1T", [128, 2, sc.T], BF16)
        v1 = sb("v1", [128, sc.nt, 256], BF16)
        q1 = [sb(f"q1{i}", [128, 8, 128], BF16) for i in range(2)]
        P = [sb(f"P{i}", [128, 384], BF16) for i in range(2)]
        PT = [sb(f"PT{i}", [128, 3, 128], BF16) for i in range(2)]
        mx = [sb(f"mx{i}", [128, 2], F32) for i in range(2)]
        rs = sb("rs", [128, 16], F32)
        es_ = sb("es_", [128, 16], F32)
        rinv = sb("rinv", [128, 16], F32)
        at = sb("at", [128, 1024], BF16)
        atT = sb("atT", [128, 8, 128], BF16)
        xs = [sb(f"xs{i}", [128, 1024], F32) for i in range(2)]
        r = sb("r", [128, 1024], F32)
        tmp = [sb(f"tmp{i}", [128, 1024], F32) for i in range(2)]
        xn = [sb(f"xn{i}", [128, 1024], F32) for i in range(2)]
        xnT = [sb(f"xnT{i}", [128, 8, 128], BF16) for i in range(2)]
        st6 = sb("st6", [128, 2, 6], F32)
        mv = sb("mv", [128, 2], F32)
        lnv = sb("lnv", [128, 1], F32)
        rstd = sb("rstd", [128, 2], F32)
        nmr = sb("nmr", [128, 1], F32)
        pS = [ps(f"pS{i}", [128, 512]) for i in range(2)]
        pPT = [ps(f"pPT{i}", [128, 8, 128], BF16) for i in range(2)]
        pO = ps("pO", [128, 1024])
        pmix = ps("pmix", [128, 1024])
        pg.memset("dve", C.epsb[:], EPS, ["epsb"])
        if fresh:
            load_cast(C, wo, W["wo1"], 8, 1024, stage, "wo")
            pg.dma("sp", gt[:], W["l1_ln1g"], [], ["Hg"], key="Hg")
            pg.dma("sp", bt[:], W["l1_ln1b"], [], ["Hb"], key="Hb")
            pg.dma("sp", sinks[:], W["sinks"], [], ["sinks"], key="sinks")
            pg.dma("sp", band[:], W["band"], [], ["band"], key="band")
        pg.dma("sp", valid[:], g.valid, [], ["valid"], key="valid")
        pg.dma("sp", kpen[:], g.kpen, [], ["kpen"], key="kpen")
        pg.dma("sp", k1T[:], g.k1T.rearrange("k p t -> p k t"), [], ["k1T"], key="k1T")
        pg.dma("sp", v1[:], g.v1.rearrange("(t p) c -> p t c", p=128), [], ["v1"], key="v1")
        def front(i):
            b = i % 2
            pg.dma("sp", q1[b][:], g.q1T[:, :, i * 128:(i + 1) * 128].rearrange("k p t -> p k t"), [], [f"q1{b}"],
                   key=f"q1{b}")
            pg.dma("sp", xs[b][:], g.x1[i * 128:(i + 1) * 128, :], [], [f"xs{b}"], key=f"xs{b}")
            k0 = (i - 1) * 128

            def s1(hd):
                sbuf = hd % 2
                half = hd % 2
                gk = QPERM[hd] // 4
                o = pS[sbuf][:, 0:384]
                pg.mm(o, q1[b][half * 64:(half + 1) * 64, hd // 2, :],
                      k1T[half * 64:(half + 1) * 64, gk // 2, k0:k0 + 384], True, False, [f"q1{b}", "k1T"],
                      [f"pS{sbuf}"])
                pg.mm(o, C.identB[:], band[:, :], False, False, ["identB", "band"], [f"pS{sbuf}"])
                pg.mm(o, C.onesB[0:1, 0:128], kpen[0:1, k0:k0 + 384], False, True, ["onesB", "kpen"], [f"pS{sbuf}"])

            def s2(hd):
                sbuf = hd % 2
                pg.add("dve", lambda e, o=mx[sbuf][:, 0:1], i_=pS[sbuf][:, 0:384]: e.reduce_max(
                    out=o, in_=i_, axis=AX.X), [f"pS{sbuf}"], [f"mxa{sbuf}"])
                pg.ts("dve", mx[sbuf][:, 1:2], mx[sbuf][:, 0:1], sinks[:, hd:hd + 1], -1.0, ALU.max, ALU.mult,
                      [f"mxa{sbuf}", "sinks"], [f"mx{sbuf}"])
                pg.act(P[sbuf][:, :], pS[sbuf][:, 0:384], AF.Exp, [f"pS{sbuf}", f"mx{sbuf}"], [f"P{sbuf}", f"rs{hd}"],
                       bias=mx[sbuf][:, 1:2], accum_out=rs[:, hd:hd + 1])
                pg.act(es_[:, hd:hd + 1], sinks[:, hd:hd + 1], AF.Exp, ["sinks", f"mx{sbuf}"], [f"es{hd}"],
                       bias=mx[sbuf][:, 1:2])

            def s3(hd):
                sbuf = hd % 2
                for kt in range(3):
                    pg.tr(pPT[sbuf][:, kt, :], P[sbuf][:, kt * 128:(kt + 1) * 128], C.identB[:],
                          [f"P{sbuf}", "identB"], [f"pPT{sbuf}"])
                pg.copy("dve" if hd % 2 else "act", PT[sbuf][:, :, :], pPT[sbuf][:, 0:3, :], [f"pPT{sbuf}"],
                        [f"PT{sbuf}"])

            def s4(hd):
                sbuf = hd % 2
                gk = QPERM[hd] // 4
                for kt in range(3):
                    pg.mm(pO[:, hd * 64:(hd + 1) * 64], PT[sbuf][:, kt, :], v1[:, i - 1 + kt, gk * 64:(gk + 1) * 64],
                          kt == 0, kt == 2, [f"PT{sbuf}", "v1"], [f"pO{hd // 8}"])

            for step in range(16 + 3):
                if step < 16:
                    s1(step)
                if 0 <= step - 1 < 16:
                    s2(step - 1)
                if 0 <= step - 2 < 16:
                    s3(step - 2)
                if 0 <= step - 3 < 16:
                    s4(step - 3)
            pg.tt("dve", rinv[:, :], rs[:, :], es_[:, :], ALU.add, [f"rs{h}" for h in range(16)] +
                  [f"es{h}" for h in range(16)], ["rinv0"])
            pg.add("dve", lambda e: e.reciprocal(out=rinv[:, :], in_=rinv[:, :]), ["rinv0"], ["rinv"])
            pg.tt("dve", at[:, :].rearrange("p (h d) -> p h d", h=16), pO[:, :].rearrange("p (h d) -> p h d", h=16),
                  rinv[:, :].unsqueeze(2).broadcast_to([128, 16, 64]), ALU.mult, ["pO0", "pO1", "rinv"], ["at"])
            for k in range(8):
                pg.tr(pPT[0][:, k, :], at[:, k * 128:(k + 1) * 128], C.identB[:], ["at", "identB"], ["pPT0"])
            pg.copy("act", atT[:, :, :], pPT[0][:, :, :], ["pPT0"], ["atT"])
            for hlf in range(2):
                for k in range(8):
                    pg.mm(pmix[:, hlf * 512:(hlf + 1) * 512], atT[:, k, :], wo[:, k, hlf * 512:(hlf + 1) * 512],
                          k == 0, k == 7, ["atT", f"wo{k}"], [f"pmix{hlf}"])
                pg.stt("dve", r[:, hlf * 512:(hlf + 1) * 512], xs[b][:, hlf * 512:(hlf + 1) * 512], ALPHA,
                       pmix[:, hlf * 512:(hlf + 1) * 512], ALU.mult, ALU.add, [f"xs{b}", f"pmix{hlf}"], ["r"])
            layer_norm_tile(C, r, xn[b], tmp[b], gt, bt, st6, mv, lnv, rstd, nmr, valid[:, i:i + 1], "H", "r",
                            f"xn{b}", ttag=f"tmp{b}", defer=True)

        def back(i):
            b = i % 2
            layer_norm_back(C, xn[b], tmp[b], gt, bt, valid[:, i:i + 1], "H", f"xn{b}", f"tmp{b}")
            pg.dma("pool", g.xn1[i * 128:(i + 1) * 128, :], xn[b][:], [f"xn{b}"], [], key=f"xn{b}")
            psT = pO[:, :].rearrange("p (k t) -> p k t", k=8)
            for k in range(8):
                pg.tr(psT[:, k, :], xn[b][:, k * 128:(k + 1) * 128], C.identF[:], [f"xn{b}", "identF"],
                      ["pO0" if k < 4 else "pO1"])
            pg.copy("act", xnT[b][:, 0:4, :], psT[:, 0:4, :], ["pO0"], [f"xnT{b}"])
            pg.copy("dve", xnT[b][:, 4:8, :], psT[:, 4:8, :], ["pO1"], [f"xnT{b}"])
            pg.dma("pool", g.xn1T[:, :, 1 + i * 128:1 + (i + 1) * 128].rearrange("k p t -> p k t"), xnT[b][:],
                   [f"xnT{b}"], [], key=f"xnT{b}")


        for i in range(1, sc.nt):
            if i < sc.nt - 1:
                front(i)
            if i >= 2:
                back(i - 1)


def make_in_maps(cfg, inputs, n_cores=8):
    sh = prep_shared(inputs)
    P, Sg = cfg
    maps = []
    for c in range(n_cores):
        m = dict(sh)
        m.update(prep_seg(P, inputs["x_prompt"][c // 2], (c % 2) * P.NQ))
        m.update(prep_seg(Sg, inputs["x_sample"][0], c * Sg.NQ))
        maps.append(m)
    return maps


def assemble(cfg, res, inputs):
    P, Sg = cfg
    yp = np.zeros(inputs["x_prompt"].shape, np.float32)
    ys = np.zeros(inputs["x_sample"].shape, np.float32)
    for c in range(8):
        yp[c // 2, (c % 2) * P.NQ:(c % 2 + 1) * P.NQ] = res[c]["P_out"]
        ys[0, c * Sg.NQ:(c + 1) * Sg.NQ] = res[c]["S_out"]
    return yp, ys


def kernel(**inputs):
    inputs = {k: np.asarray(v) for k, v in inputs.items()}
    cfg = FULL_CFG
    nc, _ = build(cfg)
    maps = make_in_maps(cfg, inputs)
    res = run_bass_kernel_spmd(nc, maps, core_ids=list(range(8)))
    return assemble(cfg, res.results, inputs)
```

```python
import numpy as np
import ml_dtypes
from contextlib import ExitStack
import concourse.bass as bass
import concourse.mybir as mybir
from concourse.bass_utils import run_bass_kernel_spmd

F32 = mybir.dt.float32
BF16 = mybir.dt.bfloat16
AF = mybir.ActivationFunctionType
ALU = mybir.AluOpType
AX = mybir.AxisListType

D = 1024
DFF = 2816
NEG = -30000.0
ALPHA = float(4 ** 0.25)
EPS = 1e-5
HALO = 256
KH = 512
NB = 256
QPERM = [0, 4, 1, 5, 2, 6, 3, 7, 8, 12, 9, 13, 10, 14, 11, 15]
MLA_SCALE = float(96 ** -0.5)


class SegCfg:
    def __init__(self, name, S, NQ):
        self.name = name
        self.S = S
        self.NQ = NQ
        self.T = NQ + 2 * HALO
        self.T2 = self.T + 2 * KH
        self.nt = self.T // 128
        self.nt2 = self.T2 // 128
        self.nts = S // 128
        self.nqt = self.nt + 1


FULL_CFG = [SegCfg("P", 8192, 4096), SegCfg("S", 16384, 2048)]


DMAQ = {"pool": "sp"}
LOADQ = "act"
HOIST = True
NOCLEAR = False


class _Op:
    __slots__ = ("eng", "fn", "reads", "writes", "dma", "key", "needs_inc", "count", "deps",
                 "sem", "semval", "waits")


class Prog:
    ENGS = ("pe", "act", "dve", "pool", "sp")

    def __init__(self, nc, es, ndma=38):
        self.nc = nc
        self.sets = []
        for s in range(2):
            eng = {e: es.enter_context(nc.semaphore(f"s{s}_{e}")) for e in self.ENGS}
            dma = [es.enter_context(nc.semaphore(f"s{s}_d{i}")) for i in range(ndma)]
            self.sets.append((eng, dma))
        self.phase = 0
        self.ops = []
        self.first = True

    def add(self, eng, fn, reads=(), writes=(), dma=False, key=None):
        op = _Op()
        op.eng = eng
        op.fn = fn
        op.reads = tuple(reads)
        op.writes = tuple(writes)
        op.dma = dma
        op.key = key
        op.needs_inc = False
        op.count = 0
        self.ops.append(op)
        return op

    def mm(self, out, lhsT, rhs, start, stop, reads, writes):
        self.add("pe", lambda h: h.matmul(out, lhsT, rhs, start=start, stop=stop), reads, writes)

    def tr(self, out, in_, ident, reads, writes):
        self.add("pe", lambda h: h.transpose(out, in_, ident), reads, writes)

    def act(self, out, in_, func, reads, writes, **kw):
        self.add("act", lambda h: h.activation(out=out, in_=in_, func=func, **kw), reads, writes)

    def copy(self, eng, out, in_, reads, writes, scale=None):
        if eng == "act":
            if scale is None:
                self.add("act", lambda h: h.activation(out=out, in_=in_, func=AF.Copy), reads, writes)
            else:
                self.add("act", lambda h: h.activation(out=out, in_=in_, func=AF.Copy, scale=scale),
                         reads, writes)
        else:
            if scale is None:
                self.add(eng, lambda h: h.tensor_copy(out=out, in_=in_), reads, writes)
            else:
                self.add(eng, lambda h: h.tensor_scalar(out=out, in0=in_, scalar1=scale, scalar2=None,
                                                        op0=ALU.mult), reads, writes)

    def tt(self, eng, out, in0, in1, op, reads, writes):
        self.add(eng, lambda h: h.tensor_tensor(out=out, in0=in0, in1=in1, op=op), reads, writes)

    def ts(self, eng, out, in0, s1, s2, op0, op1, reads, writes):
        if op1 is None:
            self.add(eng, lambda h: h.tensor_scalar(out=out, in0=in0, scalar1=s1, scalar2=None, op0=op0),
                     reads, writes)
        else:
            self.add(eng, lambda h: h.tensor_scalar(out=out, in0=in0, scalar1=s1, scalar2=s2, op0=op0,
                                                    op1=op1), reads, writes)

    def stt(self, eng, out, in0, scalar, in1, op0, op1, reads, writes):
        self.add(eng, lambda h: h.scalar_tensor_tensor(out=out, in0=in0, scalar=scalar, in1=in1,
                                                       op0=op0, op1=op1), reads, writes)

    def memset(self, eng, ap, val, writes):
        self.add(eng, lambda h: h.memset(ap, val), (), writes)

    def dma(self, q, out, in_, reads, writes, key):
        q = DMAQ.get(q, q)
        if writes and LOADQ:
            q = LOADQ
        self.add(q, lambda h: h.dma_start(out=out, in_=in_), reads, writes, dma=True, key=key)

    def flush(self):
        nc = self.nc
        engsem, dmapool = self.sets[self.phase % 2]
        oeng, odma = self.sets[(self.phase + 1) % 2]
        ops = self.ops
        if HOIST:
            touch = {}
            keyed = []
            for i, op in enumerate(ops):
                k = float(i)
                if op.dma and op.writes:
                    prev = [touch[b] for b in op.writes if b in touch]
                    if prev:
                        k = max(prev) + 0.5 + 1e-6 * (i % 1000)
                keyed.append((k, i, op))
                for b in op.reads + op.writes:
                    touch[b] = max(touch.get(b, -1.0), k)
            keyed.sort(key=lambda x: (x[0], x[1]))
            ops = [x[2] for x in keyed]
        last_w = {}
        readers = {}
        dmakey = {}
        for op in ops:
            deps = []
            for b in op.reads:
                w = last_w.get(b)
                if w is not None:
                    deps.append((w, 0))
                if b[0] == "p" and b[:3] != "pen":
                    for r in readers.get(b, {}).values():
                        if r.eng != op.eng:
                            deps.append((r, 2))
            for b in op.writes:
                w = last_w.get(b)
                if w is not None:
                    deps.append((w, 1))
                for r in readers.get(b, {}).values():
                    deps.append((r, 2))
            for b in op.writes:
                last_w[b] = op
                readers[b] = {}
            for b in op.reads:
                readers.setdefault(b, {})[(op.eng, op.key if op.dma else None)] = op
            op.deps = []
            for (s, kind) in deps:
                if s is op:
                    continue
                if (not s.dma) and (not op.dma) and s.eng == op.eng:
                    if kind != 0 or op.eng == "pe":
                        continue
                op.deps.append(s)
                if not s.dma:
                    s.needs_inc = True
            if op.dma:
                ent = dmakey.get(op.key)
                if ent is None:
                    assert len(dmakey) < len(dmapool), "out of dma semaphores"
                    ent = [dmapool[len(dmakey)], 0, None]
                    dmakey[op.key] = ent
                if ent[2] is not None:
                    op.deps.append(ent[2])
                ent[2] = op
                ent[1] += 16
                op.sem = ent[0]
                op.semval = ent[1]
        cnt = {e: 0 for e in self.ENGS}
        for op in ops:
            if (not op.dma) and op.needs_inc:
                cnt[op.eng] += 1
                op.count = cnt[op.eng]
        waited = {e: {} for e in self.ENGS}
        for op in ops:
            need = {}
            for s in op.deps:
                if s.dma:
                    sid, sem, val = ("d", id(s.sem)), s.sem, s.semval
                else:
                    sid, sem, val = ("e", s.eng), engsem[s.eng], s.count
                if sid not in need or need[sid][1] < val:
                    need[sid] = (sem, val)
            op.waits = []
            wd = waited[op.eng]
            for sid, (sem, val) in need.items():
                if wd.get(sid, 0) >= val:
                    continue
                wd[sid] = val
                op.waits.append((sem, val))
        per = {e: [] for e in self.ENGS}
        for op in ops:
            per[op.eng].append(op)
        finals = [(ent[0], ent[1]) for ent in dmakey.values()]
        first = self.first

        def mk(e):
            def body(h):
                if e == "sp" and not first and not NOCLEAR:
                    for s_ in list(oeng.values()) + list(odma):
                        h.sem_clear(s_)
                for op in per[e]:
                    for (sem, val) in op.waits:
                        h.wait_ge(sem, val)
                    ins = op.fn(h)
                    if op.dma:
                        ins.then_inc(op.sem, 16)
                    elif op.needs_inc:
                        ins.then_inc(engsem[e], 1)
                if e == "sp":
                    for (sem, val) in finals:
                        h.wait_ge(sem, val)
            return body

        with nc.Block() as block:
            block.tensor(mk("pe"))
            block.scalar(mk("act"))
            block.vector(mk("dve"))
            block.gpsimd(mk("pool"))
            block.sync(mk("sp"))
        self.n_ops = getattr(self, "n_ops", 0) + len(ops)
        self.ops = []
        self.phase += 1
        self.first = False


def _rope_tab(pos, theta, rot_dim):
    half = rot_dim // 2
    inv = (np.float32(1.0) / (np.float32(theta) ** (np.arange(0, rot_dim, 2, dtype=np.float32)
                                                      / np.float32(rot_dim)))).astype(np.float32)
    ang = pos.astype(np.float32)[:, None] * inv[None, :]
    cos = np.cos(ang).astype(np.float32)
    sin = np.sin(ang).astype(np.float32)
    c2 = np.concatenate([cos, cos], axis=1)
    s2 = np.concatenate([-sin, sin], axis=1)
    return c2, s2


def _bc(v, n=128):
    return np.ascontiguousarray(np.broadcast_to(np.asarray(v, np.float32)[None, :], (n, v.shape[0])))


def prep_shared(inp):
    sh = {}
    w = inp["l0_w_in"]
    kr = w[:, 2176:2208]
    sh["wa"] = np.ascontiguousarray(np.concatenate([w[:, 1920:2208], kr[:, 16:32], kr[:, 0:16]], axis=1))
    sh["wb"] = np.ascontiguousarray(w[:, 0:1920])
    wq = inp["l0_w_q_up"].reshape(384, 8, 96)
    sh["wq"] = np.ascontiguousarray(wq.reshape(384, 768))
    wqs = np.concatenate([wq[:, :, 0:64], wq[:, :, 80:96], wq[:, :, 64:80]], axis=2)
    sh["wqs"] = np.ascontiguousarray(wqs.reshape(384, 768))
    wkv = inp["l0_w_kv_up"].reshape(256, 8, 128)
    sh["wkvK"] = np.ascontiguousarray(wkv[:, :, 0:64].reshape(256, 512))
    sh["wkvV"] = np.ascontiguousarray(wkv[:, :, 64:128].reshape(256, 512))
    sh["gq"] = _bc(inp["l0_g_q_norm"])
    sh["gkv"] = _bc(inp["l0_g_kv_norm"])
    sh["wo0"] = np.ascontiguousarray(inp["l0_w_out"])
    rpb = inp["l0_rpb"]
    B = np.full((8, 2, 64, 16, 64), NEG, np.float32)
    c = np.arange(64)
    cs = np.clip(c - 8, 0, 48)
    for i in range(2):
        for wv in range(16):
            dr = wv - i
            if dr < 0 or dr > 14:
                continue
            for cc in range(64):
                kc = cs[cc] + np.arange(16)
                B[:, i, cc, wv, kc] = rpb[:, dr, kc - cc + 15]
    sh["nab"] = np.ascontiguousarray(B.reshape(8, 128, 1024).transpose(1, 0, 2))
    for l in (0, 1):
        p = f"l{l}_"
        sh[p + "ln1g"] = _bc(inp[p + "ln1_g"])
        sh[p + "ln1b"] = _bc(inp[p + "ln1_b"])
        sh[p + "ln2g"] = _bc(inp[p + "ln2_g"])
        sh[p + "ln2b"] = _bc(inp[p + "ln2_b"])
        sh[p + "wup"] = np.ascontiguousarray(inp[p + "ffn_w_up"])
        sh[p + "wdn"] = np.ascontiguousarray(inp[p + "ffn_w_down"])
        cw = inp[p + "ffn_conv_w"]
        sh[p + "cw"] = np.ascontiguousarray(cw.reshape(3, 44, 128).transpose(2, 1, 0))
        sh[p + "cb"] = np.ascontiguousarray(inp[p + "ffn_conv_b"].reshape(44, 128).T)
    w1 = inp["l1_w_in"]
    q = w1[:, 0:1024].reshape(D, 16, 64)[:, QPERM, :]
    k = w1[:, 1024:1280].reshape(D, 4, 64)
    qk = np.concatenate([q, k], axis=1)
    sw = np.concatenate([qk[:, :, 8:16], qk[:, :, 0:8]], axis=2)
    sh["w1in"] = np.ascontiguousarray(np.concatenate([qk.reshape(D, 1280), w1[:, 1280:1536],
                                                      sw.reshape(D, 320)], axis=1))
    sh["wo1"] = np.ascontiguousarray(inp["l1_w_out"].reshape(16, 64, D)[QPERM].reshape(1024, D))
    sh["sinks"] = _bc(inp["l1_sinks"][QPERM])
    sh["ident"] = np.eye(128, dtype=np.float32)
    a = np.arange(128)[:, None]
    j = np.arange(384)[None, :]
    sh["band"] = np.where((j >= a) & (j <= a + 256), 0.0, NEG).astype(ml_dtypes.bfloat16)
    qs = np.zeros((2, 128), np.float32)
    qs[0, 0:64] = 1.0
    qs[1, 64:128] = 1.0
    sh["qsel"] = qs.astype(ml_dtypes.bfloat16)
    return sh


def prep_seg(seg, x_seq, a):
    S, T, T2 = seg.S, seg.T, seg.T2
    n = seg.name
    m = {}
    m[n + "_xseq"] = np.ascontiguousarray(x_seq)
    base2 = a - HALO - KH
    xe = np.zeros((T2, D), np.float32)
    lo, hi = max(base2, 0), min(base2 + T2, S)
    xe[lo - base2:hi - base2] = x_seq[lo:hi]
    m[n + "_xext"] = xe
    pos_e = np.arange(a - HALO, a - HALO + T)
    valid = ((pos_e >= 0) & (pos_e < S)).astype(np.float32)
    m[n + "_valid"] = np.ascontiguousarray(valid.reshape(seg.nt, 128).T)
    c2, s2 = _rope_tab(np.arange(S), 10000.0, 32)
    kcs = np.concatenate([c2, s2], axis=1).reshape(seg.nts, 128, 64).transpose(1, 0, 2)
    m[n + "_kcs"] = np.ascontiguousarray(kcs)
    pos2 = np.clip(np.arange(base2, base2 + T2), 0, S - 1)
    c2, s2 = _rope_tab(pos2, 10000.0, 32)
    m[n + "_qcsT"] = np.ascontiguousarray(np.concatenate([c2.T, s2.T], axis=0))
    c2, s2 = _rope_tab(np.clip(pos_e, 0, S - 1), 500000.0, 16)
    l1cs = np.concatenate([c2, s2], axis=1).reshape(seg.nt, 128, 32).transpose(1, 0, 2)
    m[n + "_l1cs"] = np.ascontiguousarray(l1cs)
    m[n + "_kpen"] = np.where(valid > 0, 0.0, NEG).astype(ml_dtypes.bfloat16)[None, :]
    rows = S // 64
    pen = np.zeros((seg.nqt, 2, 16, 64), np.float32)
    for j in range(seg.nqt):
        r_abs = (a - HALO) // 64 + 2 * j - 1
        for i in range(2):
            rq = r_abs + i
            if rq < 0 or rq >= rows:
                continue
            rs = min(max(rq - 4, 0), rows - 8)
            for wv in range(16):
                krow = r_abs - 7 + wv
                if not (rs <= krow < rs + 8):
                    pen[j, i, wv, :] = NEG
    m[n + "_pen"] = pen.reshape(seg.nqt, 2, 1024).astype(ml_dtypes.bfloat16)
    return m


class Ctx:
    pass


def build(cfg, debug=()):
    nc = bass.Bass("TRN2", target_bir_lowering=False)
    C = Ctx()
    C.nc = nc
    C.debug = set(debug)

    def din(name, shape, dt=F32):
        return nc.dram_tensor(name, list(shape), dt, kind="ExternalInput").ap()

    def dscr(name, shape, dt):
        if name in C.debug:
            return nc.dram_tensor(name, list(shape), dt, kind="ExternalOutput").ap()
        return nc.dram_tensor(name, list(shape), dt).ap()

    W = {}
    W["wa"] = din("wa", [D, 320])
    W["wb"] = din("wb", [D, 1920])
    W["wq"] = din("wq", [384, 768])
    W["wqs"] = din("wqs", [384, 768])
    W["wkvK"] = din("wkvK", [256, 512])
    W["wkvV"] = din("wkvV", [256, 512])
    W["gq"] = din("gq", [128, 384])
    W["gkv"] = din("gkv", [128, 256])
    W["wo0"] = din("wo0", [D, D])
    W["nab"] = din("nab", [128, 8, 1024])
    for l in (0, 1):
        p = f"l{l}_"
        for nm in ("ln1g", "ln1b", "ln2g", "ln2b"):
            W[p + nm] = din(p + nm, [128, D])
        W[p + "wup"] = din(p + "wup", [D, 2 * DFF])
        W[p + "wdn"] = din(p + "wdn", [DFF, D])
        W[p + "cw"] = din(p + "cw", [128, 44, 3])
        W[p + "cb"] = din(p + "cb", [128, 44])
    W["w1in"] = din("w1in", [D, 1856])
    W["wo1"] = din("wo1", [D, D])
    W["sinks"] = din("sinks", [128, 16])
    W["ident"] = din("ident", [128, 128])
    W["band"] = din("band", [128, 384], BF16)
    W["qsel"] = din("qsel", [2, 128], BF16)
    C.W = W

    segs = []
    for sc in cfg:
        n = sc.name
        g = Ctx()
        g.c = sc
        g.xseq = din(n + "_xseq", [sc.S, D])
        g.xext = din(n + "_xext", [sc.T2, D])
        g.valid = din(n + "_valid", [128, sc.nt])
        g.kcs = din(n + "_kcs", [128, sc.nts, 64])
        g.qcsT = din(n + "_qcsT", [64, sc.T2])
        g.l1cs = din(n + "_l1cs", [128, sc.nt, 32])
        g.kpen = din(n + "_kpen", [1, sc.T], BF16)
        g.pen = din(n + "_pen", [sc.nqt, 2, 1024], BF16)
        g.Ks = dscr(n + "_Ks", [8, 97, sc.S], BF16)
        g.Vs = dscr(n + "_Vs", [8, 128, sc.nts, 65], BF16)
        g.qaug = dscr(n + "_qaug", [8, 96, sc.T2], BF16)
        g.qaT = dscr(n + "_qaT", [4, 128, sc.T2], BF16)
        g.kaT = dscr(n + "_kaT", [4, 128, sc.T2], BF16)
        g.va = dscr(n + "_va", [sc.T2, 512], BF16)
        g.catT = dscr(n + "_catT", [8, 128, sc.T], BF16)
        g.xmid = dscr(n + "_xmid", [sc.T, D], F32)
        g.xmidT = dscr(n + "_xmidT", [8, 128, sc.T + 2], BF16)
        g.x1 = dscr(n + "_x1", [sc.T, D], F32)
        g.q1T = dscr(n + "_q1T", [8, 128, sc.T], BF16)
        g.k1T = dscr(n + "_k1T", [2, 128, sc.T], BF16)
        g.v1 = dscr(n + "_v1", [sc.T, 256], BF16)
        g.xn1 = dscr(n + "_xn1", [sc.T, D], F32)
        g.xn1T = dscr(n + "_xn1T", [8, 128, sc.T + 2], BF16)
        g.out = nc.dram_tensor(n + "_out", [sc.NQ, D], F32, kind="ExternalOutput").ap()
        segs.append(g)

    with ExitStack() as es:
        pg = Prog(nc, es)
        C.pg = pg
        C.identF = es.enter_context(nc.sbuf_tensor("identF", [128, 128], F32))
        C.identB = es.enter_context(nc.sbuf_tensor("identB", [128, 128], BF16))
        C.ones = es.enter_context(nc.sbuf_tensor("ones", [128, 64], F32))
        C.onesB = es.enter_context(nc.sbuf_tensor("onesB", [128, 128], BF16))
        C.zerosB = es.enter_context(nc.sbuf_tensor("zerosB", [128, 8], BF16))
        pg.dma("sp", C.identF[:], W["ident"], [], ["identF"], key="identF")
        pg.copy("dve", C.identB[:], C.identF[:], ["identF"], ["identB"])
        pg.memset("pool", C.ones[:], 1.0, ["ones"])
        pg.memset("pool", C.onesB[:], 1.0, ["onesB"])
        pg.memset("pool", C.zerosB[:], 0.0, ["zerosB"])
        pg.flush()
        stop_after = C.debug_stop = [d for d in C.debug if d.startswith("stop:")]
        stop = stop_after[0][5:] if stop_after else None
        phases = [("A", phase_A), ("B", phase_B), ("C", phase_C), ("D", phase_D), ("E", phase_E),
                  ("F", lambda C_, g_: phase_FFN(C_, g_, 0)), ("G", phase_G), ("H", phase_H),
                  ("I", lambda C_, g_: phase_FFN(C_, g_, 1))]
        only = [d[5:] for d in C.debug if d.startswith("only:")]
        for (pn, fn) in phases:
            with ExitStack() as pes:
                C.pes = pes
                C.shared = {}
                for g in segs:
                    if only and pn not in only[0]:
                        continue
                    fn(C, g)
                    pg.flush()
            if stop == pn:
                break
    C.n_ops = pg.n_ops
    return nc, C


_UID = [0]


def shared_sb(C, name, shp, dt):
    if name in C.shared:
        return C.shared[name], False
    _UID[0] += 1
    t = C.pes.enter_context(C.nc.sbuf_tensor(f"sh{_UID[0]}_{name}", list(shp), dt))
    C.shared[name] = t
    return t, True


def _pools(C, es):
    nc = C.nc
    _UID[0] += 1
    u = _UID[0]

    def sb(n, shp, dt):
        return es.enter_context(nc.sbuf_tensor(f"sb{u}_{n}", list(shp), dt))

    def ps(n, shp, dt=F32):
        return es.enter_context(nc.psum_tensor(f"ps{u}_{n}", list(shp), dt))

    return sb, ps


def load_cast(C, dst, src, K, N, stage, name, engs=("pool", "act", "dve"), ctr=[0], cwmax=2048):
    pg = C.pg
    for k in range(K):
        for c0 in range(0, N, cwmax):
            cw = min(cwmax, N - c0)
            i = ctr[0]
            ctr[0] += 1
            b = i % 2
            pg.dma("sp", stage[b][:, 0:cw], src[k * 128:(k + 1) * 128, c0:c0 + cw], [], [f"stg{b}"],
                   key=f"stg{b}")
            pg.copy(engs[i % len(engs)], dst[:, k, c0:c0 + cw], stage[b][:, 0:cw], [f"stg{b}"],
                    [f"{name}{k}"])


def load_x_T(C, src_rows, xs, xT, psT, b, tagx):
    pg = C.pg
    pg.dma("sp", xs[b][:], src_rows, [], [f"xs{b}"], key=f"xs{b}")
    for k in range(8):
        pg.tr(psT[:, k, :], xs[b][:, k * 128:(k + 1) * 128], C.identF[:], [f"xs{b}", "identF"],
              ["psTa" if k < 4 else "psTb"])
    pg.copy("act", xT[b][:, 0:4, :], psT[:, 0:4, :], ["psTa"], [f"{tagx}{b}a"])
    pg.copy("dve", xT[b][:, 4:8, :], psT[:, 4:8, :], ["psTb"], [f"{tagx}{b}b"])


def rms_rstd(C, ss, lnv, rstd, n, tag):
    pg = C.pg
    pg.act(lnv[:, 0:1], ss[:, 0:1], AF.Ln, [tag + "ss"], [tag + "lnv"], scale=1.0 / n, bias=C.epsb[:, 0:1])
    pg.act(rstd[:, 0:1], lnv[:, 0:1], AF.Exp, [tag + "lnv"], [tag + "rstd"], scale=-0.5)


def phase_A(C, g):
    pg, nc, sc, W = C.pg, C.nc, g.c, C.W
    with ExitStack() as es:
        sb, ps = _pools(C, es)
        wa = sb("wa_sb", [128, 8, 320], BF16)
        wkK = sb("wkK", [128, 2, 512], BF16)
        wkV = sb("wkV", [128, 2, 512], BF16)
        gkv = sb("gkv", [128, 256], F32)
        kcs = sb("kcs", [128, sc.nts, 64], F32)
        C.epsb = sb("epsb", [128, 1], F32)
        stage = [sb(f"stg{i}", [128, 2048], F32) for i in range(2)]
        xs = [sb(f"xs{i}", [128, 1024], F32) for i in range(2)]
        xT = [sb(f"xT{i}", [128, 8, 128], BF16) for i in range(2)]
        junk = sb("junk", [128, 256], F32)
        ss = sb("ss", [128, 1], F32)
        lnv = sb("lnv", [128, 1], F32)
        rstd = sb("rstd", [128, 1], F32)
        ckr = [sb(f"ckr{i}", [128, 352], BF16) for i in range(2)]
        m1 = sb("m1", [128, 32], F32)
        m2 = sb("m2", [128, 32], F32)
        cTg = sb("cTg", [128, 2, 512], BF16)
        kst = sb("kst", [64, 8, 512], BF16)
        krT = sb("krT", [32, 512], BF16)
        vst = [sb(f"vst{i}", [128, 8, 65], BF16) for i in range(2)]
        psT = ps("psT", [128, 8, 128])
        pkv = ps("pkv", [128, 512])
        pkv2 = [pkv, ps("pkvb", [128, 512])]
        pcT = ps("pcT", [128, 3, 128], BF16)
        pV = ps("pV", [128, 512])
        pK = [ps(f"pK{i}", [128, 512]) for i in range(2)]

        pg.memset("dve", C.epsb[:], EPS, ["epsb"])
        load_cast(C, wa, W["wa"], 8, 320, stage, "wa")
        load_cast(C, wkK, W["wkvK"], 2, 512, stage, "wkK")
        load_cast(C, wkV, W["wkvV"], 2, 512, stage, "wkV")
        pg.dma("sp", gkv[:], W["gkv"], [], ["gkv"], key="gkv")
        pg.dma("sp", kcs[:], g.kcs, [], ["kcs"], key="kcs")
        for i in range(2):
            pg.memset("pool", ckr[i][:], 0.0, [f"ckr_c{i}", f"ckr_r{i}"])
        for i in range(2):
            pg.memset("pool", vst[i][:], 1.0, [f"vst{i}"])
        wa_r = [f"wa{k}" for k in range(8)]

        def front1(t):
            b = t % 2
            load_x_T(C, g.xseq[t * 128:(t + 1) * 128, :], xs, xT, psT, b, "xT")

        def front2(t):
            b = t % 2
            ck = ckr[b]
            pk = pkv2[b]
            pkk = f"pkv{b}"
            for k in range(8):
                pg.mm(pk[:, 0:320], xT[b][:, k, :], wa[:, k, :], k == 0, k == 7,
                      [f"xT{b}a" if k < 4 else f"xT{b}b", wa_r[k]], [pkk])
            pg.act(junk[:, 0:256], pk[:, 0:256], AF.Square, [pkk], ["junk", "Ass"], accum_out=ss[:, 0:1])
            pg.tt("dve", m1[:], pk[:, 256:288], kcs[:, t, 0:32], ALU.mult, [pkk, "kcs"], ["m1"])
            pg.tt("dve", m2[:], pk[:, 288:320], kcs[:, t, 32:64], ALU.mult, [pkk, "kcs"], ["m2"])
            pg.tt("pool", ck[:, 320:352], m1[:], m2[:], ALU.add, ["m1", "m2"], [f"ckr_r{b}"])
            rms_rstd(C, ss, lnv, rstd, 256, "A")
            pg.stt("dve", ck[:, 0:256], pk[:, 0:256], rstd[:, 0:1], gkv[:], ALU.mult, ALU.mult,
                   [pkk, "Arstd", "gkv"], [f"ckr_c{b}"])

        def back(t):
            b = t % 2
            ck = ckr[b]
            grp, j = t // 4, t % 4
            pg.tr(pcT[:, 0, :], ck[:, 0:128], C.identB[:], [f"ckr_c{b}", "identB"], ["pcT"])
            pg.tr(pcT[:, 1, :], ck[:, 128:256], C.identB[:], [f"ckr_c{b}", "identB"], ["pcT"])
            pg.tr(pcT[0:32, 2, :], ck[:, 320:352], C.identB[:], [f"ckr_r{b}", "identB"], ["pcT"])
            pg.copy("act", cTg[:, :, j * 128:(j + 1) * 128], pcT[:, 0:2, :], ["pcT"], [f"cTg{j}"])
            pg.copy("dve", krT[0:32, j * 128:(j + 1) * 128], pcT[0:32, 2, :], ["pcT"], [f"krT{j}"])
            vb = t % 2
            for k in range(2):
                pg.mm(pV[:, :], cTg[:, k, j * 128:(j + 1) * 128], wkV[:, k, :], k == 0, k == 1,
                      [f"cTg{j}", f"wkV{k}"], ["pV"])
            pg.copy("dve", vst[vb][:, :, 0:64], pV[:, :].rearrange("p (h d) -> p h d", h=8), ["pV"], [f"vst{vb}"])
            pg.dma("pool", g.Vs[:, :, t, :].rearrange("h p c -> p h c"), vst[vb][:], [f"vst{vb}"], [],
                   key=f"vst{vb}")
            if j == 3:
                ctg_r = [f"cTg{jj}" for jj in range(4)]
                for h in range(8):
                    pb = h % 2
                    for k in range(2):
                        pg.mm(pK[pb][0:64, :], wkK[:, k, h * 64:(h + 1) * 64], cTg[:, k, :], k == 0, k == 1,
                              ctg_r + [f"wkK{k}"], [f"pK{pb}"])
                    pg.copy("act" if h % 2 else "dve", kst[0:64, h, :], pK[pb][0:64, :], [f"pK{pb}"], [f"kstn{h}"])
                pg.dma("pool", g.Ks[:, 0:64, grp * 512:(grp + 1) * 512].rearrange("h d c -> d h c"), kst[:],
                       [f"kstn{h}" for h in range(8)], [], key="kst")
                for h in range(8):
                    pg.dma("pool", g.Ks[h, 64:96, grp * 512:(grp + 1) * 512], krT[:, :],
                           [f"krT{jj}" for jj in range(4)], [], key=f"krT{h}")

        for t in range(sc.nts + 2):
            if t >= 2:
                back(t - 2)
            if 1 <= t <= sc.nts:
                front2(t - 1)
            if t < sc.nts:
                front1(t)


def phase_B(C, g):
    pg, nc, sc, W = C.pg, C.nc, g.c, C.W
    with ExitStack() as es:
        sb, ps = _pools(C, es)
        wb, fresh = shared_sb(C, "wb_sb", [128, 8, 1920], BF16)
        wq, _ = shared_sb(C, "wq_sb", [128, 3, 768], BF16)
        wqs, _ = shared_sb(C, "wqs_sb", [128, 3, 768], BF16)
        gq, _ = shared_sb(C, "gq", [128, 384], F32)
        C.epsb = sb("epsb", [128, 1], F32)
        ctab = sb("ctab", [96, 512], F32)
        stab = sb("stab", [96, 512], F32)
        stage = [sb(f"stg{i}", [128, 2048], F32) for i in range(2)]
        xs = [sb(f"xs{i}", [128, 1024], F32) for i in range(2)]
        xTg = sb("xTg", [128, 8, 512], BF16)
        junk = sb("junk", [128, 384], F32)
        ss = sb("ss", [128, 1], F32)
        lnv = sb("lnv", [128, 1], F32)
        rstd = sb("rstd", [128, 1], F32)
        cq = sb("cq", [128, 384], BF16)
        cqT = sb("cqT", [128, 3, 512], BF16)
        qkst = [sb(f"qkst{i}", [128, 512], BF16) for i in range(2)]
        vast = [sb(f"vast{i}", [128, 512], BF16) for i in range(2)]
        qst = [sb(f"qst{i}", [96, 512], BF16) for i in range(2)]
        r1 = sb("r1", [96, 512], F32)
        r2 = sb("r2", [96, 512], F32)
        psT = ps("psT", [128, 8, 128])
        pA = [ps(f"pA{i}", [128, 512]) for i in range(2)]
        pB = [ps(f"pB{i}", [128, 512]) for i in range(2)]
        pcq = ps("pcq", [128, 3, 128], BF16)

        pg.memset("dve", C.epsb[:], EPS, ["epsb"])
        if fresh:
            load_cast(C, wb, W["wb"], 8, 1920, stage, "wb")
            load_cast(C, wq, W["wq"], 3, 768, stage, "wq")
            load_cast(C, wqs, W["wqs"], 3, 768, stage, "wqs")
            pg.dma("sp", gq[:], W["gq"], [], ["gq"], key="gq")
        wb_r = [f"wb{k}" for k in range(8)]
        ngrp = sc.nt2 // 4
        for grp in range(ngrp):
            c0 = grp * 512
            xr = []
            for j in range(4):
                t = grp * 4 + j
                b = t % 2
                pg.dma("sp", xs[b][:], g.xext[t * 128:(t + 1) * 128, :], [], [f"xs{b}"], key=f"xs{b}")
                for k in range(8):
                    pg.tr(psT[:, k, :], xs[b][:, k * 128:(k + 1) * 128], C.identF[:], [f"xs{b}", "identF"],
                          ["psTa" if k < 4 else "psTb"])
                pg.copy("act", xTg[:, 0:4, j * 128:(j + 1) * 128], psT[:, 0:4, :], ["psTa"], [f"xTg{j}"])
                pg.copy("dve", xTg[:, 4:8, j * 128:(j + 1) * 128], psT[:, 4:8, :], ["psTb"], [f"xTg{j}"])
                xr.append(f"xTg{j}")
            for which, col0, dst, scl in (("q", 0, g.qaT, 0.125), ("k", 512, g.kaT, None)):
                for p in range(4):
                    pb = p % 2
                    for k in range(8):
                        pg.mm(pA[pb][:, :], wb[:, k, col0 + p * 128: col0 + (p + 1) * 128], xTg[:, k, :],
                              k == 0, k == 7, xr + [wb_r[k]], [f"pA{pb}"])
                    pg.copy("act" if pb else "dve", qkst[pb][:, :], pA[pb][:, :], [f"pA{pb}"], [f"qkst{pb}"],
                            scale=scl)
                    pg.dma("pool", dst[p, :, c0:c0 + 512], qkst[pb][:], [f"qkst{pb}"], [], key=f"qkst{pb}")
            for j in range(4):
                t = grp * 4 + j
                pb = j % 2
                for k in range(8):
                    pg.mm(pA[pb][:, :], xTg[:, k, j * 128:(j + 1) * 128], wb[:, k, 1024:1536], k == 0, k == 7,
                          [f"xTg{j}", wb_r[k]], [f"pA{pb}"])
                pg.copy("act", vast[pb][:, :], pA[pb][:, :], [f"pA{pb}"], [f"vast{pb}"])
                pg.dma("pool", g.va[t * 128:(t + 1) * 128, :], vast[pb][:], [f"vast{pb}"], [], key=f"vast{pb}")
                for k in range(8):
                    pg.mm(pB[pb][:, 0:384], xTg[:, k, j * 128:(j + 1) * 128], wb[:, k, 1536:1920], k == 0,
                          k == 7, [f"xTg{j}", wb_r[k]], [f"pB{pb}"])
                pg.act(junk[:, 0:384], pB[pb][:, 0:384], AF.Square, [f"pB{pb}"], ["junk", "Bss"],
                       accum_out=ss[:, 0:1])
                rms_rstd(C, ss, lnv, rstd, 384, "B")
                pg.stt("dve", cq[:, :], pB[pb][:, 0:384], rstd[:, 0:1], gq[:], ALU.mult, ALU.mult,
                       [f"pB{pb}", "Brstd", "gq"], ["cq"])
                for k in range(3):
                    pg.tr(pcq[:, k, :], cq[:, k * 128:(k + 1) * 128], C.identB[:], ["cq", "identB"], ["pcq"])
                pg.copy("act", cqT[:, :, j * 128:(j + 1) * 128], pcq[:, :, :], ["pcq"], [f"cqT{j}"])
            pg.dma("sp", ctab[64:96, :], g.qcsT[0:32, c0:c0 + 512], [], ["ctab"], key="ctab")
            pg.dma("sp", stab[64:96, :], g.qcsT[32:64, c0:c0 + 512], [], ["stab"], key="stab")
            cq_r = [f"cqT{j}" for j in range(4)]
            for h in range(8):
                pb = h % 2
                for k in range(3):
                    pg.mm(pA[pb][0:96, :], wq[:, k, h * 96:(h + 1) * 96], cqT[:, k, :], k == 0, k == 2,
                          cq_r + [f"wq{k}"], [f"pA{pb}"])
                for k in range(3):
                    pg.mm(pB[pb][0:96, :], wqs[:, k, h * 96:(h + 1) * 96], cqT[:, k, :], k == 0, k == 2,
                          cq_r + [f"wqs{k}"], [f"pB{pb}"])
                pg.copy("act", qst[pb][0:64, :], pA[pb][0:64, :], [f"pA{pb}"], [f"qst{pb}n"])
                pg.tt("dve", r1[64:96, :], pA[pb][64:96, :], ctab[64:96, :], ALU.mult, [f"pA{pb}", "ctab"], ["r1"])
                pg.tt("dve", r2[64:96, :], pB[pb][64:96, :], stab[64:96, :], ALU.mult, [f"pB{pb}", "stab"], ["r2"])
                pg.tt("pool", qst[pb][64:96, :], r1[64:96, :], r2[64:96, :], ALU.add, ["r1", "r2"], [f"qst{pb}r"])
                pg.dma("pool", g.qaug[h, :, c0:c0 + 512], qst[pb][:], [f"qst{pb}n", f"qst{pb}r"], [],
                       key=f"qst{pb}")


def phase_C(C, g):
    pg, nc, sc, W = C.pg, C.nc, g.c, C.W
    with ExitStack() as es:
        sb, ps = _pools(C, es)
        nab, fresh = shared_sb(C, "nab", [128, 8, 1024], BF16)
        qsel, _ = shared_sb(C, "qsel", [2, 128], BF16)
        stage = [sb(f"stg{i}", [128, 2048], F32) for i in range(2)]
        kaT = sb("kaT", [128, 4, sc.T2], BF16)
        va = sb("va", [128, sc.nt2, 512], BF16)
        qa = [sb(f"qa{i}", [128, 4, 128], BF16) for i in range(2)]
        pen = [sb(f"pen{i}", [2, 1024], BF16) for i in range(2)]
        P = [sb(f"P{i}", [128, 1024], BF16) for i in range(2)]
        PT = [sb(f"PT{i}", [128, 8, 128], BF16) for i in range(2)]
        mx = [sb(f"mx{i}", [128, 2], F32) for i in range(2)]
        rs = sb("rs", [128, 8], F32)
        rinv = sb("rinv", [128, 8], F32)
        ao = [sb(f"ao{i}", [128, 512], BF16) for i in range(2)]
        aoT = [sb(f"aoT{i}", [128, 4, 128], BF16) for i in range(2)]
        pS = [ps(f"pS{i}", [128, 1024]) for i in range(2)]
        pPT = [ps(f"pPT{i}", [128, 8, 128], BF16) for i in range(2)]
        pO = ps("pO", [128, 512])
        paT = ps("paT", [128, 4, 128], BF16)

        for h in range(8 if fresh else 0):
            b = h % 2
            pg.dma("sp", stage[b][:, 0:1024], W["nab"][:, h, :], [], [f"stg{b}"], key=f"stg{b}")
            pg.copy("pool" if b else "dve", nab[:, h, :], stage[b][:, 0:1024], [f"stg{b}"], [f"nab{h}"])
        if fresh:
            pg.dma("sp", qsel[:], W["qsel"], [], ["qsel"], key="qsel")
        for p in range(4):
            pg.dma("sp", kaT[:, p, :], g.kaT[p, :, :], [], [f"kaT{p}"], key=f"kaT{p}")
        pg.dma("sp", va[:], g.va.rearrange("(t p) c -> p t c", p=128), [], ["va"], key="va")

        nq = sc.nqt
        for j in range(nq):
            qb = j % 2
            q0 = KH + 128 * j - 64
            pg.dma("sp", qa[qb][:], g.qaT[:, :, q0:q0 + 128].rearrange("k p t -> p k t"), [], [f"qa{qb}"],
                   key=f"qa{qb}")
            pg.dma("sp", pen[qb][:], g.pen[j, :, :], [], [f"pen{qb}"], key=f"pen{qb}")
            k0 = 128 * j

            def s1(h):
                sbuf = h % 2
                half = h % 2
                pr = h // 2
                for n2 in range(2):
                    o = pS[sbuf][:, n2 * 512:(n2 + 1) * 512]
                    pg.mm(o, qa[qb][half * 64:(half + 1) * 64, pr, :],
                          kaT[half * 64:(half + 1) * 64, pr, k0 + n2 * 512:k0 + (n2 + 1) * 512], True, False,
                          [f"qa{qb}", f"kaT{pr}"], [f"pS{sbuf}_{n2}"])
                    pg.mm(o, C.identB[:], nab[:, h, n2 * 512:(n2 + 1) * 512], False, False,
                          ["identB", f"nab{h}"], [f"pS{sbuf}_{n2}"])
                    pg.mm(o, qsel[0:2, :], pen[qb][0:2, n2 * 512:(n2 + 1) * 512], False, True,
                          ["qsel", f"pen{qb}"], [f"pS{sbuf}_{n2}"])

            def s2(h):
                sbuf = h % 2
                rd = [f"pS{sbuf}_0", f"pS{sbuf}_1"]
                pg.add("dve", lambda e, o=mx[sbuf][:, 0:1], i_=pS[sbuf][:, :]: e.reduce_max(
                    out=o, in_=i_, axis=AX.X), rd, [f"mxa{sbuf}"])
                pg.ts("dve", mx[sbuf][:, 1:2], mx[sbuf][:, 0:1], -1.0, None, ALU.mult, None, [f"mxa{sbuf}"],
                      [f"mx{sbuf}"])
                pg.act(P[sbuf][:, :], pS[sbuf][:, :], AF.Exp, rd + [f"mx{sbuf}"], [f"P{sbuf}", f"rs{h}"],
                       bias=mx[sbuf][:, 1:2], accum_out=rs[:, h:h + 1])

            def s3(h):
                sbuf = h % 2
                for kt in range(8):
                    pg.tr(pPT[sbuf][:, kt, :], P[sbuf][:, kt * 128:(kt + 1) * 128], C.identB[:],
                          [f"P{sbuf}", "identB"], [f"pPT{sbuf}"])
                pg.copy("dve" if h % 2 else "act", PT[sbuf][:, :, :], pPT[sbuf][:, :, :], [f"pPT{sbuf}"],
                        [f"PT{sbuf}"])

            def s4(h):
                sbuf = h % 2
                for kt in range(8):
                    pg.mm(pO[:, h * 64:(h + 1) * 64], PT[sbuf][:, kt, :], va[:, j + kt, h * 64:(h + 1) * 64],
                          kt == 0, kt == 7, [f"PT{sbuf}", "va"], ["pO"])

            for step in range(8 + 3):
                if step < 8:
                    s1(step)
                if 0 <= step - 1 < 8:
                    s2(step - 1)
                if 0 <= step - 2 < 8:
                    s3(step - 2)
                if 0 <= step - 3 < 8:
                    s4(step - 3)
            pg.add("dve", lambda e: e.reciprocal(out=rinv[:, :], in_=rs[:, :]), [f"rs{h}" for h in range(8)],
                   ["rinv"])
            ab = j % 2
            pg.tt("dve", ao[ab][:, :].rearrange("p (h d) -> p h d", h=8),
                  pO[:, :].rearrange("p (h d) -> p h d", h=8),
                  rinv[:, :].unsqueeze(2).broadcast_to([128, 8, 64]), ALU.mult, ["pO", "rinv"], [f"ao{ab}"])
            for k in range(4):
                pg.tr(paT[:, k, :], ao[ab][:, k * 128:(k + 1) * 128], C.identB[:], [f"ao{ab}", "identB"], ["paT"])
            pg.copy("act", aoT[ab][:, :, :], paT[:, :, :], ["paT"], [f"aoT{ab}"])
            e0 = 128 * j - 64
            lo, hi = max(e0, 0), min(e0 + 128, sc.T)
            pg.dma("pool", g.catT[0:4, :, lo:hi].rearrange("k p t -> p k t"), aoT[ab][:, :, lo - e0:hi - e0],
                   [f"aoT{ab}"], [], key=f"aoT{ab}")


def phase_D(C, g):
    pg, nc, sc, W = C.pg, C.nc, g.c, C.W
    with ExitStack() as es:
        sb, ps = _pools(C, es)
        ksb = [sb(f"ksb{i}", [96, sc.S], BF16) for i in range(2)]
        vsb = [sb(f"vsb{i}", [128, sc.nts, 65], BF16) for i in range(2)]
        qsb = [sb(f"qsb{i}", [96, sc.T], BF16) for i in range(2)]
        PTs = [sb(f"PTs{i}", [128, 512], BF16) for i in range(4)]
        rrow = sb("rrow", [65, 512], F32)
        rbc = sb("rbc", [64, 512], F32)
        on = [sb(f"on{i}", [64, 512], BF16) for i in range(2)]
        pST = [ps(f"pST{i}", [128, 512]) for i in range(3)]
        pOT = [ps(f"pOT{i}", [128, 512]) for i in range(2)]
        pR = ps("pR", [64, 512])
        nqb = sc.T // 512
        it = 0
        for h in range(8):
            hb = h % 2
            pg.dma("sp", ksb[hb][:, :], g.Ks[h, 0:96, :], [], [f"ksb{hb}"], key=f"ksb{hb}")
            pg.dma("sp", vsb[hb][:, :, :], g.Vs[h, :, :, :], [], [f"vsb{hb}"], key=f"vsb{hb}")
            pg.dma("sp", qsb[hb][0:96, :], g.qaug[h, :, KH:KH + sc.T], [], [f"qsb{hb}"], key=f"qsb{hb}")
            for qb in range(nqb):
                ob = (h * nqb + qb) % 2
                cols = slice(qb * 512, (qb + 1) * 512)

                def st(kt, it_):
                    sbuf = it_ % 3
                    pg.mm(pST[sbuf][:, :], ksb[hb][0:96, kt * 128:(kt + 1) * 128], qsb[hb][0:96, cols], True, True,
                          [f"ksb{hb}", f"qsb{hb}"], [f"pST{sbuf}"])

                def ex_pv(kt, it_):
                    sbuf = it_ % 3
                    pbuf = it_ % 4
                    pg.act(PTs[pbuf][:, :], pST[sbuf][:, :], AF.Exp, [f"pST{sbuf}"], [f"PTs{pbuf}"], scale=MLA_SCALE)
                    pg.mm(pOT[ob][0:65, :], vsb[hb][:, kt, 0:65], PTs[pbuf][:, :], kt == 0, kt == sc.nts - 1,
                          [f"vsb{hb}", f"PTs{pbuf}"], [f"pOT{ob}"])

                base = it
                for kt in range(sc.nts + 2):
                    if kt < sc.nts:
                        st(kt, base + kt)
                    if kt - 2 >= 0:
                        ex_pv(kt - 2, base + kt - 2)
                it = base + sc.nts
                pg.add("dve", lambda e, o=rrow[64:65, :], i_=pOT[ob][64:65, :]: e.reciprocal(out=o, in_=i_),
                       [f"pOT{ob}"], ["rrow"])
                pg.mm(pR[0:64, :], C.ones[64:65, 0:64], rrow[64:65, :], True, True, ["ones", "rrow"], ["pR"])
                pg.copy("act", rbc[:, :], pR[0:64, :], ["pR"], ["rbc"])
                pg.tt("dve", on[ob][:, :], pOT[ob][0:64, :], rbc[:, :], ALU.mult, [f"pOT{ob}", "rbc"], [f"on{ob}"])
                pg.dma("pool", g.catT[4 + h // 2, (h % 2) * 64:(h % 2) * 64 + 64, cols], on[ob][:, :], [f"on{ob}"],
                       [], key=f"on{ob}")


def layer_norm_tile(C, r, xn, tmp, gt, bt, st6, mv, lnv, rstd, nmr, vcol, tag, rtag, xtag, ttag=None,
                    defer=False):
    pg = C.pg
    if ttag is None:
        ttag = tag + "tmp"
    for hlf in range(2):
        pg.add("dve", lambda e, o=st6[:, hlf, :], i_=r[:, hlf * 512:(hlf + 1) * 512]: e.bn_stats(out=o, in_=i_),
               [rtag], [tag + f"st{hlf}"])
    pg.add("dve", lambda e: e.bn_aggr(out=mv[:, :], in_=st6[:, :, :].rearrange("p a b -> p (a b)")),
           [tag + "st0", tag + "st1"], [tag + "mv"])
    pg.act(lnv[:, 0:1], mv[:, 1:2], AF.Ln, [tag + "mv"], [tag + "lnv"], bias=C.epsb[:, 0:1])
    pg.act(rstd[:, 0:1], lnv[:, 0:1], AF.Exp, [tag + "lnv"], [tag + "rstd0"], scale=-0.5)
    if vcol is not None:
        pg.tt("dve", rstd[:, 1:2], rstd[:, 0:1], vcol, ALU.mult, [tag + "rstd0", "valid"], [tag + "rstd"])
        rs_ap = rstd[:, 1:2]
    else:
        pg.copy("dve", rstd[:, 1:2], rstd[:, 0:1], [tag + "rstd0"], [tag + "rstd"])
        rs_ap = rstd[:, 1:2]
    pg.stt("dve", nmr[:, 0:1], mv[:, 0:1], -1.0, rs_ap, ALU.mult, ALU.mult, [tag + "mv", tag + "rstd"],
           [tag + "nmr"])
    pg.act(tmp[:, :], r[:, :], AF.Identity, [rtag, tag + "rstd", tag + "nmr"], [ttag + "a", ttag + "b"], scale=rs_ap,
           bias=nmr[:, 0:1])
    if not defer:
        layer_norm_back(C, xn, tmp, gt, bt, vcol, tag, xtag, ttag)


def layer_norm_back(C, xn, tmp, gt, bt, vcol, tag, xtag, ttag):
    pg = C.pg
    if vcol is not None:
        pg.tt("pool", tmp[:, 0:384], tmp[:, 0:384], gt[:, 0:384], ALU.mult, [ttag + "a", tag + "g"], [ttag + "a"])
        pg.tt("dve", tmp[:, 384:1024], tmp[:, 384:1024], gt[:, 384:1024], ALU.mult, [ttag + "b", tag + "g"],
              [ttag + "b"])
        pg.stt("dve", xn[:, :], bt[:, :], vcol, tmp[:, :], ALU.mult, ALU.add,
               [ttag + "a", ttag + "b", tag + "b", "valid"], [xtag])
    else:
        pg.tt("pool", tmp[:, :], tmp[:, :], gt[:, :], ALU.mult, [ttag + "a", ttag + "b", tag + "g"],
              [ttag + "a", ttag + "b"])
        pg.tt("pool", xn[:, :], tmp[:, :], bt[:, :], ALU.add, [ttag + "a", ttag + "b", tag + "b"], [xtag])


def phase_E(C, g):
    pg, nc, sc, W = C.pg, C.nc, g.c, C.W
    with ExitStack() as es:
        sb, ps = _pools(C, es)
        wo, fresh = shared_sb(C, "wo_sb", [128, 8, 1024], BF16)
        gt, _ = shared_sb(C, "gt", [128, 1024], F32)
        bt, _ = shared_sb(C, "bt", [128, 1024], F32)
        stage = [sb(f"stg{i}", [128, 2048], F32) for i in range(2)]
        valid = sb("valid", [128, sc.nt], F32)
        C.epsb = sb("epsb", [128, 1], F32)
        cat = [sb(f"cat{i}", [128, 8, 512], BF16) for i in range(2)]
        xs = [sb(f"xs{i}", [128, 1024], F32) for i in range(2)]
        r = [sb(f"r{i}", [128, 1024], F32) for i in range(2)]
        tmp = [sb(f"tmp{i}", [128, 1024], F32) for i in range(2)]
        xn = [sb(f"xn{i}", [128, 1024], F32) for i in range(2)]
        xnT = [sb(f"xnT{i}", [128, 8, 128], BF16) for i in range(2)]
        st6 = sb("st6", [128, 2, 6], F32)
        mv = sb("mv", [128, 2], F32)
        lnv = sb("lnv", [128, 1], F32)
        rstd = sb("rstd", [128, 2], F32)
        nmr = sb("nmr", [128, 1], F32)
        pmix = [ps(f"pmix{i}", [128, 1024]) for i in range(2)]
        psT = ps("psT", [128, 8, 128])
        pg.memset("dve", C.epsb[:], EPS, ["epsb"])
        if fresh:
            load_cast(C, wo, W["wo0"], 8, 1024, stage, "wo")
            pg.dma("sp", gt[:], W["l0_ln1g"], [], ["Eg"], key="Eg")
            pg.dma("sp", bt[:], W["l0_ln1b"], [], ["Eb"], key="Eb")
        pg.dma("sp", valid[:], g.valid, [], ["valid"], key="valid")
        def front(t):
            grp, j = t // 4, t % 4
            cb = grp % 2
            b = t % 2
            if j == 0:
                pg.dma("sp", cat[cb][:], g.catT[:, :, grp * 512:(grp + 1) * 512].rearrange("k p t -> p k t"), [],
                       [f"cat{cb}"], key=f"cat{cb}")
            pg.dma("sp", xs[b][:], g.xext[KH + t * 128:KH + (t + 1) * 128, :], [], [f"xs{b}"], key=f"xs{b}")
            for hlf in range(2):
                for k in range(8):
                    pg.mm(pmix[b][:, hlf * 512:(hlf + 1) * 512], cat[cb][:, k, j * 128:(j + 1) * 128],
                          wo[:, k, hlf * 512:(hlf + 1) * 512], k == 0, k == 7, [f"cat{cb}", f"wo{k}"],
                          [f"pmix{b}_{hlf}"])
                pg.stt("dve", r[b][:, hlf * 512:(hlf + 1) * 512], xs[b][:, hlf * 512:(hlf + 1) * 512], ALPHA,
                       pmix[b][:, hlf * 512:(hlf + 1) * 512], ALU.mult, ALU.add, [f"xs{b}", f"pmix{b}_{hlf}"],
                       [f"r{b}"])
            layer_norm_tile(C, r[b], xn[b], tmp[b], gt, bt, st6, mv, lnv, rstd, nmr, valid[:, t:t + 1], "E",
                            f"r{b}", f"xn{b}", ttag=f"tmp{b}", defer=True)

        def back(t):
            b = t % 2
            layer_norm_back(C, xn[b], tmp[b], gt, bt, valid[:, t:t + 1], "E", f"xn{b}", f"tmp{b}")
            pg.dma("pool", g.xmid[t * 128:(t + 1) * 128, :], xn[b][:], [f"xn{b}"], [], key=f"xn{b}")
            for k in range(8):
                pg.tr(psT[:, k, :], xn[b][:, k * 128:(k + 1) * 128], C.identF[:], [f"xn{b}", "identF"],
                      ["psTa" if k < 4 else "psTb"])
            pg.copy("act", xnT[b][:, 0:4, :], psT[:, 0:4, :], ["psTa"], [f"xnT{b}"])
            pg.copy("dve", xnT[b][:, 4:8, :], psT[:, 4:8, :], ["psTb"], [f"xnT{b}"])
            pg.dma("pool", g.xmidT[:, :, 1 + t * 128:1 + (t + 1) * 128].rearrange("k p t -> p k t"), xnT[b][:],
                   [f"xnT{b}"], [], key=f"xnT{b}")

        for t in range(sc.nt + 1):
            if t < sc.nt:
                front(t)
            if t >= 1:
                back(t - 1)


def phase_FFN(C, g, l):
    pg, nc, sc, W = C.pg, C.nc, g.c, C.W
    p = f"l{l}_"
    if l == 0:
        xT_src, x_src, dst, t_lo, ntok = g.xmidT, g.xmid, g.x1, 0, sc.T
    else:
        xT_src, x_src, dst, t_lo, ntok = g.xn1T, g.xn1, g.out, HALO, sc.NQ
    with ExitStack() as es:
        sb, ps = _pools(C, es)
        wup, fresh = shared_sb(C, "wup", [128, 8, 2 * DFF], BF16)
        wdn, _ = shared_sb(C, "wdn", [128, 22, 1024], BF16)
        cw, _ = shared_sb(C, "cw", [128, 44, 3], F32)
        cbt, _ = shared_sb(C, "cbt", [128, 44], F32)
        gt, _ = shared_sb(C, "gt", [128, 1024], F32)
        bt, _ = shared_sb(C, "bt", [128, 1024], F32)
        stage = [sb(f"stg{i}", [128, 512], F32) for i in range(2)]
        if fresh:
            load_cast(C, wup, W[p + "wup"], 8, 2 * DFF, stage, "wup", cwmax=512)
            load_cast(C, wdn, W[p + "wdn"], 22, 1024, stage, "wdn", cwmax=512)
        C.epsb = sb("epsb", [128, 1], F32)
        xTb = [sb(f"xTb{i}", [128, 8, NB + 2], BF16) for i in range(2)]
        uT = [sb(f"uT{i}", [128, 22, NB], BF16) for i in range(2)]
        cg = [sb(f"cg{i}", [128, NB], F32) for i in range(2)]
        cv = [sb(f"cv{i}", [128, NB], F32) for i in range(2)]
        gg = [sb(f"gg{i}", [128, NB], F32) for i in range(2)]
        xs = [sb(f"xs{i}", [128, 1024], F32) for i in range(2)]
        r = sb("r", [128, 1024], F32)
        tmp = sb("tmp", [128, 1024], F32)
        xo = [sb(f"xo{i}", [128, 1024], F32) for i in range(2)]
        st6 = sb("st6", [128, 2, 6], F32)
        mv = sb("mv", [128, 2], F32)
        lnv = sb("lnv", [128, 1], F32)
        rstd = sb("rstd", [128, 2], F32)
        nmr = sb("nmr", [128, 1], F32)
        ph = [ps(f"ph{i}", [128, 512]) for i in range(4)]
        py = [ps(f"py{i}", [128, 1024]) for i in range(2)]
        pg.memset("dve", C.epsb[:], EPS, ["epsb"])
        if fresh:
            pg.dma("sp", cw[:], W[p + "cw"], [], ["cw"], key="cw")
            pg.dma("sp", cbt[:], W[p + "cb"], [], ["cb"], key="cb")
            pg.dma("sp", gt[:], W[p + "ln2g"], [], ["Fg"], key="Fg")
            pg.dma("sp", bt[:], W[p + "ln2b"], [], ["Fb"], key="Fb")
        wup_r = [f"wup{k}" for k in range(8)]
        NW = NB + 2
        nblk = ntok // NB
        NT = NB // 128

        def down_steps(blk):
            tb = t_lo + blk * NB
            ub = blk % 2
            seq = []
            for j in range(NT):
                tok0 = tb + j * 128
                b = (blk * NT + j) % 2
                pj = py[j % 2]
                for hlf in range(2):
                    for c in range(22):
                        def mmf(j=j, hlf=hlf, c=c, pj=pj, b=b, tok0=tok0):
                            if hlf == 0 and c == 0:
                                pg.dma("sp", xs[b][:], x_src[tok0:tok0 + 128, :], [], [f"xs{b}"], key=f"xs{b}")
                            pg.mm(pj[:, hlf * 512:(hlf + 1) * 512], uT[ub][:, c, j * 128:(j + 1) * 128],
                                  wdn[:, c, hlf * 512:(hlf + 1) * 512], c == 0, c == 21, [f"uT{ub}_{c}", f"wdn{c}"],
                                  [f"py{j % 2}_{hlf}"])
                            if c == 21:
                                pg.stt("dve", r[:, hlf * 512:(hlf + 1) * 512], xs[b][:, hlf * 512:(hlf + 1) * 512],
                                       ALPHA, pj[:, hlf * 512:(hlf + 1) * 512], ALU.mult, ALU.add,
                                       [f"xs{b}", f"py{j % 2}_{hlf}"], ["r"])
                                if hlf == 1:
                                    layer_norm_tile(C, r, xo[b], tmp, gt, bt, st6, mv, lnv, rstd, nmr, None, "F",
                                                    "r", f"xo{b}")
                                    o0 = tok0 - t_lo
                                    pg.dma("pool", dst[o0:o0 + 128, :], xo[b][:], [f"xo{b}"], [], key=f"xo{b}")
                        seq.append(mmf)
            per = (len(seq) + 21) // 22
            return [seq[i * per:(i + 1) * per] for i in range(22)]

        for blk in range(nblk + 1):
            prev = down_steps(blk - 1) if blk >= 1 else None
            if blk < nblk:
                tb = t_lo + blk * NB
                xb = blk % 2
                ub = blk % 2
                pg.dma("sp", xTb[xb][:], xT_src[:, :, tb:tb + NW].rearrange("k p t -> p k t"), [], [f"xTb{xb}"],
                       key=f"xTb{xb}")
                if l == 0 and blk == 0:
                    pg.memset("pool", xTb[xb][:, :, 0:1], 0.0, [f"xTb{xb}"])
                if l == 0 and blk == nblk - 1:
                    pg.memset("pool", xTb[xb][:, :, NW - 1:NW], 0.0, [f"xTb{xb}"])
            for c in range(22):
                i2 = c % 2
                hg, hv = ph[2 * i2], ph[2 * i2 + 1]
                if blk < nblk:
                    for (hp, col0, tg) in ((hg, c * 128, "g"), (hv, DFF + c * 128, "v")):
                        for k in range(8):
                            pg.mm(hp[:, 0:NW], wup[:, k, col0:col0 + 128], xTb[xb][:, k, :], k == 0, k == 7,
                                  [f"xTb{xb}", wup_r[k]], [f"ph{i2}{tg}"])
                if prev is not None:
                    for f in prev[c]:
                        f()
                if blk < nblk:
                    for (hp, ch, dstt, tg) in ((hg, c, cg[i2], "g"), (hv, 22 + c, cv[i2], "v")):
                        pg.act(dstt[:, :], hp[:, 1:NB + 1], AF.Identity, [f"ph{i2}{tg}", "cw", "cb"], [f"c{tg}{i2}"],
                               scale=cw[:, ch, 1:2], bias=cbt[:, ch:ch + 1])
                        pg.stt("dve", dstt[:, :], hp[:, 0:NB], cw[:, ch, 0:1], dstt[:, :], ALU.mult, ALU.add,
                               [f"ph{i2}{tg}", "cw", f"c{tg}{i2}"], [f"c{tg}{i2}"])
                        pg.stt("dve", dstt[:, :], hp[:, 2:NB + 2], cw[:, ch, 2:3], dstt[:, :], ALU.mult, ALU.add,
                               [f"ph{i2}{tg}", "cw", f"c{tg}{i2}"], [f"c{tg}{i2}"])
                    pg.act(gg[i2][:, :], cg[i2][:, :], AF.Gelu, [f"cg{i2}"], [f"gg{i2}"])
                    pg.tt("pool", uT[ub][:, c, :], gg[i2][:, :], cv[i2][:, :], ALU.mult, [f"gg{i2}", f"cv{i2}"],
                          [f"uT{ub}_{c}"])


def phase_G(C, g):
    pg, nc, sc, W = C.pg, C.nc, g.c, C.W
    with ExitStack() as es:
        sb, ps = _pools(C, es)
        w1, fresh = shared_sb(C, "w1_sb", [128, 8, 1856], BF16)
        stage = [sb(f"stg{i}", [128, 2048], F32) for i in range(2)]
        l1cs = sb("l1cs", [128, sc.nt, 32], F32)
        xs = [sb(f"xs{i}", [128, 1024], F32) for i in range(2)]
        xT = [sb(f"xT{i}", [128, 8, 128], BF16) for i in range(2)]
        m1 = sb("m1", [128, 20, 16], F32)
        m2 = sb("m2", [128, 20, 16], F32)
        m3 = sb("m3", [128, 20, 16], F32)
        qk = [sb(f"qk{i}", [128, 20, 64], BF16) for i in range(2)]
        vst = [sb(f"vst{i}", [128, 256], BF16) for i in range(2)]
        qkT = [sb(f"qkT{i}", [128, 10, 128], BF16) for i in range(2)]
        psT = ps("psT", [128, 8, 128])
        pq = ps("pq", [128, 2048])
        pqT = ps("pqT", [128, 10, 128], BF16)
        if fresh:
            load_cast(C, w1, W["w1in"], 8, 1856, stage, "w1")
        pg.dma("sp", l1cs[:], g.l1cs, [], ["l1cs"], key="l1cs")
        w1_r = [f"w1{k}" for k in range(8)]
        for t in range(sc.nt):
            b = t % 2
            load_x_T(C, g.x1[t * 128:(t + 1) * 128, :], xs, xT, psT, b, "xT")
            for (c0, cw_, bank) in ((0, 512, 0), (512, 512, 1), (1024, 512, 2), (1536, 320, 3)):
                for k in range(8):
                    pg.mm(pq[:, bank * 512:bank * 512 + cw_], xT[b][:, k, :], w1[:, k, c0:c0 + cw_], k == 0, k == 7,
                          [f"xT{b}a" if k < 4 else f"xT{b}b", w1_r[k]], [f"pq{bank}"])
            qkv = pq[:, 0:1280].rearrange("p (h d) -> p h d", h=20)
            sw = pq[:, 1536:1856].rearrange("p (h d) -> p h d", h=20)
            cosb = l1cs[:, t, 0:16].unsqueeze(1).broadcast_to([128, 20, 16])
            sinb = l1cs[:, t, 16:32].unsqueeze(1).broadcast_to([128, 20, 16])
            pg.tt("dve", m1[:, :, :], qkv[:, :, 0:16], cosb, ALU.mult, ["pq0", "pq1", "pq2", "l1cs"], ["Gm1"])
            pg.tt("dve", m2[:, :, :], sw, sinb, ALU.mult, ["pq3", "l1cs"], ["Gm2"])
            pg.tt("pool", m3[:, :, :], m1[:, :, :], m2[:, :, :], ALU.add, ["Gm1", "Gm2"], ["Gm3"])
            pg.copy("act", qk[b][:, 0:16, 0:16], m3[:, 0:16, :], ["Gm3"], [f"qk{b}a"], scale=0.125)
            pg.copy("pool", qk[b][:, 16:20, 0:16], m3[:, 16:20, :], ["Gm3"], [f"qk{b}b"])
            pg.copy("act", qk[b][:, 0:16, 16:64], qkv[:, 0:16, 16:64], ["pq0", "pq1"], [f"qk{b}c"], scale=0.125)
            pg.copy("dve", qk[b][:, 16:20, 16:64], qkv[:, 16:20, 16:64], ["pq2"], [f"qk{b}d"])
            pg.copy("dve", vst[b][:, :], pq[:, 1280:1536], ["pq2"], [f"vst{b}"])
            pg.dma("pool", g.v1[t * 128:(t + 1) * 128, :], vst[b][:], [f"vst{b}"], [], key=f"vst{b}")
            qrd = [f"qk{b}a", f"qk{b}b", f"qk{b}c", f"qk{b}d", "identB"]
            for pr in range(10):
                pg.tr(pqT[:, pr, :], qk[b][:, 2 * pr:2 * pr + 2, :].rearrange("p h d -> p (h d)"), C.identB[:], qrd,
                      ["pqT"])
            pg.copy("act", qkT[b][:, 0:5, :], pqT[:, 0:5, :], ["pqT"], [f"qkT{b}"])
            pg.copy("dve", qkT[b][:, 5:10, :], pqT[:, 5:10, :], ["pqT"], [f"qkT{b}"])
            pg.dma("pool", g.q1T[:, :, t * 128:(t + 1) * 128].rearrange("k p t -> p k t"), qkT[b][:, 0:8, :],
                   [f"qkT{b}"], [], key=f"qkT{b}q")
            pg.dma("pool", g.k1T[:, :, t * 128:(t + 1) * 128].rearrange("k p t -> p k t"), qkT[b][:, 8:10, :],
                   [f"qkT{b}"], [], key=f"qkT{b}k")


def phase_H(C, g):
    pg, nc, sc, W = C.pg, C.nc, g.c, C.W
    with ExitStack() as es:
        sb, ps = _pools(C, es)
        wo, fresh = shared_sb(C, "wo_sb", [128, 8, 1024], BF16)
        gt, _ = shared_sb(C, "gt", [128, 1024], F32)
        bt, _ = shared_sb(C, "bt", [128, 1024], F32)
        sinks, _ = shared_sb(C, "sinks", [128, 16], F32)
        band, _ = shared_sb(C, "band", [128, 384], BF16)
        stage = [sb(f"stg{i}", [128, 2048], F32) for i in range(2)]
        valid = sb("valid", [128, sc.nt], F32)
        kpen = sb("kpen", [1, sc.T], BF16)
        C.epsb = sb("epsb", [128, 1], F32)
        k1T = sb("k1T", [128, 2, sc.T], BF16)
        v1 = sb("v1", [128, sc.nt, 256], BF16)
        q1 = [sb(f"q1{i}", [128, 8, 128], BF16) for i in range(2)]
        P = [sb(f"P{i}", [128, 384], BF16) for i in range(2)]
        PT = [sb(f"PT{i}", [128, 3, 128], BF16) for i in range(2)]
        mx = [sb(f"mx{i}", [128, 2], F32) for i in range(2)]
        rs = sb("rs", [128, 16], F32)
        es_ = sb("es_", [128, 16], F32)
        rinv = sb("rinv", [128, 16], F32)
        at = sb("at", [128, 1024], BF16)
        atT = sb("atT", [128, 8, 128], BF16)
        xs = [sb(f"xs{i}", [128, 1024], F32) for i in range(2)]
        r = sb("r", [128, 1024], F32)
        tmp = [sb(f"tmp{i}", [128, 1024], F32) for i in range(2)]
        xn = [sb(f"xn{i}", [128, 1024], F32) for i in range(2)]
        xnT = [sb(f"xnT{i}", [128, 8, 128], BF16) for i in range(2)]
        st6 = sb("st6", [128, 2, 6], F32)
        mv = sb("mv", [128, 2], F32)
        lnv = sb("lnv", [128, 1], F32)
        rstd = sb("rstd", [128, 2], F32)
        nmr = sb("nmr", [128, 1], F32)
        pS = [ps(f"pS{i}", [128, 512]) for i in range(2)]
        pPT = [ps(f"pPT{i}", [128, 8, 128], BF16) for i in range(2)]
        pO = ps("pO", [128, 1024])
        pmix = ps("pmix", [128, 1024])
        pg.memset("dve", C.epsb[:], EPS, ["epsb"])
        if fresh:
            load_cast(C, wo, W["wo1"], 8, 1024, stage, "wo")
            pg.dma("sp", gt[:], W["l1_ln1g"], [], ["Hg"], key="Hg")
            pg.dma("sp", bt[:], W["l1_ln1b"], [], ["Hb"], key="Hb")
            pg.dma("sp", sinks[:], W["sinks"], [], ["sinks"], key="sinks")
            pg.dma("sp", band[:], W["band"], [], ["band"], key="band")
        pg.dma("sp", valid[:], g.valid, [], ["valid"], key="valid")
        pg.dma("sp", kpen[:], g.kpen, [], ["kpen"], key="kpen")
        pg.dma("sp", k1T[:], g.k1T.rearrange("k p t -> p k t"), [], ["k1T"], key="k1T")
        pg.dma("sp", v1[:], g.v1.rearrange("(t p) c -> p t c", p=128), [], ["v1"], key="v1")
        def front(i):
            b = i % 2
            pg.dma("sp", q1[b][:], g.q1T[:, :, i * 128:(i + 1) * 128].rearrange("k p t -> p k t"), [], [f"q1{b}"],
                   key=f"q1{b}")
            pg.dma("sp", xs[b][:], g.x1[i * 128:(i + 1) * 128, :], [], [f"xs{b}"], key=f"xs{b}")
            k0 = (i - 1) * 128

            def s1(hd):
                sbuf = hd % 2
                half = hd % 2
                gk = QPERM[hd] // 4
                o = pS[sbuf][:, 0:384]
                pg.mm(o, q1[b][half * 64:(half + 1) * 64, hd // 2, :],
                      k1T[half * 64:(half + 1) * 64, gk // 2, k0:k0 + 384], True, False, [f"q1{b}", "k1T"],
                      [f"pS{sbuf}"])
                pg.mm(o, C.identB[:], band[:, :], False, False, ["identB", "band"], [f"pS{sbuf}"])
                pg.mm(o, C.onesB[0:1, 0:128], kpen[0:1, k0:k0 + 384], False, True, ["onesB", "kpen"], [f"pS{sbuf}"])

            def s2(hd):
                sbuf = hd % 2
                pg.add("dve", lambda e, o=mx[sbuf][:, 0:1], i_=pS[sbuf][:, 0:384]: e.reduce_max(
                    out=o, in_=i_, axis=AX.X), [f"pS{sbuf}"], [f"mxa{sbuf}"])
                pg.ts("dve", mx[sbuf][:, 1:2], mx[sbuf][:, 0:1], sinks[:, hd:hd + 1], -1.0, ALU.max, ALU.mult,
                      [f"mxa{sbuf}", "sinks"], [f"mx{sbuf}"])
                pg.act(P[sbuf][:, :], pS[sbuf][:, 0:384], AF.Exp, [f"pS{sbuf}", f"mx{sbuf}"], [f"P{sbuf}", f"rs{hd}"],
                       bias=mx[sbuf][:, 1:2], accum_out=rs[:, hd:hd + 1])
                pg.act(es_[:, hd:hd + 1], sinks[:, hd:hd + 1], AF.Exp, ["sinks", f"mx{sbuf}"], [f"es{hd}"],
                       bias=mx[sbuf][:, 1:2])

            def s3(hd):
                sbuf = hd % 2
                for kt in range(3):
                    pg.tr(pPT[sbuf][:, kt, :], P[sbuf][:, kt * 128:(kt + 1) * 128], C.identB[:],
                          [f"P{sbuf}", "identB"], [f"pPT{sbuf}"])
                pg.copy("dve" if hd % 2 else "act", PT[sbuf][:, :, :], pPT[sbuf][:, 0:3, :], [f"pPT{sbuf}"],
                        [f"PT{sbuf}"])

            def s4(hd):
                sbuf = hd % 2
                gk = QPERM[hd] // 4
                for kt in range(3):
                    pg.mm(pO[:, hd * 64:(hd + 1) * 64], PT[sbuf][:, kt, :], v1[:, i - 1 + kt, gk * 64:(gk + 1) * 64],
                          kt == 0, kt == 2, [f"PT{sbuf}", "v1"], [f"pO{hd // 8}"])

            for step in range(16 + 3):
                if step < 16:
                    s1(step)
                if 0 <= step - 1 < 16:
                    s2(step - 1)
                if 0 <= step - 2 < 16:
                    s3(step - 2)
                if 0 <= step - 3 < 16:
                    s4(step - 3)
            pg.tt("dve", rinv[:, :], rs[:, :], es_[:, :], ALU.add, [f"rs{h}" for h in range(16)] +
                  [f"es{h}" for h in range(16)], ["rinv0"])
            pg.add("dve", lambda e: e.reciprocal(out=rinv[:, :], in_=rinv[:, :]), ["rinv0"], ["rinv"])
            pg.tt("dve", at[:, :].rearrange("p (h d) -> p h d", h=16), pO[:, :].rearrange("p (h d) -> p h d", h=16),
                  rinv[:, :].unsqueeze(2).broadcast_to([128, 16, 64]), ALU.mult, ["pO0", "pO1", "rinv"], ["at"])
            for k in range(8):
                pg.tr(pPT[0][:, k, :], at[:, k * 128:(k + 1) * 128], C.identB[:], ["at", "identB"], ["pPT0"])
            pg.copy("act", atT[:, :, :], pPT[0][:, :, :], ["pPT0"], ["atT"])
            for hlf in range(2):
                for k in range(8):
                    pg.mm(pmix[:, hlf * 512:(hlf + 1) * 512], atT[:, k, :], wo[:, k, hlf * 512:(hlf + 1) * 512],
                          k == 0, k == 7, ["atT", f"wo{k}"], [f"pmix{hlf}"])
                pg.stt("dve", r[:, hlf * 512:(hlf + 1) * 512], xs[b][:, hlf * 512:(hlf + 1) * 512], ALPHA,
                       pmix[:, hlf * 512:(hlf + 1) * 512], ALU.mult, ALU.add, [f"xs{b}", f"pmix{hlf}"], ["r"])
            layer_norm_tile(C, r, xn[b], tmp[b], gt, bt, st6, mv, lnv, rstd, nmr, valid[:, i:i + 1], "H", "r",
                            f"xn{b}", ttag=f"tmp{b}", defer=True)

        def back(i):
            b = i % 2
            layer_norm_back(C, xn[b], tmp[b], gt, bt, valid[:, i:i + 1], "H", f"xn{b}", f"tmp{b}")
            pg.dma("pool", g.xn1[i * 128:(i + 1) * 128, :], xn[b][:], [f"xn{b}"], [], key=f"xn{b}")
            psT = pO[:, :].rearrange("p (k t) -> p k t", k=8)
            for k in range(8):
                pg.tr(psT[:, k, :], xn[b][:, k * 128:(k + 1) * 128], C.identF[:], [f"xn{b}", "identF"],
                      ["pO0" if k < 4 else "pO1"])
            pg.copy("act", xnT[b][:, 0:4, :], psT[:, 0:4, :], ["pO0"], [f"xnT{b}"])
            pg.copy("dve", xnT[b][:, 4:8, :], psT[:, 4:8, :], ["pO1"], [f"xnT{b}"])
            pg.dma("pool", g.xn1T[:, :, 1 + i * 128:1 + (i + 1) * 128].rearrange("k p t -> p k t"), xnT[b][:],
                   [f"xnT{b}"], [], key=f"xnT{b}")


        for i in range(1, sc.nt):
            if i < sc.nt - 1:
                front(i)
            if i >= 2:
                back(i - 1)


def make_in_maps(cfg, inputs, n_cores=8):
    sh = prep_shared(inputs)
    P, Sg = cfg
    maps = []
    for c in range(n_cores):
        m = dict(sh)
        m.update(prep_seg(P, inputs["x_prompt"][c // 2], (c % 2) * P.NQ))
        m.update(prep_seg(Sg, inputs["x_sample"][0], c * Sg.NQ))
        maps.append(m)
    return maps


def assemble(cfg, res, inputs):
    P, Sg = cfg
    yp = np.zeros(inputs["x_prompt"].shape, np.float32)
    ys = np.zeros(inputs["x_sample"].shape, np.float32)
    for c in range(8):
        yp[c // 2, (c % 2) * P.NQ:(c % 2 + 1) * P.NQ] = res[c]["P_out"]
        ys[0, c * Sg.NQ:(c + 1) * Sg.NQ] = res[c]["S_out"]
    return yp, ys


def kernel(**inputs):
    inputs = {k: np.asarray(v) for k, v in inputs.items()}
    cfg = FULL_CFG
    nc, _ = build(cfg)
    maps = make_in_maps(cfg, inputs)
    res = run_bass_kernel_spmd(nc, maps, core_ids=list(range(8)))
    return assemble(cfg, res.results, inputs)
```
